# Optimizing a Trainium2 kernel written in Bass

```python
import jax, jax.numpy as jnp
from jax import lax
import numpy as np

D_MODEL = 2048
BATCH = 2
SEQ = 4096
DEPTH = 1

SWA_Q_HEADS = 32
SWA_KV_HEADS = 4
SWA_HEAD_DIM = 64
SWA_WINDOW = 128
SWA_BLOCK = 128
ROPE_THETA = 500000.0
ROPE_DIM = SWA_HEAD_DIM // 4
DN_K_HEADS = 16
DN_V_HEADS = 32
DN_HEAD_K = 128
DN_HEAD_V = 128
DN_CONV = 4
DN_CHUNK = 64
EPS = 1e-6

SWA_Q = SWA_Q_HEADS * SWA_HEAD_DIM
SWA_KV = SWA_KV_HEADS * SWA_HEAD_DIM
DN_KEY = DN_K_HEADS * DN_HEAD_K
DN_VAL = DN_V_HEADS * DN_HEAD_V
DN_CONV_CH = 2 * DN_KEY + DN_VAL
IN_SIZES = (SWA_Q, SWA_KV, SWA_KV, SWA_Q, DN_CONV_CH, DN_VAL, DN_V_HEADS, DN_V_HEADS, D_MODEL, D_MODEL)
IN_WIDTH = sum(IN_SIZES)

kernel_name = "hybrid_swa_sink_gated_deltanet_adaln"


def _split_points():
    pts, acc = [], 0
    for s in IN_SIZES[:-1]:
        acc += s
        pts.append(acc)
    return pts


def rms_norm(x, w):
    xf = x.astype(jnp.float32)
    y = xf * lax.rsqrt(jnp.mean(xf * xf, axis=-1, keepdims=True) + EPS)
    return (y * w.astype(jnp.float32)).astype(x.dtype)


def l2_norm(x):
    xf = x.astype(jnp.float32)
    return xf * lax.rsqrt(jnp.sum(xf * xf, axis=-1, keepdims=True) + EPS)


def partial_rope(x, positions):
    half = ROPE_DIM // 2
    inv_freq = ROPE_THETA ** (-jnp.arange(half, dtype=jnp.float32) * (2.0 / ROPE_DIM))
    ang = positions.astype(jnp.float32)[..., None] * inv_freq
    cos = jnp.cos(ang)[:, :, None, :]
    sin = jnp.sin(ang)[:, :, None, :]
    xr = x[..., :ROPE_DIM].astype(jnp.float32)
    x1, x2 = xr[..., :half], xr[..., half:]
    rot = jnp.concatenate([x1 * cos - x2 * sin, x2 * cos + x1 * sin], axis=-1).astype(x.dtype)
    return jnp.concatenate([rot, x[..., ROPE_DIM:]], axis=-1)


def swa_attention(q, k, v, sinks):
    B, T = q.shape[0], q.shape[1]
    nb = T // SWA_BLOCK
    G = SWA_Q_HEADS // SWA_KV_HEADS
    f32 = jnp.float32
    qb = q.astype(f32).reshape(B, nb, SWA_BLOCK, SWA_KV_HEADS, G, SWA_HEAD_DIM)

    def band(t):
        tb = t.astype(f32).reshape(B, nb, SWA_BLOCK, SWA_KV_HEADS, SWA_HEAD_DIM)
        prev = jnp.concatenate([jnp.zeros_like(tb[:, :1]), tb[:, :-1]], axis=1)
        return jnp.concatenate([prev, tb], axis=2)

    kb, vb = band(k), band(v)
    s = jnp.einsum('bnqhgd,bnkhd->bnhgqk', qb, kb) * (SWA_HEAD_DIM ** -0.5)
    qi = jnp.arange(SWA_BLOCK)[:, None]
    kj = jnp.arange(2 * SWA_BLOCK)[None, :]
    rel = qi + SWA_BLOCK - kj
    in_window = (rel >= 0) & (rel < SWA_WINDOW)
    first = (jnp.arange(nb) == 0)[:, None, None]
    pad_key = (kj < SWA_BLOCK)[None]
    valid = in_window[None] & ~(first & pad_key)
    s = jnp.where(valid[None, :, None, None], s, -jnp.inf)
    sink = sinks.astype(f32).reshape(SWA_KV_HEADS, G)[None, None, :, :, None, None]
    m = jnp.maximum(jnp.max(s, axis=-1, keepdims=True), sink)
    p = jnp.exp(s - m)
    denom = jnp.sum(p, axis=-1, keepdims=True) + jnp.exp(sink - m)
    o = jnp.einsum('bnhgqk,bnkhd->bnqhgd', p / denom, vb)
    return o.reshape(B, T, SWA_Q)


def causal_conv_silu(x, w):
    T = x.shape[1]
    xp = jnp.pad(x, ((0, 0), (DN_CONV - 1, 0), (0, 0)))
    y = xp[:, 0:T] * w[0]
    for j in range(1, DN_CONV):
        y = y + xp[:, j:j + T] * w[j]
    return jax.nn.silu(y)


def gated_delta_rule(q, k, v, g, beta):
    B, T, H, dk = q.shape
    dv = v.shape[-1]
    C = DN_CHUNK
    N = T // C
    f32 = jnp.float32

    def chunks(t):
        t = t.astype(f32).reshape((B, N, C, H) + t.shape[3:])
        return jnp.moveaxis(t, 3, 1)

    q = chunks(q) * (dk ** -0.5)
    k, v, beta, g = chunks(k), chunks(v), chunks(beta), chunks(g)
    g = jnp.cumsum(g, axis=-1)
    tril = jnp.tril(jnp.ones((C, C), dtype=bool))
    strict = jnp.tril(jnp.ones((C, C), dtype=bool), -1)
    decay = jnp.exp(jnp.where(tril, g[..., :, None] - g[..., None, :], -jnp.inf))
    k_beta = k * beta[..., None]
    v_beta = v * beta[..., None]
    L = jnp.where(strict, jnp.einsum('bhncd,bhnsd->bhncs', k_beta, k) * decay, 0.0)
    eye = jnp.eye(C, dtype=f32)
    t_inv = lax.linalg.triangular_solve(eye + L, jnp.broadcast_to(eye, L.shape),
                                        left_side=True, lower=True, unit_diagonal=True)
    u = t_inv @ v_beta
    w = t_inv @ (k_beta * jnp.exp(g)[..., None])
    qk = jnp.where(tril, jnp.einsum('bhncd,bhnsd->bhncs', q, k) * decay, 0.0)
    q_dec = q * jnp.exp(g)[..., None]
    k_dec = k * jnp.exp(g[..., -1:] - g)[..., None]
    g_last = jnp.exp(g[..., -1])

    def step(S, xs):
        u_c, w_c, qk_c, qd_c, kd_c, gl_c = xs
        v_new = u_c - w_c @ S
        o = qd_c @ S + qk_c @ v_new
        S = S * gl_c[..., None, None] + jnp.swapaxes(kd_c, -1, -2) @ v_new
        return S, o

    xs = tuple(jnp.moveaxis(t, 2, 0) for t in (u, w, qk, q_dec, k_dec, g_last))
    S0 = jnp.zeros((B, H, dk, dv), f32)
    _, o = lax.scan(step, S0, xs)
    return jnp.transpose(o, (1, 0, 3, 2, 4)).reshape(B, T, H, dv)


def hybrid_layer(x, c, positions, w_ada, b_ada, norm_w, w_in, q_norm_w, k_norm_w, sinks,
                 conv_w, a_log, dt_bias, dn_norm_w, w_o_swa, w_o_dn, w_out):
    B, T, _ = x.shape
    f32 = jnp.float32
    mod = jax.nn.silu(c) @ w_ada + b_ada
    shift, scale, gate = jnp.split(mod[:, None, :], 3, axis=-1)
    h = rms_norm(x, norm_w) * (1.0 + scale) + shift
    proj = h @ w_in
    aq, ak, av, ag, d_qkv, dz, db, da, mg_a, mg_b = jnp.split(proj, _split_points(), axis=-1)

    aq = rms_norm(aq.reshape(B, T, SWA_Q_HEADS, SWA_HEAD_DIM), q_norm_w)
    ak = rms_norm(ak.reshape(B, T, SWA_KV_HEADS, SWA_HEAD_DIM), k_norm_w)
    aq = partial_rope(aq, positions)
    ak = partial_rope(ak, positions)
    av = av.reshape(B, T, SWA_KV_HEADS, SWA_HEAD_DIM)
    a_out = swa_attention(aq, ak, av, sinks).astype(x.dtype) * jax.nn.silu(ag)
    y_a = a_out @ w_o_swa

    d_qkv = causal_conv_silu(d_qkv, conv_w)
    dq, dk, dv = jnp.split(d_qkv, [DN_KEY, 2 * DN_KEY], axis=-1)
    rep = DN_V_HEADS // DN_K_HEADS
    dq = jnp.repeat(l2_norm(dq.reshape(B, T, DN_K_HEADS, DN_HEAD_K)), rep, axis=2)
    dk = jnp.repeat(l2_norm(dk.reshape(B, T, DN_K_HEADS, DN_HEAD_K)), rep, axis=2)
    dv = dv.reshape(B, T, DN_V_HEADS, DN_HEAD_V)
    beta = jax.nn.sigmoid(db.astype(f32))
    g = -jnp.exp(a_log.astype(f32)) * jax.nn.softplus(da.astype(f32) + dt_bias.astype(f32))
    o = gated_delta_rule(dq, dk, dv, g, beta)
    o = rms_norm(o, dn_norm_w) * jax.nn.silu(dz.reshape(B, T, DN_V_HEADS, DN_HEAD_V).astype(f32))
    y_b = o.reshape(B, T, DN_VAL).astype(x.dtype) @ w_o_dn

    y = jax.nn.sigmoid(mg_a) * y_a + jax.nn.sigmoid(mg_b) * y_b
    return x + gate * (y @ w_out)


def setup_inputs(seed: int = 0) -> dict:
    key = jax.random.key(seed)
    ks = jax.random.split(key, 20)
    D = D_MODEL
    nrm = jax.random.normal
    x = nrm(ks[0], (BATCH, SEQ, D), jnp.float32)
    c = nrm(ks[1], (BATCH, D), jnp.float32)
    offs = jax.random.randint(ks[2], (BATCH, 1), 0, 4096, dtype=jnp.int32)
    positions = (jnp.arange(SEQ, dtype=jnp.int32)[None, :] + offs).astype(jnp.int32)
    w_ada = nrm(ks[3], (DEPTH, D, 3 * D), jnp.float32) * (0.5 * D ** -0.5)
    b_ada = nrm(ks[4], (DEPTH, 3 * D), jnp.float32) * 0.01
    norm_w = 1.0 + 0.02 * nrm(ks[5], (DEPTH, D), jnp.float32)
    w_in = nrm(ks[6], (DEPTH, D, IN_WIDTH), jnp.float32) * (D ** -0.5)
    q_norm_w = 1.0 + 0.02 * nrm(ks[7], (DEPTH, SWA_HEAD_DIM), jnp.float32)
    k_norm_w = 1.0 + 0.02 * nrm(ks[8], (DEPTH, SWA_HEAD_DIM), jnp.float32)
    sinks = nrm(ks[9], (DEPTH, SWA_Q_HEADS), jnp.float32)
    conv_w = nrm(ks[10], (DEPTH, DN_CONV, DN_CONV_CH), jnp.float32) * 0.5
    a_log = jnp.log(jax.random.uniform(ks[11], (DEPTH, DN_V_HEADS), jnp.float32, 1.0, 16.0))
    dt = jnp.exp(jax.random.uniform(ks[12], (DEPTH, DN_V_HEADS), jnp.float32,
                                    float(np.log(1e-3)), float(np.log(1e-1))))
    dt_bias = dt + jnp.log(-jnp.expm1(-dt))
    dn_norm_w = 1.0 + 0.02 * nrm(ks[13], (DEPTH, DN_HEAD_V), jnp.float32)
    w_o_swa = nrm(ks[14], (DEPTH, SWA_Q, D), jnp.float32) * (SWA_Q ** -0.5)
    w_o_dn = nrm(ks[15], (DEPTH, DN_VAL, D), jnp.float32) * (DN_VAL ** -0.5)
    w_out = nrm(ks[16], (DEPTH, D, D), jnp.float32) * (D ** -0.5)
    return {"x": x, "c": c, "positions": positions, "w_ada": w_ada, "b_ada": b_ada,
            "norm_w": norm_w, "w_in": w_in, "q_norm_w": q_norm_w, "k_norm_w": k_norm_w,
            "sinks": sinks, "conv_w": conv_w, "a_log": a_log, "dt_bias": dt_bias,
            "dn_norm_w": dn_norm_w, "w_o_swa": w_o_swa, "w_o_dn": w_o_dn, "w_out": w_out}


def reference(x, c, positions, w_ada, b_ada, norm_w, w_in, q_norm_w, k_norm_w, sinks,
              conv_w, a_log, dt_bias, dn_norm_w, w_o_swa, w_o_dn, w_out):
    for l in range(DEPTH):
        x = hybrid_layer(x, c, positions, w_ada[l], b_ada[l], norm_w[l], w_in[l], q_norm_w[l],
                         k_norm_w[l], sinks[l], conv_w[l], a_log[l], dt_bias[l], dn_norm_w[l],
                         w_o_swa[l], w_o_dn[l], w_out[l])
    return x
```

```python
import numpy as np
from contextlib import ExitStack
import concourse.bass as bass
import concourse.mybir as mybir
from concourse.bass_utils import run_bass_kernel_spmd

F32 = mybir.dt.float32
BF16 = mybir.dt.bfloat16
I32 = mybir.dt.int32
AF = mybir.ActivationFunctionType
ALU = mybir.AluOpType
AX = mybir.AxisListType

D = 2048
T = 4096
EPS = 1e-6
NCT = 43
SAME_ENGINE_SYNC = True


class Prog:
    def __init__(self):
        self.ins = []
        self.keys = set()
        self.ep = ()

    def op(self, eng, fn, r=(), w=()):
        self.keys.update(r); self.keys.update(w)
        self.ins.append(dict(eng=eng, fn=fn, r=tuple(r) + self.ep, w=tuple(w), dma=None))

    def dma(self, eng, fn, r=(), w=(), sem=None, inc=16):
        assert sem is not None
        self.keys.update(r); self.keys.update(w)
        self.ins.append(dict(eng=eng, fn=fn, r=tuple(r) + self.ep, w=tuple(w), dma=sem, inc=inc))

    def barrier(self, fn, exclude=()):
        ks = [k for k in sorted(self.keys, key=str) if k not in exclude]
        self.ins.append(dict(eng="gpsimd", fn=fn, r=(), w=tuple(ks) + ("__epoch",), dma=None))
        self.ep = ("__epoch",)

    def finalize(self):
        ins = self.ins
        last_w, readers, last_dma = {}, {}, {}
        for idx, I in enumerate(ins):
            deps = set()
            for k in I["r"]:
                if k in last_w:
                    deps.add(last_w[k])
            for k in I["w"]:
                if k in last_w:
                    deps.add(last_w[k])
                deps.update(readers.get(k, ()))
            if I["dma"] is not None:
                if I["dma"] in last_dma:
                    deps.add(last_dma[I["dma"]])
                last_dma[I["dma"]] = idx
            deps.discard(idx)
            I["deps"] = deps
            for k in I["r"]:
                readers.setdefault(k, []).append(idx)
            for k in I["w"]:
                last_w[k] = idx
                readers[k] = []
        need = [False] * len(ins)
        for idx, I in enumerate(ins):
            for d in I["deps"]:
                P = ins[d]
                if P["dma"] is not None:
                    continue
                if P["eng"] != I["eng"]:
                    need[d] = True
                elif SAME_ENGINE_SYNC and I["eng"] != "tensor":
                    need[d] = True
        cnt, dcnt = {}, {}
        for idx, I in enumerate(ins):
            if I["dma"] is not None:
                dcnt[I["dma"]] = dcnt.get(I["dma"], 0) + I["inc"]
                I["sig"] = (("dma", I["dma"]), dcnt[I["dma"]], I["inc"])
            elif need[idx]:
                cnt[I["eng"]] = cnt.get(I["eng"], 0) + 1
                I["sig"] = (("eng", I["eng"]), cnt[I["eng"]], 1)
            else:
                I["sig"] = None
        known = {}
        for idx, I in enumerate(ins):
            waits = {}
            for d in I["deps"]:
                P = ins[d]
                if P["sig"] is None:
                    continue
                if P["dma"] is None and P["eng"] == I["eng"] and (I["eng"] == "tensor" or not SAME_ENGINE_SYNC):
                    continue
                s, v, _ = P["sig"]
                waits[s] = max(waits.get(s, 0), v)
            kn = known.setdefault(I["eng"], {})
            out = []
            for s, v in waits.items():
                if kn.get(s, 0) >= v:
                    continue
                kn[s] = v
                out.append((s, v))
            I["waits"] = out
        self.dma_keys = sorted(dcnt.keys(), key=str)
        return self

    def emit(self, nc, es):
        engs = ["tensor", "vector", "scalar", "gpsimd", "sync"]
        sems = {}
        for e in engs:
            sems[("eng", e)] = es.enter_context(nc.semaphore("se_" + e))
        for i, k in enumerate(self.dma_keys):
            sems[("dma", k)] = es.enter_context(nc.semaphore("sd_%d" % i))
        per = {e: [I for I in self.ins if I["eng"] == e] for e in engs}
        block = es.enter_context(nc.Block())

        def make(name):
            def body(e):
                for I in per[name]:
                    for s, v in I["waits"]:
                        e.wait_ge(sems[s], v)
                    bi = I["fn"](e)
                    if I["sig"] is not None:
                        s, v, inc = I["sig"]
                        bi.then_inc(sems[s], inc)
            return body

        block.tensor(make("tensor"))
        block.vector(make("vector"))
        block.scalar(make("scalar"))
        block.gpsimd(make("gpsimd"))
        block.sync(make("sync"))


class _Stop(Exception):
    pass


def build(debug=False, stop_after=99, upto=None):
    nc = bass.Bass("TRN2", target_bir_lowering=False)
    P = Prog()
    es = ExitStack()

    def din(name, shape, dt=F32):
        return nc.dram_tensor(name, list(shape), dt, kind="ExternalInput").ap()

    def dscr(name, shape, dt=BF16):
        if debug:
            return nc.dram_tensor(name, list(shape), dt, kind="ExternalOutput").ap()
        return nc.dram_tensor(name, list(shape), dt).ap()

    def sb(name, shape, dt=F32):
        return es.enter_context(nc.sbuf_tensor(name, list(shape), dt))

    def ps(name, shape, dt=F32):
        return es.enter_context(nc.psum_tensor(name, list(shape), dt))

    xT = din("xT", [T // 128, 128, 16, 128])
    cT = din("cT", [128, 16])
    wada = din("wada", [36, 128, 16, 128])
    bada = din("bada", [128, 36])
    nw = din("nw", [128, 16])
    wa = din("wa", [NCT, 128, 16, 128])
    cw = din("cw", [128, 16, 4])
    s_dn = dscr("s_dn", [16, 128, T])
    s_z = dscr("s_z", [8, 128, T])
    s_hT = dscr("s_hT", [16, 128, T]) if debug else None

    ones_bf = sb("ones_bf", [128, 128], BF16)
    big = sb("big", [128, 32768])
    hT = big[:, :].bitcast(BF16).rearrange("p (a b) -> p a b", a=16)
    mod = sb("mod", [128, 36])
    gam = sb("gam", [128, 16])
    cT_s = sb("cT_s", [128, 16])
    sc_s = sb("sc_s", [128, 16])
    bada_s = sb("bada_s", [128, 36])
    nw_s = sb("nw_s", [128, 16])
    cw_s = sb("cw_s", [128, 16, 4])
    P.op("gpsimd", lambda e: e.memset(ones_bf[:, :], 1.0), w=["ones_bf"])
    onesD_bf = sb("onesD_bf", [128, 128], BF16)
    P.op("gpsimd", lambda e: e.memset(onesD_bf[:, :], 1.0 / D), w=["onesD_bf"])
    P.dma("sync", lambda e: e.dma_start(out=cT_s[:, :], in_=cT[:, :]), w=["cT_s"], sem="small0")
    P.dma("sync", lambda e: e.dma_start(out=bada_s[:, :], in_=bada[:, :]), w=["bada_s"], sem="small1")
    P.dma("sync", lambda e: e.dma_start(out=nw_s[:, :], in_=nw[:, :]), w=["nw_s"], sem="small2")
    P.dma("sync", lambda e: e.dma_start(out=cw_s[:, :, :], in_=cw[:, :, :]), w=["cw_s"], sem="small3")

    psA = [ps("psA%d" % i, [128, 512]) for i in range(4)]
    psB = [ps("psB%d" % i, [128, 512]) for i in range(2)]
    psC = [ps("psC%d" % i, [128, 512]) for i in range(2)]

    P.op("scalar", lambda e: e.activation(out=sc_s[:, :], in_=cT_s[:, :], func=AF.Silu), r=["cT_s"], w=["sc_s"])
    wad = [sb("wad%d" % i, [128, 16, 128]) for i in range(2)]
    for t in range(36):
        buf = wad[t % 2]
        key = "wad%d" % (t % 2)
        P.dma("sync", lambda e, buf=buf, t=t: e.dma_start(out=buf[:, :, :], in_=wada[t]), w=[key], sem=key)
        for kt in range(16):
            P.op("tensor", lambda e, buf=buf, t=t, kt=kt: e.matmul(
                psA[0][:, t:t + 1], lhsT=buf[:, kt, :], rhs=sc_s[:, kt:kt + 1], start=(kt == 0), stop=(kt == 15)),
                r=[key, "sc_s"], w=["psA0"])
    P.op("vector", lambda e: e.tensor_tensor(out=mod[:, :], in0=psA[0][:, 0:36], in1=bada_s[:, :], op=ALU.add),
         r=["psA0", "bada_s"], w=["mod"])
    P.op("vector", lambda e: e.scalar_tensor_tensor(out=gam[:, :], in0=mod[:, 16:32], scalar=1.0, in1=nw_s[:, :],
                                                     op0=ALU.add, op1=ALU.mult), r=["mod", "nw_s"], w=["gam"])

    TB1 = 128
    xb = [sb("xb%d" % i, [128, 16, TB1]) for i in range(2)]
    sq = sb("sq", [128, 16, TB1], BF16)
    rstd = sb("rstd", [128, TB1])
    for tb in range(T // TB1):
        xbuf = xb[tb % 2]
        xk = "xb%d" % (tb % 2)
        t0 = tb * TB1
        P.dma("sync", lambda e, xbuf=xbuf, t0=t0: e.dma_start(out=xbuf[:, :, :], in_=xT[t0 // TB1]),
              w=[xk], sem=xk)
        P.op("scalar", lambda e, xbuf=xbuf: e.activation(out=sq[:, :, :], in_=xbuf[:, :, :], func=AF.Square),
             r=[xk], w=["sq"])
        pb = psB[tb % 2]
        pk = "psB%d" % (tb % 2)
        for dt in range(16):
            P.op("tensor", lambda e, pb=pb, dt=dt: e.matmul(pb[:, 0:TB1], lhsT=onesD_bf[:, :], rhs=sq[:, dt, :],
                                                           start=(dt == 0), stop=(dt == 15)),
                 r=["sq", "onesD_bf"], w=[pk])
        P.op("scalar", lambda e, pb=pb: e.activation(out=rstd[:, :], in_=pb[:, 0:TB1], func=AF.Sqrt, bias=EPS, scale=1.0),
             r=[pk], w=["rstd"])
        P.op("vector", lambda e: e.reciprocal(out=rstd[:, :], in_=rstd[:, :]), r=["rstd"], w=["rstd"])
        P.op("vector", lambda e, xbuf=xbuf: e.tensor_tensor(
            out=xbuf[:, :, :], in0=xbuf[:, :, :], in1=rstd[:, :].unsqueeze(1).to_broadcast([128, 16, TB1]), op=ALU.mult),
            r=[xk, "rstd"], w=[xk])
        for dt in range(16):
            P.op("scalar", lambda e, t0=t0, xbuf=xbuf, dt=dt: e.activation(
                out=hT[:, dt, t0:t0 + TB1], in_=xbuf[:, dt, :], func=AF.Identity, bias=mod[:, dt:dt + 1], scale=gam[:, dt:dt + 1]),
                r=[xk, "mod", "gam"], w=["hT%d_%d" % (tb, dt)])
    hT_keys = ["hT%d_%d" % (tb, dt) for tb in range(T // TB1) for dt in range(16)]
    if debug:
        P.dma("sync", lambda e: e.dma_start(out=s_hT.rearrange("k p t -> p k t"), in_=hT[:, :, :]),
              r=hT_keys, w=["s_hT"], sem="dbg")

    pos = din("pos", [1, T], I32)
    cst = din("cst", [128, 1536])
    sm = din("sm", [128, 32])
    wos = din("wos", [4, 128, 16, 128])
    wod = din("wod", [4, 128, 32, 128])
    wout = din("wout", [4, 128, 16, 128])
    xres = din("xres", [4, 128, T])
    outT = nc.dram_tensor("outT", [512, T], F32, kind="ExternalOutput").ap()
    s_aq = dscr("s_aq", [4, 128, T])
    s_kk = dscr("s_kk", [128, T])
    s_vv = dscr("s_vv", [128, T])
    s_ag = dscr("s_ag", [4, 128, T])
    s_sa = dscr("s_sa", [4, 128, T])
    s_sb = dscr("s_sb", [4, 128, T])
    g_src = dscr("g_src", [12 * 128, T])
    g_dst = nc.dram_tensor("g_dst", [48 * 128, T], BF16).ap()
    y_src = dscr("y_src", [4 * 128, T])
    y_dst = nc.dram_tensor("y_dst", [16 * 128, T], BF16).ap()
    cst_s = sb("cst_s", [128, 1536])
    cst_bf = sb("cst_bf", [128, 1536], BF16)
    sm_s = sb("sm_s", [128, 32])
    P.dma("sync", lambda e: e.dma_start(out=cst_s[:, :], in_=cst[:, :]), w=["cst_s"], sem="small0")
    P.dma("sync", lambda e: e.dma_start(out=sm_s[:, :], in_=sm[:, :]), w=["sm_s"], sem="small1")
    P.op("vector", lambda e: e.tensor_copy(out=cst_bf[:, :], in_=cst_s[:, :]), r=["cst_s"], w=["cst_bf"])
    I_bf = cst_bf[:, 0:128]
    blk64_bf = cst_bf[:, 128:256]
    Pm_bf = cst_bf[:, 256:384]
    tri64 = cst_s[0:64, 384:448]
    sel63 = cst_s[0:64, 448:576]
    mU = cst_s[0:64, 384:448]
    mUs = cst_s[0:64, 640:704]
    maskCP = cst_bf[:, 704:960]
    invf = cst_s[:, 960:961]
    onesE_bf = cst_bf[:, 1024:1152]
    onesO_bf = cst_bf[:, 1152:1280]
    I64f = cst_s[0:64, 0:64]
    PI = float(np.pi)

    try:
        wb = [sb("wb%d" % i, [128, 16, 128], BF16) for i in range(2)]
        QT = 1024
        Y = [sb("Y%d" % i, [128, 3 + QT]) for i in range(2)]
        acc = sb("acc", [128, QT])
        sil = sb("sil", [128, QT])
        sqb = sb("sqb", [128, QT], BF16)
        rs2 = sb("rs2", [128, QT])
        sqf = sq[:, :, :].rearrange("p a b -> p (a b)")
        ob = [sqf[:, i * QT:(i + 1) * QT] for i in range(2)]
        Cf = wad[0][:, :, :].rearrange("p a b -> p (a b)").bitcast(BF16)
        Sf = wad[1][:, :, :].rearrange("p a b -> p (a b)").bitcast(BF16)
        rs2i = rs2[:, :].bitcast(I32)
        for qi in range(4):
            q0 = qi * QT
            P.dma("sync", lambda e, q0=q0: e.dma_start(out=rs2i, in_=pos[0:1, q0:q0 + QT].partition_broadcast(128)),
                  w=["rs2"], sem="posld")
            P.op("vector", lambda e: e.tensor_copy(out=acc[:, :], in_=rs2i), r=["rs2"], w=["acc"])
            P.op("vector", lambda e: e.tensor_scalar(out=acc[:, :], in0=acc[:, :], scalar1=invf, scalar2=None, op0=ALU.mult),
                 r=["acc", "cst_s"], w=["acc"])
            for tab, tk, offs in ((Sf, "wad1", 0.0), (Cf, "wad0", PI / 2)):
                if offs != 0.0:
                    P.op("vector", lambda e, offs=offs: e.tensor_scalar(out=acc[:, :], in0=acc[:, :], scalar1=offs, scalar2=None,
                                                                        op0=ALU.add), r=["acc"], w=["acc"])
                P.op("vector", lambda e: e.tensor_scalar(out=sil[:, :], in0=acc[:, :], scalar1=1.0 / (2 * PI), scalar2=None,
                                                         op0=ALU.mult), r=["acc"], w=["sil"])
                P.op("vector", lambda e: e.tensor_copy(out=rs2i, in_=sil[:, :]), r=["sil"], w=["rs2"])
                P.op("vector", lambda e: e.tensor_copy(out=sil[:, :], in_=rs2i), r=["rs2"], w=["sil"])
                P.op("vector", lambda e: e.scalar_tensor_tensor(out=sil[:, :], in0=sil[:, :], scalar=-2 * PI, in1=acc[:, :],
                                                                op0=ALU.mult, op1=ALU.add), r=["sil", "acc"], w=["sil"])
                P.op("vector", lambda e: e.tensor_scalar(out=rs2[:, :], in0=sil[:, :], scalar1=PI, scalar2=None, op0=ALU.is_gt),
                     r=["sil"], w=["rs2"])
                P.op("vector", lambda e: e.scalar_tensor_tensor(out=sil[:, :], in0=rs2[:, :], scalar=-2 * PI, in1=sil[:, :],
                                                                op0=ALU.mult, op1=ALU.add), r=["sil", "rs2"], w=["sil"])
                P.op("vector", lambda e: e.tensor_scalar(out=rs2[:, :], in0=sil[:, :], scalar1=-PI, scalar2=None, op0=ALU.is_lt),
                     r=["sil"], w=["rs2"])
                P.op("vector", lambda e: e.scalar_tensor_tensor(out=sil[:, :], in0=rs2[:, :], scalar=2 * PI, in1=sil[:, :],
                                                                op0=ALU.mult, op1=ALU.add), r=["sil", "rs2"], w=["sil"])
                P.op("scalar", lambda e, tab=tab, q0=q0: e.activation(out=tab[:, q0:q0 + QT], in_=sil[:, :], func=AF.Sin),
                     r=["sil"], w=[tk])

        if upto == "tab":
            raise _Stop()
        yc = 0
        oc = 0
        pa = 0
        n_ct = min(NCT - 1, stop_after)
        P.dma("gpsimd", lambda e: e.dma_start(out=wb[0][:, :, :], in_=wa[0]), w=["wb0"], sem="wb0")
        for ct in range(n_ct):
            wbuf = wb[ct % 2]
            wk = "wb%d" % (ct % 2)
            if ct + 1 < NCT:
                P.dma("gpsimd", lambda e, ct=ct: e.dma_start(out=wb[(ct + 1) % 2][:, :, :], in_=wa[ct + 1]),
                      w=["wb%d" % ((ct + 1) % 2)], sem="wb%d" % ((ct + 1) % 2))
            if ct < 16:
                kind = "conv"
            elif ct < 24:
                kind, fn, dst, dkey = "act", AF.Silu, s_z[ct - 16], "s_z%d" % (ct - 16)
            elif ct < 28:
                kind, dst, dkey = "rope", s_aq[ct - 24], "s_aq%d" % (ct - 24)
            elif ct == 28:
                kind, dst, dkey = "rope", s_kk, "s_kk"
            elif ct == 29:
                kind, fn, dst, dkey = "act", AF.Copy, s_vv, "s_vv"
            elif ct < 34:
                kind, fn, dst, dkey = "act", AF.Silu, s_ag[ct - 30], "s_ag%d" % (ct - 30)
            elif ct < 38:
                kind, fn, dst, dkey = "act", AF.Sigmoid, s_sa[ct - 34], "s_sa%d" % (ct - 34)
            else:
                kind, fn, dst, dkey = "act", AF.Sigmoid, s_sb[ct - 38], "s_sb%d" % (ct - 38)
            for qi in range(T // QT):
                q0 = qi * QT
                if kind in ("conv", "rope"):
                    Yb = Y[yc % 2]
                    Yk = "Y%d" % (yc % 2)
                    Yp = Y[(yc + 1) % 2]
                    Ypk = "Y%d" % ((yc + 1) % 2)
                    yc += 1
                    if kind == "conv":
                        if qi == 0:
                            P.op("gpsimd", lambda e, Yb=Yb: e.memset(Yb[:, 0:3], 0.0), w=[Yk])
                        else:
                            P.op("gpsimd", lambda e, Yb=Yb, Yp=Yp: e.tensor_copy(out=Yb[:, 0:3], in_=Yp[:, QT:QT + 3]),
                                 r=[Ypk], w=[Yk])
                obuf = ob[oc % 2]
                okey = "ob%d" % (oc % 2)
                oc += 1
                for bi in range(QT // 512):
                    t0 = q0 + bi * 512
                    pbank = psA[pa % 4]
                    pkey = "psA%d" % (pa % 4)
                    pa += 1
                    for kt in range(16):
                        P.op("tensor", lambda e, pbank=pbank, wbuf=wbuf, kt=kt, t0=t0: e.matmul(
                            pbank[:, :], lhsT=wbuf[:, kt, :], rhs=hT[:, kt, t0:t0 + 512], start=(kt == 0), stop=(kt == 15)),
                            r=[wk] + ["hT%d_%d" % (t0 // TB1 + i, kt) for i in range(512 // TB1)], w=[pkey])
                    if kind in ("conv", "rope"):
                        P.op("scalar", lambda e, pbank=pbank, Yb=Yb, bi=bi: e.copy(out=Yb[:, 3 + bi * 512:3 + bi * 512 + 512],
                                                                                 in_=pbank[:, :]), r=[pkey], w=[Yk])
                    else:
                        P.op("scalar", lambda e, pbank=pbank, obuf=obuf, bi=bi, fn=fn: e.activation(
                            out=obuf[:, bi * 512:bi * 512 + 512], in_=pbank[:, :], func=fn), r=[pkey], w=[okey])
                if kind == "conv":
                    P.op("vector", lambda e, Yb=Yb, ct=ct: e.tensor_scalar(out=acc[:, :], in0=Yb[:, 0:QT], scalar1=cw_s[:, ct, 0:1],
                                                                           scalar2=None, op0=ALU.mult),
                         r=[Yk, "cw_s"], w=["acc"])
                    P.op("scalar", lambda e, Yb=Yb, ct=ct: e.activation(out=rs2[:, :], in_=Yb[:, 3:3 + QT], func=AF.Copy, scale=cw_s[:, ct, 3:4]),
                         r=[Yk, "cw_s"], w=["rs2"])
                    P.op("gpsimd", lambda e, Yb=Yb, ct=ct: e.tensor_scalar(out=sil[:, :], in0=Yb[:, 2:2 + QT], scalar1=cw_s[:, ct, 2:3],
                                                                           scalar2=None, op0=ALU.mult),
                         r=[Yk, "cw_s"], w=["sil"])
                    P.op("vector", lambda e, Yb=Yb, ct=ct: e.scalar_tensor_tensor(
                        out=acc[:, :], in0=Yb[:, 1:1 + QT], scalar=cw_s[:, ct, 1:2], in1=acc[:, :],
                        op0=ALU.mult, op1=ALU.add), r=[Yk, "cw_s", "acc"], w=["acc"])
                    P.op("gpsimd", lambda e: e.tensor_tensor(out=sil[:, :], in0=sil[:, :], in1=rs2[:, :], op=ALU.add),
                         r=["sil", "rs2"], w=["sil"])
                    P.op("vector", lambda e: e.tensor_tensor(out=acc[:, :], in0=acc[:, :], in1=sil[:, :], op=ALU.add),
                         r=["acc", "sil"], w=["acc"])
                    if ct >= 8:
                        P.op("scalar", lambda e, obuf=obuf: e.activation(out=obuf[:, :], in_=acc[:, :], func=AF.Silu),
                             r=["acc"], w=[okey])
                    else:
                        P.op("scalar", lambda e: e.activation(out=sil[:, :], in_=acc[:, :], func=AF.Silu), r=["acc"], w=["sil"])
                        P.op("scalar", lambda e: e.activation(out=sqb[:, :], in_=sil[:, :], func=AF.Square), r=["sil"], w=["sqb"])
                        for bi in range(QT // 512):
                            pb = psB[bi]
                            pk = "psB%d" % bi
                            P.op("tensor", lambda e, pb=pb, bi=bi: e.matmul(pb[:, :], lhsT=ones_bf[:, :],
                                                                           rhs=sqb[:, bi * 512:bi * 512 + 512], start=True, stop=True),
                                 r=["sqb", "ones_bf"], w=[pk])
                            P.op("scalar", lambda e, pb=pb, bi=bi: e.activation(
                                out=rs2[:, bi * 512:bi * 512 + 512], in_=pb[:, :], func=AF.Sqrt, bias=EPS, scale=1.0),
                                r=[pk], w=["rs2"])
                        P.op("vector", lambda e: e.reciprocal(out=rs2[:, :], in_=rs2[:, :]), r=["rs2"], w=["rs2"])
                        qscale = (128.0 ** -0.5) if ct < 4 else 1.0
                        P.op("vector", lambda e, obuf=obuf, qscale=qscale: e.scalar_tensor_tensor(
                            out=obuf[:, :], in0=sil[:, :], scalar=qscale, in1=rs2[:, :], op0=ALU.mult, op1=ALU.mult),
                            r=["sil", "rs2"], w=[okey])
                    dst = s_dn[ct]
                    dkey = "s_dn%d" % ct
                elif kind == "rope":
                    isq = ct < 28
                    P.op("scalar", lambda e, Yb=Yb: e.activation(out=sqb[:, :], in_=Yb[:, 3:3 + QT], func=AF.Square), r=[Yk], w=["sqb"])
                    for bi in range(QT // 512):
                        pb = psB[bi]
                        pk = "psB%d" % bi
                        P.op("tensor", lambda e, pb=pb, bi=bi: e.matmul(pb[:, :], lhsT=blk64_bf, rhs=sqb[:, bi * 512:bi * 512 + 512],
                                                                       start=True, stop=True), r=["sqb", "cst_bf"], w=[pk])
                        sc_ = 64.0 if isq else 1.0
                        P.op("scalar", lambda e, pb=pb, bi=bi, sc_=sc_: e.activation(
                            out=rs2[:, bi * 512:bi * 512 + 512], in_=pb[:, :], func=AF.Sqrt, bias=EPS * sc_, scale=sc_),
                            r=[pk], w=["rs2"])
                    P.op("vector", lambda e: e.reciprocal(out=rs2[:, :], in_=rs2[:, :]), r=["rs2"], w=["rs2"])
                    nwc = sm_s[:, 0:1] if isq else sm_s[:, 1:2]
                    P.op("vector", lambda e, Yb=Yb, nwc=nwc: e.scalar_tensor_tensor(
                        out=sqb[:, :], in0=Yb[:, 3:3 + QT], scalar=nwc, in1=rs2[:, :], op0=ALU.mult, op1=ALU.mult),
                        r=[Yk, "rs2", "sm_s"], w=["sqb"])
                    P.op("vector", lambda e, q0=q0: e.tensor_tensor(out=acc[:, :], in0=sqb[:, :], in1=Cf[:, q0:q0 + QT], op=ALU.mult),
                         r=["sqb", "wad0"], w=["acc"])
                    for bi in range(QT // 512):
                        pb = psC[bi]
                        pk = "psC%d" % bi
                        P.op("tensor", lambda e, pb=pb, bi=bi: e.matmul(pb[:, :], lhsT=Pm_bf, rhs=sqb[:, bi * 512:bi * 512 + 512],
                                                                       start=True, stop=True), r=["sqb", "cst_bf"], w=[pk])
                        P.op("vector", lambda e, pb=pb, bi=bi, q0=q0: e.tensor_tensor(
                            out=sil[:, bi * 512:bi * 512 + 512], in0=pb[:, :], in1=Sf[:, q0 + bi * 512:q0 + bi * 512 + 512], op=ALU.mult),
                            r=[pk, "wad1"], w=["sil"])
                    P.op("gpsimd", lambda e, obuf=obuf: e.tensor_tensor(out=obuf[:, :], in0=acc[:, :], in1=sil[:, :], op=ALU.add),
                         r=["acc", "sil"], w=[okey])
                P.dma("sync", lambda e, dst=dst, obuf=obuf, q0=q0: e.dma_start(out=dst[:, q0:q0 + QT], in_=obuf[:, :]),
                      r=[okey], w=[dkey], sem="st_" + okey)

        if upto == "ct":
            raise _Stop()
        xf0 = xb[0][:, :, :].rearrange("p a b -> p (a b)")
        xf1 = xb[1][:, :, :].rearrange("p a b -> p (a b)")
        v3 = lambda ap: ap.rearrange("p (n c) -> p n c", c=8)
        beta_t = v3(xf0[0:64, 0:512])
        gc_t = v3(xf0[0:64, 512:1024])
        eg_t = v3(xf0[0:64, 1024:1536])
        ekd_t = v3(xf0[0:64, 1536:2048])
        bge_t = v3(xf1[0:64, 0:512])
        egl_t = v3(xf1[:, 512:1024])
        negA = xf1[0:64, 1024:1032]
        wbuf = wb[(NCT - 1) % 2]
        wk = "wb%d" % ((NCT - 1) % 2)
        for n in range(64):
            pbank = psA[n // 32]
            pkey = "psA%d" % (n // 32)
            for kt in range(16):
                P.op("tensor", lambda e, pbank=pbank, wbuf=wbuf, kt=kt, n=n: e.matmul(
                    pbank[0:64, (n % 32) * 16:(n % 32) * 16 + 16], lhsT=hT[:, kt, n * 64:n * 64 + 64], rhs=wbuf[:, kt, 0:16],
                    start=(kt == 0), stop=(kt == 15)), r=[wk, "hT%d_%d" % (n * 64 // TB1, kt)], w=[pkey])
        bgraw = acc[0:64, :].rearrange("p (n c) -> p n c", c=16)
        tmpb = sil[0:64, 0:512].rearrange("p (n c) -> p n c", c=8)
        for half in range(2):
            P.op("scalar", lambda e, half=half: e.copy(out=acc[0:64, half * 512:half * 512 + 512], in_=psA[half][0:64, :]),
                 r=["psA%d" % half], w=["acc"])
        if upto == "bg1":
            raise _Stop()
        P.op("scalar", lambda e: e.activation(out=beta_t, in_=bgraw[:, :, 0:8], func=AF.Sigmoid), r=["acc"], w=["beta_t"])
        P.op("vector", lambda e: e.tensor_tensor(out=tmpb, in0=bgraw[:, :, 8:16],
                                                 in1=sm_s[0:64, 16:24].unsqueeze(1).to_broadcast([64, 64, 8]), op=ALU.add),
             r=["acc", "sm_s"], w=["sil"])
        P.op("scalar", lambda e: e.activation(out=tmpb, in_=tmpb, func=AF.Exp), r=["sil"], w=["sil"])
        P.op("scalar", lambda e: e.activation(out=tmpb, in_=tmpb, func=AF.Ln, bias=1.0, scale=1.0), r=["sil"], w=["sil"])
        P.op("scalar", lambda e: e.activation(out=negA, in_=sm_s[0:64, 8:16], func=AF.Exp), r=["sm_s"], w=["negA"])
        P.op("vector", lambda e: e.tensor_scalar(out=negA, in0=negA, scalar1=-1.0, scalar2=None, op0=ALU.mult),
             r=["negA"], w=["negA"])
        P.op("vector", lambda e: e.tensor_tensor(out=tmpb, in0=tmpb, in1=negA.unsqueeze(1).to_broadcast([64, 64, 8]), op=ALU.mult),
             r=["sil", "negA"], w=["sil"])
        if upto == "bg2":
            raise _Stop()
        P.op("tensor", lambda e: e.matmul(psB[0][0:64, :], lhsT=tri64, rhs=sil[0:64, 0:512], start=True, stop=True),
             r=["sil", "cst_s"], w=["psB0"])
        gcf = xf0[0:64, 512:1024]
        P.op("vector", lambda e: e.tensor_copy(out=gcf, in_=psB[0][0:64, :]), r=["psB0"], w=["gc_t"])
        if upto == "bg3":
            raise _Stop()
        P.op("tensor", lambda e: e.matmul(psB[1][:, :], lhsT=sel63, rhs=gcf, start=True, stop=True), r=["gc_t", "cst_s"], w=["psB1"])
        P.op("scalar", lambda e: e.activation(out=xf1[:, 512:1024], in_=psB[1][:, :], func=AF.Exp),
             r=["psB1"], w=["egl_t"])
        if upto == "bg4":
            raise _Stop()
        P.op("vector", lambda e: e.tensor_tensor(out=sil[0:64, 512:1024], in0=psB[1][0:64, :], in1=gcf, op=ALU.subtract),
             r=["egl_t", "gc_t"], w=["sil"])
        if upto == "bg4a":
            raise _Stop()
        P.op("scalar", lambda e: e.activation(out=xf0[0:64, 1536:2048], in_=sil[0:64, 512:1024], func=AF.Exp),
             r=["sil"], w=["ekd_t"])
        if upto == "bg4b":
            raise _Stop()
        P.op("scalar", lambda e: e.activation(out=xf0[0:64, 1024:1536], in_=gcf, func=AF.Exp),
             r=["gc_t"], w=["eg_t"])
        if upto == "bg4c":
            raise _Stop()
        P.op("vector", lambda e: e.tensor_tensor(out=bge_t, in0=beta_t, in1=eg_t, op=ALU.mult),
             r=["beta_t", "eg_t"], w=["bge_t"])
        if upto == "bg5":
            raise _Stop()
        esink = xf1[:, 1040:1044]
        P.op("scalar", lambda e: e.activation(out=esink, in_=sm_s[:, 2:6], func=AF.Exp), r=["sm_s"], w=["esink"])

        if debug:
            dbg_bg = nc.dram_tensor("dbg_bg", [6, 128, 512], F32, kind="ExternalOutput").ap()
            for i_, (ap_, k_) in enumerate(((xf0[0:64, 0:512], "beta_t"), (xf0[0:64, 512:1024], "gc_t"), (xf0[0:64, 1024:1536], "eg_t"),
                                            (xf0[0:64, 1536:2048], "ekd_t"), (xf1[0:64, 0:512], "bge_t"), (xf1[:, 512:1024], "egl_t"))):
                P.dma("sync", lambda e, i_=i_, ap_=ap_: e.dma_start(out=dbg_bg[i_, 0:ap_.shape[0], :], in_=ap_), r=[k_], w=["dbg_bg%d" % i_], sem="dbg")
        if upto == "p2":
            raise _Stop()
        bar_t = xf1[:, 1048:1052]
        P.barrier(lambda e: e.memset(bar_t[:, 0:1], 0.0))
        bigw = [0]

        def carve(words, dt=F32, shape=None):
            n = words if dt == F32 else (words + 1) // 2
            ap = big[:, bigw[0]:bigw[0] + n]
            bigw[0] += n
            assert bigw[0] <= 32768, bigw[0]
            if dt != F32:
                ap = ap.bitcast(dt)
            return ap
        SB = 512
        swq = [carve(4 * SB, BF16).rearrange("p (a b) -> p a b", a=4) for _ in range(2)]
        swk = [carve(SB, BF16) for _ in range(2)]
        swv = [carve(SB, BF16) for _ in range(2)]
        swg = [carve(4 * SB, BF16).rearrange("p (a b) -> p a b", a=4) for _ in range(2)]
        Vp = [carve(256, BF16).rearrange("p (a b) -> p a b", a=2) for _ in range(2)]
        swqe = carve(4 * SB, BF16).rearrange("p (a b) -> p a b", a=4)
        swqo = carve(4 * SB, BF16).rearrange("p (a b) -> p a b", a=4)
        PT = carve(2048, BF16)
        rden = carve(512)
        aout = carve(512)
        ast = [carve(4 * SB, BF16).rearrange("p (a b) -> p a b", a=4) for _ in range(2)]
        psS = [psA[0], psA[1], psA[2], psA[3]]
        g_src3 = g_src.rearrange("(k p) t -> p k t", p=128)
        for sbi in range(T // SB):
            u = sbi % 2
            c0 = sbi * SB
            P.dma("sync", lambda e, u=u, c0=c0: e.dma_start(out=swq[u][:, :, :], in_=s_aq.rearrange("k p t -> p k t")[:, :, c0:c0 + SB]),
                  r=["s_aq%d" % i for i in range(4)], w=["swq%d" % u], sem="swq%d" % u)
            P.dma("sync", lambda e, u=u, c0=c0: e.dma_start(out=swk[u], in_=s_kk[:, c0:c0 + SB]), r=["s_kk"], w=["swk%d" % u], sem="swk%d" % u)
            P.dma("sync", lambda e, u=u, c0=c0: e.dma_start(out=swv[u], in_=s_vv[:, c0:c0 + SB]), r=["s_vv"], w=["swv%d" % u], sem="swv%d" % u)
            P.dma("sync", lambda e, u=u, c0=c0: e.dma_start(out=swg[u][:, :, :], in_=s_ag.rearrange("k p t -> p k t")[:, :, c0:c0 + SB]),
                  r=["s_ag%d" % i for i in range(4)], w=["swg%d" % u], sem="swg%d" % u)
            P.op("gpsimd", lambda e, u=u: e.tensor_scalar(out=swqe, in0=swq[u], scalar1=cst_s[:, 961:962], scalar2=None, op0=ALU.mult),
                 r=["swq%d" % u, "cst_s"], w=["swqm"])
            P.op("vector", lambda e, u=u: e.tensor_scalar(out=swqo, in0=swq[u], scalar1=cst_s[:, 962:963], scalar2=None, op0=ALU.mult),
                 r=["swq%d" % u, "cst_s"], w=["swqm"])
            for bi in range(SB // 128):
                nb = sbi * 4 + bi
                b0 = bi * 128
                vcur = nb % 2
                if upto == "swa0":
                    raise _Stop()
                P.op("tensor", lambda e, u=u, b0=b0: e.matmul(psB[0][:, 0:128], lhsT=swv[u][:, b0:b0 + 128], rhs=cst_bf[:, 1280:1408],
                                                             start=True, stop=True), r=["swv%d" % u, "cst_bf"], w=["psB0"])
                P.op("tensor", lambda e, u=u, b0=b0: e.matmul(psB[0][:, 128:256], lhsT=swv[u][:, b0:b0 + 128], rhs=cst_bf[:, 1408:1536],
                                                             start=True, stop=True), r=["swv%d" % u, "cst_bf"], w=["psB0"])
                P.op("scalar", lambda e, vcur=vcur: e.copy(out=Vp[vcur][:, :, :].rearrange("p a b -> p (a b)"), in_=psB[0][:, 0:256]),
                     r=["psB0"], w=["Vp%d" % vcur])
                if upto == "swa1":
                    raise _Stop()
                kbs = []
                if nb > 0:
                    if bi == 0:
                        kbs.append((0, swk[1 - u], "swk%d" % (1 - u), SB - 128, 1 - vcur))
                    else:
                        kbs.append((0, swk[u], "swk%d" % u, b0 - 128, 1 - vcur))
                kbs.append((1, swk[u], "swk%d" % u, b0, vcur))
                for kbi, kt_, kk_, ko, _v in kbs:
                    for h in range(8):
                        t, par = h // 2, h % 2
                        col = (kbi * 8 + h) * 128
                        pst = psS[col // 512]
                        qm_ = swqe if par == 0 else swqo
                        P.op("tensor", lambda e, kt_=kt_, ko=ko, qm_=qm_, t=t, b0=b0, pst=pst, col=col: e.matmul(
                            pst[:, col % 512:col % 512 + 128], lhsT=kt_[:, ko:ko + 128],
                            rhs=qm_[:, t, b0:b0 + 128], start=True, stop=True),
                            r=[kk_, "swqm"], w=["psA%d" % (col // 512)])
                if upto == "swa2":
                    raise _Stop()
                lo = 0 if nb > 0 else 2
                for q4 in range(lo, 4):
                    P.op("scalar", lambda e, q4=q4: e.activation(out=PT[:, q4 * 512:q4 * 512 + 512], in_=psS[q4][:, :], func=AF.Exp),
                         r=["psA%d" % q4], w=["PT"])
                for kbi in range(lo // 2, 2):
                    P.op("vector", lambda e, kbi=kbi: e.tensor_tensor(
                        out=PT[:, kbi * 1024:kbi * 1024 + 1024].rearrange("p (h q) -> p h q", h=8),
                        in0=PT[:, kbi * 1024:kbi * 1024 + 1024].rearrange("p (h q) -> p h q", h=8),
                        in1=maskCP[:, kbi * 128:kbi * 128 + 128].unsqueeze(1).to_broadcast([128, 8, 128]), op=ALU.mult),
                        r=["PT", "cst_bf"], w=["PT"])
                if upto == "swa3":
                    raise _Stop()
                for t in range(4):
                    mm = [(kbi, par, _v) for (kbi, _a, _b, _c, _v) in kbs for par in range(2)]
                    for i, (kbi, par, vv_) in enumerate(mm):
                        col = (kbi * 8 + 2 * t + par) * 128
                        P.op("tensor", lambda e, vv_=vv_, par=par, col=col, t=t, i=i, n=len(mm): e.matmul(
                            psC[0][:, t * 128:t * 128 + 128], lhsT=Vp[vv_][:, par, :], rhs=PT[:, col:col + 128],
                            start=(i == 0), stop=(i == n - 1)), r=["Vp%d" % vv_, "PT"], w=["psC0"])
                    for i, (kbi, par, vv_) in enumerate(mm):
                        col = (kbi * 8 + 2 * t + par) * 128
                        oo = onesE_bf if par == 0 else onesO_bf
                        P.op("tensor", lambda e, oo=oo, col=col, t=t, i=i, n=len(mm): e.matmul(
                            psC[1][:, t * 128:t * 128 + 128], lhsT=oo, rhs=PT[:, col:col + 128],
                            start=(i == 0), stop=(i == n - 1)), r=["cst_bf", "PT"], w=["psC1"])
                if upto == "swa4":
                    raise _Stop()
                P.op("scalar", lambda e: e.copy(out=rden, in_=psC[1][:, :]), r=["psC1"], w=["rden"])
                P.op("vector", lambda e: e.tensor_tensor(out=rden.rearrange("p (a b) -> p a b", a=4),
                                                         in0=rden.rearrange("p (a b) -> p a b", a=4),
                                                         in1=esink.unsqueeze(2).to_broadcast([128, 4, 128]), op=ALU.add),
                     r=["rden", "esink"], w=["rden"])
                P.op("vector", lambda e: e.reciprocal(out=rden, in_=rden), r=["rden"], w=["rden"])
                P.op("scalar", lambda e: e.copy(out=aout, in_=psC[0][:, :]), r=["psC0"], w=["aout"])
                P.op("vector", lambda e: e.tensor_tensor(out=aout, in0=aout, in1=rden, op=ALU.mult), r=["aout", "rden"], w=["aout"])
                P.op("gpsimd", lambda e, u=u, b0=b0: e.tensor_tensor(out=ast[u][:, :, b0:b0 + 128],
                                                                     in0=aout.rearrange("p (a b) -> p a b", a=4),
                                                                     in1=swg[u][:, :, b0:b0 + 128], op=ALU.mult),
                     r=["aout", "swg%d" % u], w=["ast%d" % u])
            P.dma("sync", lambda e, u=u, c0=c0: e.dma_start(out=g_src3[:, 0:4, c0:c0 + SB], in_=ast[u][:, :, :]),
                  r=["ast%d" % u], w=["g_src_a"], sem="ast%d" % u)
        RG = [[0, 1, 2, 3], [4, 5, 6, 7]]
        if upto == "swa":
            raise _Stop()
        P.barrier(lambda e: e.memset(bar_t[:, 1:2], 0.0), exclude=["g_dst%d" % k for k in range(4)])
        bigw[0] = 0
        dnin = [carve(16 * SB, BF16).rearrange("p (a b) -> p a b", a=16) for _ in range(2)]
        zin = [carve(8 * SB, BF16).rearrange("p (a b) -> p a b", a=8) for _ in range(2)]
        ogs1 = carve(8 * SB, BF16).rearrange("p (a b) -> p a b", a=8)
        ogs = [ogs1, ogs1]
        S32 = carve(1024).rearrange("p (a b) -> p a b", a=8)
        Sbf = carve(1024, BF16).rearrange("p (a b) -> p a b", a=8)
        P.op("vector", lambda e: e.memset(S32, 0.0), w=["S32"])
        P.op("vector", lambda e: e.memset(Sbf, 0.0), w=["Sbf"])

        def c3(n, m, dt):
            return carve(n * m, dt)[0:64, :].rearrange("p (a b) -> p a b", a=n)

        def two(f):
            return [f(), f()]

        Ktm = two(lambda: c3(4, 128, BF16)); Vtm = two(lambda: c3(8, 128, BF16)); DmT = two(lambda: c3(8, 64, F32))
        M0 = two(lambda: c3(8, 64, BF16)); N0 = two(lambda: c3(8, 64, BF16)); R0 = two(lambda: c3(8, 64, BF16))
        gcb = c3(8, 64, F32); d1 = c3(8, 64, F32); DmTs = c3(8, 64, F32); dgb = c3(8, 64, BF16)
        kbT = carve(8 * 64, BF16).rearrange("p (a b) -> p a b", a=8)
        Mb = [c3(8, 64, BF16) for _ in range(2)]; Nb = [c3(8, 64, BF16) for _ in range(2)]; Rb = [c3(8, 64, BF16) for _ in range(2)]
        Vb = c3(8, 128, BF16); Kbg = c3(8, 128, BF16)
        u32 = two(lambda: c3(8, 128, F32))
        wT = two(lambda: carve(8 * 64, BF16).rearrange("p (a b) -> p a b", a=8))
        qkT = two(lambda: c3(8, 64, BF16)); kd = two(lambda: c3(8, 128, BF16))
        vnew = c3(8, 128, BF16); o32 = c3(8, 128, F32); p3s = c3(8, 128, F32)
        ssq = carve(8)[0:64, :]; onb = c3(8, 128, BF16)
        HG = ((0, 4), (4, 8))

        def bank(i, part=64, lo=0, n=512):
            return ([psA[0], psA[1], psA[2], psA[3], psB[0], psB[1], psC[0], psC[1]][i])[0:part, lo:lo + n]
        BK = ["psA0", "psA1", "psA2", "psA3", "psB0", "psB1", "psC0", "psC1"]

        def h3(ap, a):
            return ap.rearrange("p (a b) -> p a b", a=a)

        def chunk_views(n):
            sbi, ci = n // 8, n % 8
            u, a0 = sbi % 2, ci * 64
            qTm = lambda m, u=u, a0=a0: dnin[u][:, m, a0:a0 + 64]
            kTm = lambda m, u=u, a0=a0: dnin[u][:, 4 + m, a0:a0 + 64]
            vTh = lambda h, u=u, a0=a0: dnin[u][:, 8 + h, a0:a0 + 64]
            return u, a0, qTm, kTm, vTh

        def emit_A1(n):
            s_ = n % 2
            S_ = str(s_)
            u, a0, qTm, kTm, vTh = chunk_views(n)
            dk_ = "dnin%d" % u
            if n % 8 == 0:
                c0 = (n // 8) * SB
                P.dma("sync", lambda e, u=u, c0=c0: e.dma_start(out=dnin[u][:, :, :], in_=s_dn.rearrange("k p t -> p k t")[:, :, c0:c0 + SB]),
                      r=["s_dn%d" % i for i in range(16)], w=["dnin%d" % u], sem="dnin%d" % u)
                P.dma("sync", lambda e, u=u, c0=c0: e.dma_start(out=zin[u][:, :, :], in_=s_z.rearrange("k p t -> p k t")[:, :, c0:c0 + SB]),
                      r=["s_z%d" % i for i in range(8)], w=["zin%d" % u], sem="zin%d" % u)
            for m in range(4):
                P.op("tensor", lambda e, m=m, kTm=kTm: e.matmul(bank(3, 64, m * 128, 128), lhsT=kTm(m), rhs=I_bf, start=True, stop=True),
                     r=[dk_, "cst_bf"], w=[BK[3]])
            P.op("scalar", lambda e, s_=s_: e.copy(out=Ktm[s_].rearrange("p a b -> p (a b)"), in_=bank(3)), r=[BK[3]], w=["Ktm" + S_])
            for hg, (h0, h1) in enumerate(HG):
                for h in range(h0, h1):
                    P.op("tensor", lambda e, h=h, h0=h0, vTh=vTh: e.matmul(bank(4, 64, (h - h0) * 128, 128), lhsT=vTh(h), rhs=I_bf,
                                                                          start=True, stop=True), r=[dk_, "cst_bf"], w=[BK[4]])
                P.op("scalar", lambda e, h0=h0, h1=h1, s_=s_: e.copy(out=Vtm[s_][:, h0:h1, :].rearrange("p a b -> p (a b)"), in_=bank(4)),
                     r=[BK[4]], w=["Vtm" + S_])
            P.op("vector", lambda e, n=n: e.tensor_copy(out=gcb, in_=gc_t[:, n, :].unsqueeze(2).to_broadcast([64, 8, 64])),
                 r=["gc_t"], w=["gcb"])
            for h in range(8):
                P.op("tensor", lambda e, h=h: e.matmul(bank(3, 64, h * 64, 64), lhsT=gcb[:, h, :], rhs=I64f, start=True, stop=True),
                     r=["gcb", "cst_s"], w=[BK[3]])
            P.op("vector", lambda e, n=n: e.tensor_tensor(out=d1, in0=h3(bank(3), 8), in1=gc_t[:, n, :].unsqueeze(2).to_broadcast([64, 8, 64]),
                                                          op=ALU.subtract), r=[BK[3], "gc_t"], w=["d1"])
            P.op("vector", lambda e: e.tensor_scalar(out=d1, in0=d1, scalar1=0.0, scalar2=None, op0=ALU.min), r=["d1"], w=["d1"])
            P.op("scalar", lambda e: e.activation(out=d1, in_=d1, func=AF.Exp), r=["d1"], w=["d1"])
            P.op("vector", lambda e, s_=s_: e.tensor_tensor(out=DmT[s_], in0=d1, in1=mU.unsqueeze(1).to_broadcast([64, 8, 64]), op=ALU.mult),
                 r=["d1", "cst_s"], w=["DmT" + S_])
            P.op("gpsimd", lambda e: e.tensor_tensor(out=DmTs, in0=d1, in1=mUs.unsqueeze(1).to_broadcast([64, 8, 64]), op=ALU.mult),
                 r=["d1", "cst_s"], w=["DmTs"])
            P.op("gpsimd", lambda e, n=n: e.tensor_tensor(out=dgb, in0=I64f.unsqueeze(1).to_broadcast([64, 8, 64]),
                                                          in1=beta_t[:, n, :].unsqueeze(2).to_broadcast([64, 8, 64]), op=ALU.mult),
                 r=["beta_t", "cst_s"], w=["dgb"])
            for h in range(8):
                P.op("tensor", lambda e, h=h, s_=s_: e.matmul(bank(4, 128, h * 64, 64), lhsT=Ktm[s_][:, h // 2, :], rhs=dgb[:, h, :], start=True, stop=True),
                     r=["Ktm" + S_, "dgb"], w=[BK[4]])
            P.op("scalar", lambda e: e.copy(out=kbT.rearrange("p a b -> p (a b)"), in_=bank(4, 128)), r=[BK[4]], w=["kbT"])
            for h in range(8):
                P.op("tensor", lambda e, h=h, kTm=kTm: e.matmul(bank(3, 64, h * 64, 64), lhsT=kTm(h // 2), rhs=kbT[:, h, :], start=True, stop=True),
                     r=[dk_, "kbT"], w=[BK[3]])
            P.op("vector", lambda e, s_=s_: e.scalar_tensor_tensor(out=M0[s_], in0=h3(bank(3), 8), scalar=-1.0, in1=DmTs, op0=ALU.mult, op1=ALU.mult),
                 r=[BK[3], "DmTs"], w=["M0" + S_])
            for h in range(8):
                P.op("tensor", lambda e, h=h, s_=s_: e.matmul(bank(4, 64, h * 64, 64), lhsT=M0[s_][:, h, :], rhs=cst_bf[0:64, 0:64], start=True, stop=True),
                     r=["M0" + S_, "cst_bf"], w=[BK[4]])
            P.op("scalar", lambda e, s_=s_: e.copy(out=N0[s_].rearrange("p a b -> p (a b)"), in_=bank(4)), r=[BK[4]], w=["N0" + S_])
            P.op("gpsimd", lambda e, s_=s_: e.tensor_tensor(out=R0[s_], in0=M0[s_], in1=cst_bf[0:64, 0:64].unsqueeze(1).to_broadcast([64, 8, 64]), op=ALU.add),
                 r=["M0" + S_, "cst_bf"], w=["R0" + S_])

        def emit_A2(n):
            s_ = n % 2
            S_ = str(s_)
            u, a0, qTm, kTm, vTh = chunk_views(n)
            dk_ = "dnin%d" % u
            Mc, Nc, Rc = M0[s_], N0[s_], R0[s_]
            Mk, Nk, Rk = "M0" + S_, "N0" + S_, "R0" + S_
            for lvl in range(1, 6):
                nx = lvl % 2
                if lvl < 5:
                    for h in range(8):
                        P.op("tensor", lambda e, h=h, Nc=Nc, Mc=Mc: e.matmul(bank(5, 64, h * 64, 64), lhsT=Nc[:, h, :], rhs=Mc[:, h, :],
                                                                            start=True, stop=True), r=[Nk, Mk], w=[BK[5]])
                    P.op("vector", lambda e, nx=nx: e.tensor_copy(out=Mb[nx].rearrange("p a b -> p (a b)"), in_=bank(5)),
                         r=[BK[5]], w=["Mb%d" % nx])
                for h in range(8):
                    P.op("tensor", lambda e, h=h, Nc=Nc, Mc=Mc: e.matmul(bank(6, 64, h * 64, 64), lhsT=Mc[:, h, :], rhs=Nc[:, h, :],
                                                                        start=True, stop=True), r=[Nk, Mk], w=[BK[6]])
                P.op("scalar", lambda e, nx=nx: e.copy(out=Nb[nx].rearrange("p a b -> p (a b)"), in_=bank(6)), r=[BK[6]], w=["Nb%d" % nx])
                for h in range(8):
                    P.op("tensor", lambda e, h=h, nx=nx, Rc=Rc: e.matmul(bank(7, 64, h * 64, 64), lhsT=Nb[nx][:, h, :], rhs=Rc[:, h, :],
                                                                        start=True, stop=True), r=["Nb%d" % nx, Rk], w=[BK[7]])
                P.op("vector", lambda e, nx=nx, Rc=Rc: e.tensor_tensor(out=Rb[nx], in0=h3(bank(7), 8), in1=Rc, op=ALU.add),
                     r=[BK[7], Rk], w=["Rb%d" % nx])
                Mc, Nc, Rc = Mb[nx], Nb[nx], Rb[nx]
                Mk, Nk, Rk = "Mb%d" % nx, "Nb%d" % nx, "Rb%d" % nx
            TT, TTk = Rc, Rk
            P.op("gpsimd", lambda e, n=n, s_=s_: e.tensor_tensor(out=Vb, in0=Vtm[s_], in1=beta_t[:, n, :].unsqueeze(2).to_broadcast([64, 8, 128]), op=ALU.mult),
                 r=["Vtm" + S_, "beta_t"], w=["Vb"])
            K8 = Ktm[s_].unsqueeze(2).to_broadcast([64, 4, 2, 128])
            P.op("gpsimd", lambda e, n=n, K8=K8: e.tensor_tensor(out=Kbg.rearrange("p (m r) d -> p m r d", r=2), in0=K8,
                                                               in1=bge_t[:, n, :].rearrange("p (m r) -> p m r", r=2).unsqueeze(3).to_broadcast([64, 4, 2, 128]),
                                                               op=ALU.mult), r=["Ktm" + S_, "bge_t"], w=["Kbg"])
            P.op("gpsimd", lambda e, n=n, K8=K8, s_=s_: e.tensor_tensor(out=kd[s_].rearrange("p (m r) d -> p m r d", r=2), in0=K8,
                                                                      in1=ekd_t[:, n, :].rearrange("p (m r) -> p m r", r=2).unsqueeze(3).to_broadcast([64, 4, 2, 128]),
                                                                      op=ALU.mult), r=["Ktm" + S_, "ekd_t"], w=["kd" + S_])
            for m in range(4):
                P.op("tensor", lambda e, m=m, kTm=kTm, qTm=qTm: e.matmul(bank(5, 64, m * 64, 64), lhsT=kTm(m), rhs=qTm(m), start=True, stop=True),
                     r=[dk_], w=[BK[5]])
            P.op("vector", lambda e, s_=s_: e.tensor_tensor(out=qkT[s_].rearrange("p (m r) d -> p m r d", r=2),
                                                            in0=h3(bank(5, 64, 0, 256), 4).unsqueeze(2).to_broadcast([64, 4, 2, 64]),
                                                            in1=DmT[s_].rearrange("p (m r) d -> p m r d", r=2), op=ALU.mult),
                 r=[BK[5], "DmT" + S_], w=["qkT" + S_])
            for hg, (h0, h1) in enumerate(HG):
                hs = slice(h0, h1)
                for h in range(h0, h1):
                    P.op("tensor", lambda e, h=h, h0=h0, TT=TT: e.matmul(bank(6, 64, (h - h0) * 128, 128), lhsT=TT[:, h, :], rhs=Vb[:, h, :],
                                                                        start=True, stop=True), r=[TTk, "Vb"], w=[BK[6]])
                P.op("scalar", lambda e, hs=hs, s_=s_: e.copy(out=u32[s_][:, hs, :].rearrange("p a b -> p (a b)"), in_=bank(6)), r=[BK[6]], w=["u32" + S_])
            for h in range(8):
                P.op("tensor", lambda e, h=h, TT=TT: e.matmul(bank(7, 128, h * 64, 64), lhsT=Kbg[:, h, :], rhs=TT[:, h, :],
                                                             start=True, stop=True), r=[TTk, "Kbg"], w=[BK[7]])
            P.op("scalar", lambda e, s_=s_: e.copy(out=wT[s_].rearrange("p a b -> p (a b)"), in_=bank(7, 128)), r=[BK[7]], w=["wT" + S_])

        def emit_B(n):
            s_ = n % 2
            S_ = str(s_)
            u, a0, qTm, kTm, vTh = chunk_views(n)
            dk_ = "dnin%d" % u
            for hg, (h0, h1) in enumerate(HG):
                hs = slice(h0, h1)
                for h in range(h0, h1):
                    P.op("tensor", lambda e, h=h, h0=h0, s_=s_: e.matmul(bank(0, 64, (h - h0) * 128, 128), lhsT=wT[s_][:, h, :], rhs=Sbf[:, h, :],
                                                                        start=True, stop=True), r=["wT" + S_, "Sbf"], w=[BK[0]])
                P.op("vector", lambda e, hs=hs, s_=s_: e.tensor_tensor(out=vnew[:, hs, :], in0=u32[s_][:, hs, :], in1=h3(bank(0), 4), op=ALU.subtract),
                     r=["u32" + S_, BK[0]], w=["vnew"])
                for h in range(h0, h1):
                    P.op("tensor", lambda e, h=h, h0=h0, qTm=qTm: e.matmul(bank(1, 64, (h - h0) * 128, 128), lhsT=qTm(h // 2), rhs=Sbf[:, h, :],
                                                                          start=True, stop=True), r=[dk_, "Sbf"], w=[BK[1]])
                for h in range(h0, h1):
                    P.op("tensor", lambda e, h=h, h0=h0, s_=s_: e.matmul(bank(0, 64, (h - h0) * 128, 128), lhsT=qkT[s_][:, h, :], rhs=vnew[:, h, :],
                                                                        start=True, stop=True), r=["qkT" + S_, "vnew"], w=[BK[0]])
                P.op("scalar", lambda e, hs=hs: e.copy(out=p3s[:, hs, :].rearrange("p a b -> p (a b)"), in_=bank(0)), r=[BK[0]], w=["p3s"])
                P.op("vector", lambda e, hs=hs, n=n: e.tensor_tensor(out=o32[:, hs, :], in0=h3(bank(1), 4),
                                                                     in1=eg_t[:, n, hs].unsqueeze(2).to_broadcast([64, 4, 128]), op=ALU.mult),
                     r=[BK[1], "eg_t"], w=["o32"])
                P.op("gpsimd", lambda e, hs=hs: e.tensor_tensor(out=o32[:, hs, :], in0=o32[:, hs, :], in1=p3s[:, hs, :], op=ALU.add),
                     r=["o32", "p3s"], w=["o32"])
                for h in range(h0, h1):
                    P.op("tensor", lambda e, h=h, h0=h0, s_=s_: e.matmul(bank(2, 128, (h - h0) * 128, 128), lhsT=kd[s_][:, h, :], rhs=vnew[:, h, :],
                                                                        start=True, stop=True), r=["kd" + S_, "vnew"], w=[BK[2]])
                P.op("vector", lambda e, hs=hs, n=n: e.tensor_tensor(out=S32[:, hs, :], in0=S32[:, hs, :],
                                                                     in1=egl_t[:, n, hs].unsqueeze(2).to_broadcast([128, 4, 128]), op=ALU.mult),
                     r=["S32", "egl_t"], w=["S32"])
                P.op("vector", lambda e, hs=hs: e.tensor_tensor(out=S32[:, hs, :], in0=S32[:, hs, :], in1=h3(bank(2, 128), 4), op=ALU.add),
                     r=["S32", BK[2]], w=["S32"])
                P.op("scalar", lambda e, hs=hs: e.copy(out=Sbf[:, hs, :], in_=S32[:, hs, :]), r=["S32"], w=["Sbf"])
                P.op("gpsimd", lambda e, hs=hs: e.tensor_tensor(out=p3s[:, hs, :], in0=o32[:, hs, :], in1=o32[:, hs, :], op=ALU.mult),
                     r=["o32"], w=["p3s"])
                P.op("vector", lambda e, hs=hs: e.tensor_reduce(out=ssq[:, hs], in_=p3s[:, hs, :], axis=AX.X, op=ALU.add), r=["p3s"], w=["ssq"])
                P.op("scalar", lambda e, hs=hs: e.activation(out=ssq[:, hs], in_=ssq[:, hs], func=AF.Sqrt, bias=EPS, scale=1.0 / 128), r=["ssq"], w=["ssq"])
                P.op("vector", lambda e, hs=hs: e.reciprocal(out=ssq[:, hs], in_=ssq[:, hs]), r=["ssq"], w=["ssq"])
                P.op("vector", lambda e, hs=hs: e.tensor_tensor(out=onb[:, hs, :], in0=o32[:, hs, :],
                                                                in1=ssq[:, hs].unsqueeze(2).to_broadcast([64, 4, 128]), op=ALU.mult),
                     r=["o32", "ssq"], w=["onb"])
                for h in range(h0, h1):
                    P.op("tensor", lambda e, h=h, h0=h0: e.matmul(bank(1, 128, (h - h0) * 64, 64), lhsT=onb[:, h, :], rhs=cst_bf[0:64, 0:64],
                                                                 start=True, stop=True), r=["onb", "cst_bf"], w=[BK[1]])
                P.op("vector", lambda e, hs=hs, u=u, a0=a0: e.scalar_tensor_tensor(
                    out=ogs[u][:, hs, a0:a0 + 64], in0=h3(bank(1, 128, 0, 256), 4), scalar=sm_s[:, 6:7], in1=zin[u][:, hs, a0:a0 + 64],
                    op0=ALU.mult, op1=ALU.mult), r=[BK[1], "sm_s", "zin%d" % u], w=["ogs0"])
            if n % 8 == 7:
                c0 = (n // 8) * SB
                P.dma("sync", lambda e, u=u, c0=c0: e.dma_start(out=g_src3[:, 4:12, c0:c0 + SB], in_=ogs[u][:, :, :]),
                      r=["ogs0"], w=["g_src_o"], sem="ogs0")

        def record(fn, n):
            if n < 0 or n >= 64:
                return []
            keep = P.ins
            P.ins = []
            fn(n)
            out = P.ins
            P.ins = keep
            return out

        def merge(lists):
            lists = [l for l in lists if l]
            if not lists:
                return []
            tot = max(len(l) for l in lists)
            pos = [0] * len(lists)
            out = []
            for step in range(1, tot + 1):
                for i, l in enumerate(lists):
                    tgt = (len(l) * step + tot - 1) // tot
                    while pos[i] < tgt:
                        out.append(l[pos[i]])
                        pos[i] += 1
            return out

        NCH = T // 64
        for r_ in range(-2, NCH):
            P.ins.extend(merge([record(emit_B, r_), record(emit_A2, r_ + 1), record(emit_A1, r_ + 2)]))

        if upto == "dn":
            raise _Stop()
        for k in range(12):
            P.dma("gpsimd", lambda e, k=k: e.collective_compute("AllGather", ALU.bypass, replica_groups=RG,
                                                                ins=[g_src[k * 128:(k + 1) * 128, :].opt()],
                                                                outs=[g_dst[k * 512:(k + 1) * 512, :].opt()]),
                  r=["g_src_a", "g_src_o"], w=["g_dst%d" % k], sem="cc1_%d" % k, inc=1)

        if upto == "g1":
            raise _Stop()
        P.barrier(lambda e: e.memset(bar_t[:, 2:3], 0.0))
        bigw[0] = 0
        gin = carve(48 * SB, BF16).rearrange("p (a b) -> p a b", a=48)
        wos_s = carve(4 * 16 * 128, BF16).rearrange("p (c k n) -> p c k n", c=4, k=16)
        wod_s = carve(4 * 32 * 128, BF16).rearrange("p (c k n) -> p c k n", c=4, k=32)
        sas = carve(4 * SB, BF16).rearrange("p (a b) -> p a b", a=4)
        sbs = carve(4 * SB, BF16).rearrange("p (a b) -> p a b", a=4)
        yst = carve(4 * SB, BF16).rearrange("p (a b) -> p a b", a=4)
        t1 = carve(SB)
        t2 = carve(SB)
        for c in range(4):
            P.dma("gpsimd", lambda e, c=c: e.dma_start(out=wos_s[:, c, :, :], in_=wos[c]), w=["wos_s"], sem="wos")
            P.dma("gpsimd", lambda e, c=c: e.dma_start(out=wod_s[:, c, :, :], in_=wod[c]), w=["wod_s"], sem="wod")
        g_dst3 = g_dst.rearrange("(k p) t -> p k t", p=128)
        y_src3 = y_src.rearrange("(k p) t -> p k t", p=128)
        for tb in range(T // SB):
            c0 = tb * SB
            for r4 in range(4):
                P.dma("sync", lambda e, c0=c0, r4=r4: e.dma_start(out=gin[:, r4 * 12:r4 * 12 + 12, :], in_=g_dst3[:, r4 * 12:r4 * 12 + 12, c0:c0 + SB]),
                      r=["g_dst%d" % k for k in range(12)], w=["gin_g%d" % r4], sem="gin%d" % r4)
            P.dma("sync", lambda e, c0=c0: e.dma_start(out=sas, in_=s_sa.rearrange("k p t -> p k t")[:, :, c0:c0 + SB]),
                  r=["s_sa%d" % i for i in range(4)], w=["sas"], sem="sas")
            P.dma("sync", lambda e, c0=c0: e.dma_start(out=sbs, in_=s_sb.rearrange("k p t -> p k t")[:, :, c0:c0 + SB]),
                  r=["s_sb%d" % i for i in range(4)], w=["sbs"], sem="sbs")
            accA = [psA[0], psA[1], psA[2], psA[3]]
            accB = [psB[0], psB[1], psC[0], psC[1]]
            kA = ["psA0", "psA1", "psA2", "psA3"]
            kB = ["psB0", "psB1", "psC0", "psC1"]
            items = []
            for kt in range(16):
                items.append(((kt % 4) * 4 + kt // 4, kt, False))
            for kt in range(32):
                items.append(((4 + kt % 8) * 4 + kt // 8, kt, True))
            items.sort()
            firstA, lastA = min(i for i, it in enumerate(items) if not it[2]), max(i for i, it in enumerate(items) if not it[2])
            firstB, lastB = min(i for i, it in enumerate(items) if it[2]), max(i for i, it in enumerate(items) if it[2])
            for i_, (gt, kt, is_o) in enumerate(items):
                gk = "gin_g%d" % (gt // 12)
                for c in range(4):
                    if not is_o:
                        P.op("tensor", lambda e, c=c, kt=kt, gt=gt, st=(i_ == firstA), sp=(i_ == lastA): e.matmul(
                            accA[c][:, :], lhsT=wos_s[:, c, kt, :], rhs=gin[:, gt, :], start=st, stop=sp), r=["wos_s", gk], w=[kA[c]])
                    else:
                        P.op("tensor", lambda e, c=c, kt=kt, gt=gt, st=(i_ == firstB), sp=(i_ == lastB): e.matmul(
                            accB[c][:, :], lhsT=wod_s[:, c, kt, :], rhs=gin[:, gt, :], start=st, stop=sp), r=["wod_s", gk], w=[kB[c]])
            for c in range(4):
                P.op("vector", lambda e, c=c: e.tensor_tensor(out=t1, in0=accA[c][:, :], in1=sas[:, c, :], op=ALU.mult), r=[kA[c], "sas"], w=["t1"])
                P.op("vector", lambda e, c=c: e.tensor_tensor(out=t2, in0=accB[c][:, :], in1=sbs[:, c, :], op=ALU.mult), r=[kB[c], "sbs"], w=["t2"])
                P.op("gpsimd", lambda e, c=c: e.tensor_tensor(out=yst[:, c, :], in0=t1, in1=t2, op=ALU.add), r=["t1", "t2"], w=["yst"])
            P.dma("sync", lambda e, c0=c0: e.dma_start(out=y_src3[:, :, c0:c0 + SB], in_=yst), r=["yst"], w=["y_src"], sem="yst")
        for k in range(4):
            P.dma("gpsimd", lambda e, k=k: e.collective_compute("AllGather", ALU.bypass, replica_groups=RG,
                                                                ins=[y_src[k * 128:(k + 1) * 128, :].opt()],
                                                                outs=[y_dst[k * 512:(k + 1) * 512, :].opt()]),
                  r=["y_src"], w=["y_dst%d" % k], sem="cc2_%d" % k, inc=1)

        if upto == "c2":
            raise _Stop()
        P.barrier(lambda e: e.memset(bar_t[:, 3:4], 0.0))
        bigw[0] = 0
        yin = [carve(16 * SB, BF16).rearrange("p (a b) -> p a b", a=16) for _ in range(2)]
        wout_s = carve(4 * 16 * 128, BF16).rearrange("p (c k n) -> p c k n", c=4, k=16)
        xr = [carve(4 * SB).rearrange("p (a b) -> p a b", a=4) for _ in range(2)]
        ost = [carve(4 * SB).rearrange("p (a b) -> p a b", a=4) for _ in range(2)]
        for c in range(4):
            P.dma("gpsimd", lambda e, c=c: e.dma_start(out=wout_s[:, c, :, :], in_=wout[c]), w=["wout_s"], sem="wout")
        y_dst3 = y_dst.rearrange("(k p) t -> p k t", p=128)
        outT3 = outT.rearrange("(k p) t -> p k t", p=128)
        for tb in range(T // SB):
            c0 = tb * SB
            u = tb % 2
            P.dma("sync", lambda e, c0=c0, u=u: e.dma_start(out=yin[u], in_=y_dst3[:, :, c0:c0 + SB]), r=["y_dst%d" % k for k in range(4)], w=["yin%d" % u], sem="yin%d" % u)
            P.dma("sync", lambda e, c0=c0, u=u: e.dma_start(out=xr[u], in_=xres.rearrange("k p t -> p k t")[:, :, c0:c0 + SB]), w=["xr%d" % u], sem="xr%d" % u)
            for c in range(4):
                pa_ = psA[c]
                ka_ = "psA%d" % c
                for kt in range(16):
                    P.op("tensor", lambda e, c=c, kt=kt, pa_=pa_, u=u: e.matmul(pa_[:, :], lhsT=wout_s[:, c, kt, :], rhs=yin[u][:, (kt % 4) * 4 + kt // 4, :],
                                                                               start=(kt == 0), stop=(kt == 15)), r=["wout_s", "yin%d" % u], w=[ka_])
                P.op("vector", lambda e, c=c, pa_=pa_, u=u: e.scalar_tensor_tensor(out=ost[u][:, c, :], in0=pa_[:, :], scalar=mod[:, 32 + c:33 + c],
                                                                                   in1=xr[u][:, c, :], op0=ALU.mult, op1=ALU.add),
                     r=[ka_, "mod", "xr%d" % u], w=["ost%d" % u])
            P.dma("sync", lambda e, c0=c0, u=u: e.dma_start(out=outT3[:, :, c0:c0 + SB], in_=ost[u]), r=["ost%d" % u], w=["outT"], sem="ost%d" % u)
    except _Stop:
        pass

    allw = set()
    for I in P.ins:
        if I["dma"] is not None:
            allw.update(I["w"])
    endt = sb("endt", [128, 1])
    P.barrier(lambda e: e.memset(endt[:, :], 0.0))
    P.op("sync", lambda e: e.engine_nop() if hasattr(e, "engine_nop") else None, r=sorted(allw), w=["__end"])
    P.finalize()
    P.emit(nc, es)
    es.close()
    return nc


def tile_w(w):
    K, N = w.shape
    assert N % 128 == 0
    return np.ascontiguousarray(w.reshape(K // 128, 128, N // 128, 128).transpose(2, 1, 0, 3))


def prep_inputs(inp, b, j):
    x = inp["x"][b]
    w_in = inp["w_in"][0]
    m = {}
    xT_ = np.ascontiguousarray(x.T)
    m["xT"] = np.ascontiguousarray(xT_.reshape(16, 128, T // 128, 128).transpose(2, 1, 0, 3))
    m["xres"] = np.ascontiguousarray(xT_[512 * j:512 * j + 512].reshape(4, 128, T))
    m["cT"] = np.ascontiguousarray(inp["c"][b].reshape(16, 128).T)
    m["pos"] = np.ascontiguousarray(inp["positions"][b].reshape(1, T).astype(np.int32))
    w_ada = inp["w_ada"][0]
    cols = np.concatenate([np.arange(0, 4096), 4096 + 512 * j + np.arange(512)])
    m["wada"] = tile_w(w_ada[:, cols])
    m["bada"] = np.ascontiguousarray(inp["b_ada"][0][cols].reshape(36, 128).T)
    m["nw"] = np.ascontiguousarray(inp["norm_w"][0].reshape(16, 128).T)
    o_aq, o_ak, o_av, o_ag, o_dn, o_dz, o_db, o_da, o_ma, o_mb = np.cumsum([0, 2048, 256, 256, 2048, 8192, 4096, 32, 32, 2048])
    r = np.arange
    dnq = o_dn + 512 * j + r(512)
    dnk = o_dn + 2048 + 512 * j + r(512)
    dnv = o_dn + 4096 + 1024 * j + r(1024)
    dz = o_dz + 1024 * j + r(1024)
    aq = o_aq + 512 * j + r(512)
    kk = np.concatenate([o_ak + 64 * j + r(64), o_ak + 64 * j + r(64)])
    vv = np.concatenate([o_av + 64 * j + r(64), o_av + 64 * j + r(64)])
    ag = o_ag + 512 * j + r(512)
    ma = o_ma + 512 * j + r(512)
    mb = o_mb + 512 * j + r(512)
    bg = np.concatenate([o_db + 8 * j + r(8), o_da + 8 * j + r(8)])
    cols = np.concatenate([dnq, dnk, dnv, dz, aq, kk, vv, ag, ma, mb])
    wbg = np.zeros((2048, 128), np.float32)
    wbg[:, :16] = w_in[:, bg]
    m["wa"] = tile_w(np.concatenate([w_in[:, cols], wbg], axis=1))
    cwc = inp["conv_w"][0][:, np.concatenate([dnq, dnk, dnv]) - o_dn]
    m["cw"] = np.ascontiguousarray(cwc.reshape(4, 16, 128).transpose(2, 1, 0))
    cs = slice(512 * j, 512 * j + 512)
    m["wos"] = tile_w(inp["w_o_swa"][0][:, cs])
    wod_ = inp["w_o_dn"][0][:, cs]
    m["wod"] = np.ascontiguousarray(wod_.reshape(32, 128, 4, 128).transpose(2, 1, 0, 3))
    m["wout"] = tile_w(inp["w_out"][0][:, cs])
    cst = np.zeros((128, 1536), np.float32)
    cst[:, 0:128] = np.eye(128)
    for q in range(2):
        cst[64 * q:64 * q + 64, 128 + 64 * q:128 + 64 * q + 64] = 1.0 / 64
    for q in range(2):
        for i in range(8):
            cst[64 * q + i + 8, 256 + 64 * q + i] = -1.0
            cst[64 * q + i, 256 + 64 * q + i + 8] = 1.0
    ii = np.arange(64)
    cst[0:64, 384:448] = (ii[:, None] <= ii[None, :])
    cst[63, 448:576] = 1.0
    cst[0:64, 640:704] = (ii[:, None] < ii[None, :])
    kq = np.arange(128)
    cst[:, 704:832] = (kq[None, :] < kq[:, None])
    cst[:, 832:960] = (kq[None, :] >= kq[:, None])
    invf = 500000.0 ** (-np.arange(8, dtype=np.float32) * (2.0 / 16))
    for q in range(2):
        cst[64 * q:64 * q + 8, 960] = invf
        cst[64 * q + 8:64 * q + 16, 960] = invf
    cst[:, 1024:1088] = 1.0
    cst[:, 1152 + 64:1280] = 1.0
    cst[0:64, 961] = 1.0
    cst[64:128, 962] = 1.0
    for k in range(64):
        cst[k, 1280 + k] = 1.0
        cst[k, 1408 + 64 + k] = 1.0
    m["cst"] = cst
    sm = np.zeros((128, 32), np.float32)
    sm[:, 0] = np.tile(inp["q_norm_w"][0], 2)
    sm[:, 1] = np.tile(inp["k_norm_w"][0], 2)
    sk = inp["sinks"][0][8 * j:8 * j + 8]
    for t in range(4):
        sm[0:64, 2 + t] = sk[2 * t]
        sm[64:128, 2 + t] = sk[2 * t + 1]
    sm[:, 6] = inp["dn_norm_w"][0]
    sm[:, 8:16] = inp["a_log"][0][8 * j:8 * j + 8][None, :]
    sm[:, 16:24] = inp["dt_bias"][0][8 * j:8 * j + 8][None, :]
    m["sm"] = sm
    return m


_NC_CACHE = {}


def kernel(**inp):
    inp = {k: np.asarray(v) for k, v in inp.items()}
    if "nc" not in _NC_CACHE:
        _NC_CACHE["nc"] = build()
    nc = _NC_CACHE["nc"]
    in_maps = [prep_inputs(inp, c // 4, c % 4) for c in range(8)]
    res = run_bass_kernel_spmd(nc, in_maps, core_ids=list(range(8)))
    out = np.zeros((2, T, D), np.float32)
    for c in range(8):
        b, j = c // 4, c % 4
        out[b][:, 512 * j:512 * j + 512] = np.asarray(res.results[c]["outT"]).T
    return out
```

```python
import os
import numpy as np
from contextlib import ExitStack
import concourse.bass as bass
import concourse.mybir as mybir
from concourse.bass_utils import run_bass_kernel_spmd

F32 = mybir.dt.float32
BF16 = mybir.dt.bfloat16
I32 = mybir.dt.int32
AF = mybir.ActivationFunctionType
ALU = mybir.AluOpType
AX = mybir.AxisListType

D = 2048
T = 4096
EPS = 1e-6
NCT = 43
SAME_ENGINE_SYNC = True


class Prog:
    def __init__(self):
        self.ins = []
        self.keys = set()
        self.ep = ()

    def op(self, eng, fn, r=(), w=()):
        self.keys.update(r); self.keys.update(w)
        self.ins.append(dict(eng=eng, fn=fn, r=tuple(r) + self.ep, w=tuple(w), dma=None))

    def dma(self, eng, fn, r=(), w=(), sem=None, inc=16):
        assert sem is not None
        self.keys.update(r); self.keys.update(w)
        self.ins.append(dict(eng=eng, fn=fn, r=tuple(r) + self.ep, w=tuple(w), dma=sem, inc=inc))

    def barrier(self, fn, exclude=()):
        ks = [k for k in sorted(self.keys, key=str) if k not in exclude]
        self.ins.append(dict(eng="gpsimd", fn=fn, r=(), w=tuple(ks) + ("__epoch",), dma=None))
        self.ep = ("__epoch",)

    def finalize(self):
        ins = self.ins
        last_w, readers, last_dma = {}, {}, {}
        for idx, I in enumerate(ins):
            deps = set()
            for k in I["r"]:
                if k in last_w:
                    deps.add(last_w[k])
            for k in I["w"]:
                if k in last_w:
                    deps.add(last_w[k])
                deps.update(readers.get(k, ()))
            if I["dma"] is not None:
                if I["dma"] in last_dma:
                    deps.add(last_dma[I["dma"]])
                last_dma[I["dma"]] = idx
            deps.discard(idx)
            I["deps"] = deps
            for k in I["r"]:
                readers.setdefault(k, []).append(idx)
            for k in I["w"]:
                last_w[k] = idx
                readers[k] = []
        need = [False] * len(ins)
        for idx, I in enumerate(ins):
            for d in I["deps"]:
                P = ins[d]
                if P["dma"] is not None:
                    continue
                if P["eng"] != I["eng"]:
                    need[d] = True
                elif SAME_ENGINE_SYNC and I["eng"] != "tensor":
                    need[d] = True
        cnt, dcnt = {}, {}
        for idx, I in enumerate(ins):
            if I["dma"] is not None:
                dcnt[I["dma"]] = dcnt.get(I["dma"], 0) + I["inc"]
                I["sig"] = (("dma", I["dma"]), dcnt[I["dma"]], I["inc"])
            elif need[idx]:
                cnt[I["eng"]] = cnt.get(I["eng"], 0) + 1
                I["sig"] = (("eng", I["eng"]), cnt[I["eng"]], 1)
            else:
                I["sig"] = None
        known = {}
        for idx, I in enumerate(ins):
            waits = {}
            for d in I["deps"]:
                P = ins[d]
                if P["sig"] is None:
                    continue
                if P["dma"] is None and P["eng"] == I["eng"] and (I["eng"] == "tensor" or not SAME_ENGINE_SYNC):
                    continue
                s, v, _ = P["sig"]
                waits[s] = max(waits.get(s, 0), v)
            kn = known.setdefault(I["eng"], {})
            out = []
            for s, v in waits.items():
                if kn.get(s, 0) >= v:
                    continue
                kn[s] = v
                out.append((s, v))
            I["waits"] = out
        self.dma_keys = sorted(dcnt.keys(), key=str)
        return self

    def emit(self, nc, es):
        engs = ["tensor", "vector", "scalar", "gpsimd", "sync"]
        sems = {}
        for e in engs:
            sems[("eng", e)] = es.enter_context(nc.semaphore("se_" + e))
        for i, k in enumerate(self.dma_keys):
            sems[("dma", k)] = es.enter_context(nc.semaphore("sd_%d" % i))
        per = {e: [I for I in self.ins if I["eng"] == e] for e in engs}
        block = es.enter_context(nc.Block())

        def make(name):
            def body(e):
                for I in per[name]:
                    for s, v in I["waits"]:
                        e.wait_ge(sems[s], v)
                    bi = I["fn"](e)
                    if I["sig"] is not None:
                        s, v, inc = I["sig"]
                        bi.then_inc(sems[s], inc)
            return body

        block.tensor(make("tensor"))
        block.vector(make("vector"))
        block.scalar(make("scalar"))
        block.gpsimd(make("gpsimd"))
        block.sync(make("sync"))


class _Stop(Exception):
    pass


def build(debug=False, stop_after=99, upto=None):
    nc = bass.Bass("TRN2", target_bir_lowering=False)
    P = Prog()
    es = ExitStack()

    def din(name, shape, dt=F32):
        return nc.dram_tensor(name, list(shape), dt, kind="ExternalInput").ap()

    def dscr(name, shape, dt=BF16):
        if debug:
            return nc.dram_tensor(name, list(shape), dt, kind="ExternalOutput").ap()
        return nc.dram_tensor(name, list(shape), dt).ap()

    def sb(name, shape, dt=F32):
        return es.enter_context(nc.sbuf_tensor(name, list(shape), dt))

    def ps(name, shape, dt=F32):
        return es.enter_context(nc.psum_tensor(name, list(shape), dt))

    xT = din("xT", [T // 128, 128, 16, 128])
    cT = din("cT", [128, 16])
    wada = din("wada", [36, 128, 16, 128])
    bada = din("bada", [128, 36])
    nw = din("nw", [128, 16])
    wa = din("wa", [NCT, 128, 16, 128])
    cw = din("cw", [128, 16, 4])
    s_dn = dscr("s_dn", [16, 128, T])
    s_z = dscr("s_z", [8, 128, T])
    s_hT = dscr("s_hT", [16, 128, T]) if debug else None

    ones_bf = sb("ones_bf", [128, 128], BF16)
    big = sb("big", [128, 32768])
    hT = big[:, :].bitcast(BF16).rearrange("p (a b) -> p a b", a=16)
    mod = sb("mod", [128, 36])
    gam = sb("gam", [128, 16])
    cT_s = sb("cT_s", [128, 16])
    sc_s = sb("sc_s", [128, 16])
    bada_s = sb("bada_s", [128, 36])
    nw_s = sb("nw_s", [128, 16])
    cw_s = sb("cw_s", [128, 16, 4])
    P.op("gpsimd", lambda e: e.memset(ones_bf[:, :], 1.0), w=["ones_bf"])
    onesD_bf = sb("onesD_bf", [128, 128], BF16)
    P.op("gpsimd", lambda e: e.memset(onesD_bf[:, :], 1.0 / D), w=["onesD_bf"])
    P.dma("sync", lambda e: e.dma_start(out=cT_s[:, :], in_=cT[:, :]), w=["cT_s"], sem="small0")
    P.dma("sync", lambda e: e.dma_start(out=bada_s[:, :], in_=bada[:, :]), w=["bada_s"], sem="small1")
    P.dma("sync", lambda e: e.dma_start(out=nw_s[:, :], in_=nw[:, :]), w=["nw_s"], sem="small2")
    P.dma("sync", lambda e: e.dma_start(out=cw_s[:, :, :], in_=cw[:, :, :]), w=["cw_s"], sem="small3")

    psA = [ps("psA%d" % i, [128, 512]) for i in range(4)]
    psB = [ps("psB%d" % i, [128, 512]) for i in range(2)]
    psC = [ps("psC%d" % i, [128, 512]) for i in range(2)]

    P.op("scalar", lambda e: e.activation(out=sc_s[:, :], in_=cT_s[:, :], func=AF.Silu), r=["cT_s"], w=["sc_s"])
    wad = [sb("wad%d" % i, [128, 16, 128]) for i in range(2)]
    for t in range(36):
        buf = wad[t % 2]
        key = "wad%d" % (t % 2)
        P.dma("sync", lambda e, buf=buf, t=t: e.dma_start(out=buf[:, :, :], in_=wada[t]), w=[key], sem=key)
        for kt in range(16):
            P.op("tensor", lambda e, buf=buf, t=t, kt=kt: e.matmul(
                psA[0][:, t:t + 1], lhsT=buf[:, kt, :], rhs=sc_s[:, kt:kt + 1], start=(kt == 0), stop=(kt == 15)),
                r=[key, "sc_s"], w=["psA0"])
    P.op("vector", lambda e: e.tensor_tensor(out=mod[:, :], in0=psA[0][:, 0:36], in1=bada_s[:, :], op=ALU.add),
         r=["psA0", "bada_s"], w=["mod"])
    P.op("vector", lambda e: e.scalar_tensor_tensor(out=gam[:, :], in0=mod[:, 16:32], scalar=1.0, in1=nw_s[:, :],
                                                     op0=ALU.add, op1=ALU.mult), r=["mod", "nw_s"], w=["gam"])

    TB1 = 128
    xb = [sb("xb%d" % i, [128, 16, TB1]) for i in range(2)]
    sq = sb("sq", [128, 16, TB1], BF16)
    rstd = sb("rstd", [128, TB1])
    for tb in range(T // TB1):
        xbuf = xb[tb % 2]
        xk = "xb%d" % (tb % 2)
        t0 = tb * TB1
        P.dma("sync", lambda e, xbuf=xbuf, t0=t0: e.dma_start(out=xbuf[:, :, :], in_=xT[t0 // TB1]),
              w=[xk], sem=xk)
        P.op("scalar", lambda e, xbuf=xbuf: e.activation(out=sq[:, :, :], in_=xbuf[:, :, :], func=AF.Square),
             r=[xk], w=["sq"])
        pb = psB[tb % 2]
        pk = "psB%d" % (tb % 2)
        for dt in range(16):
            P.op("tensor", lambda e, pb=pb, dt=dt: e.matmul(pb[:, 0:TB1], lhsT=onesD_bf[:, :], rhs=sq[:, dt, :],
                                                           start=(dt == 0), stop=(dt == 15)),
                 r=["sq", "onesD_bf"], w=[pk])
        P.op("scalar", lambda e, pb=pb: e.activation(out=rstd[:, :], in_=pb[:, 0:TB1], func=AF.Sqrt, bias=EPS, scale=1.0),
             r=[pk], w=["rstd"])
        P.op("vector", lambda e: e.reciprocal(out=rstd[:, :], in_=rstd[:, :]), r=["rstd"], w=["rstd"])
        P.op("vector", lambda e, xbuf=xbuf: e.tensor_tensor(
            out=xbuf[:, :, :], in0=xbuf[:, :, :], in1=rstd[:, :].unsqueeze(1).to_broadcast([128, 16, TB1]), op=ALU.mult),
            r=[xk, "rstd"], w=[xk])
        for dt in range(16):
            P.op("scalar", lambda e, t0=t0, xbuf=xbuf, dt=dt: e.activation(
                out=hT[:, dt, t0:t0 + TB1], in_=xbuf[:, dt, :], func=AF.Identity, bias=mod[:, dt:dt + 1], scale=gam[:, dt:dt + 1]),
                r=[xk, "mod", "gam"], w=["hT%d_%d" % (tb, dt)])
    hT_keys = ["hT%d_%d" % (tb, dt) for tb in range(T // TB1) for dt in range(16)]
    if debug:
        P.dma("sync", lambda e: e.dma_start(out=s_hT.rearrange("k p t -> p k t"), in_=hT[:, :, :]),
              r=hT_keys, w=["s_hT"], sem="dbg")

    pos = din("pos", [1, T], I32)
    cst = din("cst", [128, 1536])
    sm = din("sm", [128, 32])
    wos = din("wos", [4, 128, 16, 128])
    wod = din("wod", [4, 128, 32, 128])
    wout = din("wout", [4, 128, 16, 128])
    xres = din("xres", [4, 128, T])
    outT = nc.dram_tensor("outT", [512, T], F32, kind="ExternalOutput").ap()
    s_aq = dscr("s_aq", [4, 128, T])
    s_kk = dscr("s_kk", [128, T])
    s_vv = dscr("s_vv", [128, T])
    s_ag = dscr("s_ag", [4, 128, T])
    s_sa = dscr("s_sa", [4, 128, T])
    s_sb = dscr("s_sb", [4, 128, T])
    g_src = dscr("g_src", [12 * 128, T])
    g_dst = nc.dram_tensor("g_dst", [48 * 128, T], BF16).ap()
    y_src = dscr("y_src", [4 * 128, T])
    y_dst = nc.dram_tensor("y_dst", [16 * 128, T], BF16).ap()
    cst_s = sb("cst_s", [128, 1536])
    cst_bf = sb("cst_bf", [128, 1536], BF16)
    sm_s = sb("sm_s", [128, 32])
    P.dma("sync", lambda e: e.dma_start(out=cst_s[:, :], in_=cst[:, :]), w=["cst_s"], sem="small0")
    P.dma("sync", lambda e: e.dma_start(out=sm_s[:, :], in_=sm[:, :]), w=["sm_s"], sem="small1")
    P.op("vector", lambda e: e.tensor_copy(out=cst_bf[:, :], in_=cst_s[:, :]), r=["cst_s"], w=["cst_bf"])
    I_bf = cst_bf[:, 0:128]
    blk64_bf = cst_bf[:, 128:256]
    Pm_bf = cst_bf[:, 256:384]
    tri64 = cst_s[0:64, 384:448]
    sel63 = cst_s[0:64, 448:576]
    mU = cst_s[0:64, 384:448]
    mUs = cst_s[0:64, 640:704]
    maskCP = cst_bf[:, 704:960]
    invf = cst_s[:, 960:961]
    onesE_bf = cst_bf[:, 1024:1152]
    onesO_bf = cst_bf[:, 1152:1280]
    I64f = cst_s[0:64, 0:64]
    PI = float(np.pi)

    try:
        wb = [sb("wb%d" % i, [128, 16, 128], BF16) for i in range(2)]
        QT = 1024
        Y = [sb("Y%d" % i, [128, 3 + QT]) for i in range(2)]
        acc = sb("acc", [128, QT])
        sil = sb("sil", [128, QT])
        sqb = sb("sqb", [128, QT], BF16)
        rs2 = sb("rs2", [128, QT])
        sqf = sq[:, :, :].rearrange("p a b -> p (a b)")
        ob = [sqf[:, i * QT:(i + 1) * QT] for i in range(2)]
        Cf = wad[0][:, :, :].rearrange("p a b -> p (a b)").bitcast(BF16)
        Sf = wad[1][:, :, :].rearrange("p a b -> p (a b)").bitcast(BF16)
        rs2i = rs2[:, :].bitcast(I32)
        for qi in range(4):
            q0 = qi * QT
            P.dma("sync", lambda e, q0=q0: e.dma_start(out=rs2i, in_=pos[0:1, q0:q0 + QT].partition_broadcast(128)),
                  w=["rs2"], sem="posld")
            P.op("vector", lambda e: e.tensor_copy(out=acc[:, :], in_=rs2i), r=["rs2"], w=["acc"])
            P.op("vector", lambda e: e.tensor_scalar(out=acc[:, :], in0=acc[:, :], scalar1=invf, scalar2=None, op0=ALU.mult),
                 r=["acc", "cst_s"], w=["acc"])
            for tab, tk, offs in ((Sf, "wad1", 0.0), (Cf, "wad0", PI / 2)):
                if offs != 0.0:
                    P.op("vector", lambda e, offs=offs: e.tensor_scalar(out=acc[:, :], in0=acc[:, :], scalar1=offs, scalar2=None,
                                                                        op0=ALU.add), r=["acc"], w=["acc"])
                P.op("vector", lambda e: e.tensor_scalar(out=sil[:, :], in0=acc[:, :], scalar1=1.0 / (2 * PI), scalar2=None,
                                                         op0=ALU.mult), r=["acc"], w=["sil"])
                P.op("vector", lambda e: e.tensor_copy(out=rs2i, in_=sil[:, :]), r=["sil"], w=["rs2"])
                P.op("vector", lambda e: e.tensor_copy(out=sil[:, :], in_=rs2i), r=["rs2"], w=["sil"])
                P.op("vector", lambda e: e.scalar_tensor_tensor(out=sil[:, :], in0=sil[:, :], scalar=-2 * PI, in1=acc[:, :],
                                                                op0=ALU.mult, op1=ALU.add), r=["sil", "acc"], w=["sil"])
                P.op("vector", lambda e: e.tensor_scalar(out=rs2[:, :], in0=sil[:, :], scalar1=PI, scalar2=None, op0=ALU.is_gt),
                     r=["sil"], w=["rs2"])
                P.op("vector", lambda e: e.scalar_tensor_tensor(out=sil[:, :], in0=rs2[:, :], scalar=-2 * PI, in1=sil[:, :],
                                                                op0=ALU.mult, op1=ALU.add), r=["sil", "rs2"], w=["sil"])
                P.op("vector", lambda e: e.tensor_scalar(out=rs2[:, :], in0=sil[:, :], scalar1=-PI, scalar2=None, op0=ALU.is_lt),
                     r=["sil"], w=["rs2"])
                P.op("vector", lambda e: e.scalar_tensor_tensor(out=sil[:, :], in0=rs2[:, :], scalar=2 * PI, in1=sil[:, :],
                                                                op0=ALU.mult, op1=ALU.add), r=["sil", "rs2"], w=["sil"])
                P.op("scalar", lambda e, tab=tab, q0=q0: e.activation(out=tab[:, q0:q0 + QT], in_=sil[:, :], func=AF.Sin),
                     r=["sil"], w=[tk])

        if upto == "tab":
            raise _Stop()
        yc = 0
        oc = 0
        pa = 0
        n_ct = min(NCT - 1, stop_after)
        P.dma("gpsimd", lambda e: e.dma_start(out=wb[0][:, :, :], in_=wa[0]), w=["wb0"], sem="wb0")
        for ct in range(n_ct):
            wbuf = wb[ct % 2]
            wk = "wb%d" % (ct % 2)
            if ct + 1 < NCT:
                P.dma("gpsimd", lambda e, ct=ct: e.dma_start(out=wb[(ct + 1) % 2][:, :, :], in_=wa[ct + 1]),
                      w=["wb%d" % ((ct + 1) % 2)], sem="wb%d" % ((ct + 1) % 2))
            if ct < 16:
                kind = "conv"
            elif ct < 24:
                kind, fn, dst, dkey = "act", AF.Silu, s_z[ct - 16], "s_z%d" % (ct - 16)
            elif ct < 28:
                kind, dst, dkey = "rope", s_aq[ct - 24], "s_aq%d" % (ct - 24)
            elif ct == 28:
                kind, dst, dkey = "rope", s_kk, "s_kk"
            elif ct == 29:
                kind, fn, dst, dkey = "act", AF.Copy, s_vv, "s_vv"
            elif ct < 34:
                kind, fn, dst, dkey = "act", AF.Silu, s_ag[ct - 30], "s_ag%d" % (ct - 30)
            elif ct < 38:
                kind, fn, dst, dkey = "act", AF.Sigmoid, s_sa[ct - 34], "s_sa%d" % (ct - 34)
            else:
                kind, fn, dst, dkey = "act", AF.Sigmoid, s_sb[ct - 38], "s_sb%d" % (ct - 38)
            for qi in range(T // QT):
                q0 = qi * QT
                if kind in ("conv", "rope"):
                    Yb = Y[yc % 2]
                    Yk = "Y%d" % (yc % 2)
                    Yp = Y[(yc + 1) % 2]
                    Ypk = "Y%d" % ((yc + 1) % 2)
                    yc += 1
                    if kind == "conv":
                        if qi == 0:
                            P.op("gpsimd", lambda e, Yb=Yb: e.memset(Yb[:, 0:3], 0.0), w=[Yk])
                        else:
                            P.op("gpsimd", lambda e, Yb=Yb, Yp=Yp: e.tensor_copy(out=Yb[:, 0:3], in_=Yp[:, QT:QT + 3]),
                                 r=[Ypk], w=[Yk])
                obuf = ob[oc % 2]
                okey = "ob%d" % (oc % 2)
                oc += 1
                for bi in range(QT // 512):
                    t0 = q0 + bi * 512
                    pbank = psA[pa % 4]
                    pkey = "psA%d" % (pa % 4)
                    pa += 1
                    for kt in range(16):
                        P.op("tensor", lambda e, pbank=pbank, wbuf=wbuf, kt=kt, t0=t0: e.matmul(
                            pbank[:, :], lhsT=wbuf[:, kt, :], rhs=hT[:, kt, t0:t0 + 512], start=(kt == 0), stop=(kt == 15)),
                            r=[wk] + ["hT%d_%d" % (t0 // TB1 + i, kt) for i in range(512 // TB1)], w=[pkey])
                    if kind in ("conv", "rope"):
                        P.op("scalar", lambda e, pbank=pbank, Yb=Yb, bi=bi: e.copy(out=Yb[:, 3 + bi * 512:3 + bi * 512 + 512],
                                                                                 in_=pbank[:, :]), r=[pkey], w=[Yk])
                    else:
                        P.op("scalar", lambda e, pbank=pbank, obuf=obuf, bi=bi, fn=fn: e.activation(
                            out=obuf[:, bi * 512:bi * 512 + 512], in_=pbank[:, :], func=fn), r=[pkey], w=[okey])
                if kind == "conv":
                    P.op("vector", lambda e, Yb=Yb, ct=ct: e.tensor_scalar(out=acc[:, :], in0=Yb[:, 0:QT], scalar1=cw_s[:, ct, 0:1],
                                                                           scalar2=None, op0=ALU.mult),
                         r=[Yk, "cw_s"], w=["acc"])
                    for j in range(1, 4):
                        P.op("vector", lambda e, Yb=Yb, ct=ct, j=j: e.scalar_tensor_tensor(
                            out=acc[:, :], in0=Yb[:, j:j + QT], scalar=cw_s[:, ct, j:j + 1], in1=acc[:, :],
                            op0=ALU.mult, op1=ALU.add), r=[Yk, "cw_s", "acc"], w=["acc"])
                    if ct >= 8:
                        P.op("scalar", lambda e, obuf=obuf: e.activation(out=obuf[:, :], in_=acc[:, :], func=AF.Silu),
                             r=["acc"], w=[okey])
                    else:
                        P.op("scalar", lambda e: e.activation(out=sil[:, :], in_=acc[:, :], func=AF.Silu), r=["acc"], w=["sil"])
                        P.op("scalar", lambda e: e.activation(out=sqb[:, :], in_=sil[:, :], func=AF.Square), r=["sil"], w=["sqb"])
                        for bi in range(QT // 512):
                            pb = psB[bi]
                            pk = "psB%d" % bi
                            P.op("tensor", lambda e, pb=pb, bi=bi: e.matmul(pb[:, :], lhsT=ones_bf[:, :],
                                                                           rhs=sqb[:, bi * 512:bi * 512 + 512], start=True, stop=True),
                                 r=["sqb", "ones_bf"], w=[pk])
                            P.op("scalar", lambda e, pb=pb, bi=bi: e.activation(
                                out=rs2[:, bi * 512:bi * 512 + 512], in_=pb[:, :], func=AF.Sqrt, bias=EPS, scale=1.0),
                                r=[pk], w=["rs2"])
                        P.op("vector", lambda e: e.reciprocal(out=rs2[:, :], in_=rs2[:, :]), r=["rs2"], w=["rs2"])
                        qscale = (128.0 ** -0.5) if ct < 4 else 1.0
                        P.op("vector", lambda e, obuf=obuf, qscale=qscale: e.scalar_tensor_tensor(
                            out=obuf[:, :], in0=sil[:, :], scalar=qscale, in1=rs2[:, :], op0=ALU.mult, op1=ALU.mult),
                            r=["sil", "rs2"], w=[okey])
                    dst = s_dn[ct]
                    dkey = "s_dn%d" % ct
                elif kind == "rope":
                    isq = ct < 28
                    P.op("scalar", lambda e, Yb=Yb: e.activation(out=sqb[:, :], in_=Yb[:, 3:3 + QT], func=AF.Square), r=[Yk], w=["sqb"])
                    for bi in range(QT // 512):
                        pb = psB[bi]
                        pk = "psB%d" % bi
                        P.op("tensor", lambda e, pb=pb, bi=bi: e.matmul(pb[:, :], lhsT=blk64_bf, rhs=sqb[:, bi * 512:bi * 512 + 512],
                                                                       start=True, stop=True), r=["sqb", "cst_bf"], w=[pk])
                        sc_ = 64.0 if isq else 1.0
                        P.op("scalar", lambda e, pb=pb, bi=bi, sc_=sc_: e.activation(
                            out=rs2[:, bi * 512:bi * 512 + 512], in_=pb[:, :], func=AF.Sqrt, bias=EPS * sc_, scale=sc_),
                            r=[pk], w=["rs2"])
                    P.op("vector", lambda e: e.reciprocal(out=rs2[:, :], in_=rs2[:, :]), r=["rs2"], w=["rs2"])
                    nwc = sm_s[:, 0:1] if isq else sm_s[:, 1:2]
                    P.op("vector", lambda e, Yb=Yb, nwc=nwc: e.scalar_tensor_tensor(
                        out=sqb[:, :], in0=Yb[:, 3:3 + QT], scalar=nwc, in1=rs2[:, :], op0=ALU.mult, op1=ALU.mult),
                        r=[Yk, "rs2", "sm_s"], w=["sqb"])
                    P.op("vector", lambda e, q0=q0: e.tensor_tensor(out=acc[:, :], in0=sqb[:, :], in1=Cf[:, q0:q0 + QT], op=ALU.mult),
                         r=["sqb", "wad0"], w=["acc"])
                    for bi in range(QT // 512):
                        pb = psC[bi]
                        pk = "psC%d" % bi
                        P.op("tensor", lambda e, pb=pb, bi=bi: e.matmul(pb[:, :], lhsT=Pm_bf, rhs=sqb[:, bi * 512:bi * 512 + 512],
                                                                       start=True, stop=True), r=["sqb", "cst_bf"], w=[pk])
                        P.op("vector", lambda e, pb=pb, bi=bi, q0=q0: e.tensor_tensor(
                            out=sil[:, bi * 512:bi * 512 + 512], in0=pb[:, :], in1=Sf[:, q0 + bi * 512:q0 + bi * 512 + 512], op=ALU.mult),
                            r=[pk, "wad1"], w=["sil"])
                    P.op("gpsimd", lambda e, obuf=obuf: e.tensor_tensor(out=obuf[:, :], in0=acc[:, :], in1=sil[:, :], op=ALU.add),
                         r=["acc", "sil"], w=[okey])
                P.dma("sync", lambda e, dst=dst, obuf=obuf, q0=q0: e.dma_start(out=dst[:, q0:q0 + QT], in_=obuf[:, :]),
                      r=[okey], w=[dkey], sem="st_" + okey)

        if upto == "ct":
            raise _Stop()
        xf0 = xb[0][:, :, :].rearrange("p a b -> p (a b)")
        xf1 = xb[1][:, :, :].rearrange("p a b -> p (a b)")
        v3 = lambda ap: ap.rearrange("p (n c) -> p n c", c=8)
        beta_t = v3(xf0[0:64, 0:512])
        gc_t = v3(xf0[0:64, 512:1024])
        eg_t = v3(xf0[0:64, 1024:1536])
        ekd_t = v3(xf0[0:64, 1536:2048])
        bge_t = v3(xf1[0:64, 0:512])
        egl_t = v3(xf1[:, 512:1024])
        negA = xf1[0:64, 1024:1032]
        wbuf = wb[(NCT - 1) % 2]
        wk = "wb%d" % ((NCT - 1) % 2)
        for n in range(64):
            pbank = psA[n // 32]
            pkey = "psA%d" % (n // 32)
            for kt in range(16):
                P.op("tensor", lambda e, pbank=pbank, wbuf=wbuf, kt=kt, n=n: e.matmul(
                    pbank[0:64, (n % 32) * 16:(n % 32) * 16 + 16], lhsT=hT[:, kt, n * 64:n * 64 + 64], rhs=wbuf[:, kt, 0:16],
                    start=(kt == 0), stop=(kt == 15)), r=[wk, "hT%d_%d" % (n * 64 // TB1, kt)], w=[pkey])
        bgraw = acc[0:64, :].rearrange("p (n c) -> p n c", c=16)
        tmpb = sil[0:64, 0:512].rearrange("p (n c) -> p n c", c=8)
        for half in range(2):
            P.op("scalar", lambda e, half=half: e.copy(out=acc[0:64, half * 512:half * 512 + 512], in_=psA[half][0:64, :]),
                 r=["psA%d" % half], w=["acc"])
        if upto == "bg1":
            raise _Stop()
        P.op("scalar", lambda e: e.activation(out=beta_t, in_=bgraw[:, :, 0:8], func=AF.Sigmoid), r=["acc"], w=["beta_t"])
        P.op("vector", lambda e: e.tensor_tensor(out=tmpb, in0=bgraw[:, :, 8:16],
                                                 in1=sm_s[0:64, 16:24].unsqueeze(1).to_broadcast([64, 64, 8]), op=ALU.add),
             r=["acc", "sm_s"], w=["sil"])
        P.op("scalar", lambda e: e.activation(out=tmpb, in_=tmpb, func=AF.Exp), r=["sil"], w=["sil"])
        P.op("scalar", lambda e: e.activation(out=tmpb, in_=tmpb, func=AF.Ln, bias=1.0, scale=1.0), r=["sil"], w=["sil"])
        P.op("scalar", lambda e: e.activation(out=negA, in_=sm_s[0:64, 8:16], func=AF.Exp), r=["sm_s"], w=["negA"])
        P.op("vector", lambda e: e.tensor_scalar(out=negA, in0=negA, scalar1=-1.0, scalar2=None, op0=ALU.mult),
             r=["negA"], w=["negA"])
        P.op("vector", lambda e: e.tensor_tensor(out=tmpb, in0=tmpb, in1=negA.unsqueeze(1).to_broadcast([64, 64, 8]), op=ALU.mult),
             r=["sil", "negA"], w=["sil"])
        if upto == "bg2":
            raise _Stop()
        P.op("tensor", lambda e: e.matmul(psB[0][0:64, :], lhsT=tri64, rhs=sil[0:64, 0:512], start=True, stop=True),
             r=["sil", "cst_s"], w=["psB0"])
        gcf = xf0[0:64, 512:1024]
        P.op("vector", lambda e: e.tensor_copy(out=gcf, in_=psB[0][0:64, :]), r=["psB0"], w=["gc_t"])
        if upto == "bg3":
            raise _Stop()
        P.op("tensor", lambda e: e.matmul(psB[1][:, :], lhsT=sel63, rhs=gcf, start=True, stop=True), r=["gc_t", "cst_s"], w=["psB1"])
        P.op("scalar", lambda e: e.activation(out=xf1[:, 512:1024], in_=psB[1][:, :], func=AF.Exp),
             r=["psB1"], w=["egl_t"])
        if upto == "bg4":
            raise _Stop()
        P.op("vector", lambda e: e.tensor_tensor(out=sil[0:64, 512:1024], in0=psB[1][0:64, :], in1=gcf, op=ALU.subtract),
             r=["egl_t", "gc_t"], w=["sil"])
        if upto == "bg4a":
            raise _Stop()
        P.op("scalar", lambda e: e.activation(out=xf0[0:64, 1536:2048], in_=sil[0:64, 512:1024], func=AF.Exp),
             r=["sil"], w=["ekd_t"])
        if upto == "bg4b":
            raise _Stop()
        P.op("scalar", lambda e: e.activation(out=xf0[0:64, 1024:1536], in_=gcf, func=AF.Exp),
             r=["gc_t"], w=["eg_t"])
        if upto == "bg4c":
            raise _Stop()
        P.op("vector", lambda e: e.tensor_tensor(out=bge_t, in0=beta_t, in1=eg_t, op=ALU.mult),
             r=["beta_t", "eg_t"], w=["bge_t"])
        if upto == "bg5":
            raise _Stop()
        esink = xf1[:, 1040:1044]
        P.op("scalar", lambda e: e.activation(out=esink, in_=sm_s[:, 2:6], func=AF.Exp), r=["sm_s"], w=["esink"])

        if debug:
            dbg_bg = nc.dram_tensor("dbg_bg", [6, 128, 512], F32, kind="ExternalOutput").ap()
            for i_, (ap_, k_) in enumerate(((xf0[0:64, 0:512], "beta_t"), (xf0[0:64, 512:1024], "gc_t"), (xf0[0:64, 1024:1536], "eg_t"),
                                            (xf0[0:64, 1536:2048], "ekd_t"), (xf1[0:64, 0:512], "bge_t"), (xf1[:, 512:1024], "egl_t"))):
                P.dma("sync", lambda e, i_=i_, ap_=ap_: e.dma_start(out=dbg_bg[i_, 0:ap_.shape[0], :], in_=ap_), r=[k_], w=["dbg_bg%d" % i_], sem="dbg")
        if upto == "p2":
            raise _Stop()
        bar_t = xf1[:, 1048:1052]
        P.barrier(lambda e: e.memset(bar_t[:, 0:1], 0.0))
        bigw = [0]

        def carve(words, dt=F32, shape=None):
            n = words if dt == F32 else (words + 1) // 2
            ap = big[:, bigw[0]:bigw[0] + n]
            bigw[0] += n
            assert bigw[0] <= 32768, bigw[0]
            if dt != F32:
                ap = ap.bitcast(dt)
            return ap
        SB = 512
        swq = [carve(4 * SB, BF16).rearrange("p (a b) -> p a b", a=4) for _ in range(2)]
        swk = [carve(SB, BF16) for _ in range(2)]
        swv = [carve(SB, BF16) for _ in range(2)]
        swg = [carve(4 * SB, BF16).rearrange("p (a b) -> p a b", a=4) for _ in range(2)]
        Vp = [carve(256, BF16).rearrange("p (a b) -> p a b", a=2) for _ in range(2)]
        swqe = carve(4 * SB, BF16).rearrange("p (a b) -> p a b", a=4)
        swqo = carve(4 * SB, BF16).rearrange("p (a b) -> p a b", a=4)
        PT = carve(2048, BF16)
        rden = carve(512)
        aout = carve(512)
        ast = [carve(4 * SB, BF16).rearrange("p (a b) -> p a b", a=4) for _ in range(2)]
        psS = [psA[0], psA[1], psA[2], psA[3]]
        g_src3 = g_src.rearrange("(k p) t -> p k t", p=128)
        for sbi in range(T // SB):
            u = sbi % 2
            c0 = sbi * SB
            P.dma("sync", lambda e, u=u, c0=c0: e.dma_start(out=swq[u][:, :, :], in_=s_aq.rearrange("k p t -> p k t")[:, :, c0:c0 + SB]),
                  r=["s_aq%d" % i for i in range(4)], w=["swq%d" % u], sem="swq%d" % u)
            P.dma("sync", lambda e, u=u, c0=c0: e.dma_start(out=swk[u], in_=s_kk[:, c0:c0 + SB]), r=["s_kk"], w=["swk%d" % u], sem="swk%d" % u)
            P.dma("sync", lambda e, u=u, c0=c0: e.dma_start(out=swv[u], in_=s_vv[:, c0:c0 + SB]), r=["s_vv"], w=["swv%d" % u], sem="swv%d" % u)
            P.dma("sync", lambda e, u=u, c0=c0: e.dma_start(out=swg[u][:, :, :], in_=s_ag.rearrange("k p t -> p k t")[:, :, c0:c0 + SB]),
                  r=["s_ag%d" % i for i in range(4)], w=["swg%d" % u], sem="swg%d" % u)
            P.op("gpsimd", lambda e, u=u: e.tensor_scalar(out=swqe, in0=swq[u], scalar1=cst_s[:, 961:962], scalar2=None, op0=ALU.mult),
                 r=["swq%d" % u, "cst_s"], w=["swqm"])
            P.op("vector", lambda e, u=u: e.tensor_scalar(out=swqo, in0=swq[u], scalar1=cst_s[:, 962:963], scalar2=None, op0=ALU.mult),
                 r=["swq%d" % u, "cst_s"], w=["swqm"])
            for bi in range(SB // 128):
                nb = sbi * 4 + bi
                b0 = bi * 128
                vcur = nb % 2
                if upto == "swa0":
                    raise _Stop()
                P.op("tensor", lambda e, u=u, b0=b0: e.matmul(psB[0][:, 0:128], lhsT=swv[u][:, b0:b0 + 128], rhs=cst_bf[:, 1280:1408],
                                                             start=True, stop=True), r=["swv%d" % u, "cst_bf"], w=["psB0"])
                P.op("tensor", lambda e, u=u, b0=b0: e.matmul(psB[0][:, 128:256], lhsT=swv[u][:, b0:b0 + 128], rhs=cst_bf[:, 1408:1536],
                                                             start=True, stop=True), r=["swv%d" % u, "cst_bf"], w=["psB0"])
                P.op("scalar", lambda e, vcur=vcur: e.copy(out=Vp[vcur][:, :, :].rearrange("p a b -> p (a b)"), in_=psB[0][:, 0:256]),
                     r=["psB0"], w=["Vp%d" % vcur])
                if upto == "swa1":
                    raise _Stop()
                kbs = []
                if nb > 0:
                    if bi == 0:
                        kbs.append((0, swk[1 - u], "swk%d" % (1 - u), SB - 128, 1 - vcur))
                    else:
                        kbs.append((0, swk[u], "swk%d" % u, b0 - 128, 1 - vcur))
                kbs.append((1, swk[u], "swk%d" % u, b0, vcur))
                for kbi, kt_, kk_, ko, _v in kbs:
                    for h in range(8):
                        t, par = h // 2, h % 2
                        col = (kbi * 8 + h) * 128
                        pst = psS[col // 512]
                        qm_ = swqe if par == 0 else swqo
                        P.op("tensor", lambda e, kt_=kt_, ko=ko, qm_=qm_, t=t, b0=b0, pst=pst, col=col: e.matmul(
                            pst[:, col % 512:col % 512 + 128], lhsT=kt_[:, ko:ko + 128],
                            rhs=qm_[:, t, b0:b0 + 128], start=True, stop=True),
                            r=[kk_, "swqm"], w=["psA%d" % (col // 512)])
                if upto == "swa2":
                    raise _Stop()
                lo = 0 if nb > 0 else 2
                for q4 in range(lo, 4):
                    P.op("scalar", lambda e, q4=q4: e.activation(out=PT[:, q4 * 512:q4 * 512 + 512], in_=psS[q4][:, :], func=AF.Exp),
                         r=["psA%d" % q4], w=["PT"])
                for kbi in range(lo // 2, 2):
                    P.op("vector", lambda e, kbi=kbi: e.tensor_tensor(
                        out=PT[:, kbi * 1024:kbi * 1024 + 1024].rearrange("p (h q) -> p h q", h=8),
                        in0=PT[:, kbi * 1024:kbi * 1024 + 1024].rearrange("p (h q) -> p h q", h=8),
                        in1=maskCP[:, kbi * 128:kbi * 128 + 128].unsqueeze(1).to_broadcast([128, 8, 128]), op=ALU.mult),
                        r=["PT", "cst_bf"], w=["PT"])
                if upto == "swa3":
                    raise _Stop()
                for t in range(4):
                    mm = [(kbi, par, _v) for (kbi, _a, _b, _c, _v) in kbs for par in range(2)]
                    for i, (kbi, par, vv_) in enumerate(mm):
                        col = (kbi * 8 + 2 * t + par) * 128
                        P.op("tensor", lambda e, vv_=vv_, par=par, col=col, t=t, i=i, n=len(mm): e.matmul(
                            psC[0][:, t * 128:t * 128 + 128], lhsT=Vp[vv_][:, par, :], rhs=PT[:, col:col + 128],
                            start=(i == 0), stop=(i == n - 1)), r=["Vp%d" % vv_, "PT"], w=["psC0"])
                    for i, (kbi, par, vv_) in enumerate(mm):
                        col = (kbi * 8 + 2 * t + par) * 128
                        oo = onesE_bf if par == 0 else onesO_bf
                        P.op("tensor", lambda e, oo=oo, col=col, t=t, i=i, n=len(mm): e.matmul(
                            psC[1][:, t * 128:t * 128 + 128], lhsT=oo, rhs=PT[:, col:col + 128],
                            start=(i == 0), stop=(i == n - 1)), r=["cst_bf", "PT"], w=["psC1"])
                if upto == "swa4":
                    raise _Stop()
                P.op("scalar", lambda e: e.copy(out=rden, in_=psC[1][:, :]), r=["psC1"], w=["rden"])
                P.op("vector", lambda e: e.tensor_tensor(out=rden.rearrange("p (a b) -> p a b", a=4),
                                                         in0=rden.rearrange("p (a b) -> p a b", a=4),
                                                         in1=esink.unsqueeze(2).to_broadcast([128, 4, 128]), op=ALU.add),
                     r=["rden", "esink"], w=["rden"])
                P.op("vector", lambda e: e.reciprocal(out=rden, in_=rden), r=["rden"], w=["rden"])
                P.op("scalar", lambda e: e.copy(out=aout, in_=psC[0][:, :]), r=["psC0"], w=["aout"])
                P.op("vector", lambda e: e.tensor_tensor(out=aout, in0=aout, in1=rden, op=ALU.mult), r=["aout", "rden"], w=["aout"])
                P.op("gpsimd", lambda e, u=u, b0=b0: e.tensor_tensor(out=ast[u][:, :, b0:b0 + 128],
                                                                     in0=aout.rearrange("p (a b) -> p a b", a=4),
                                                                     in1=swg[u][:, :, b0:b0 + 128], op=ALU.mult),
                     r=["aout", "swg%d" % u], w=["ast%d" % u])
            P.dma("sync", lambda e, u=u, c0=c0: e.dma_start(out=g_src3[:, 0:4, c0:c0 + SB], in_=ast[u][:, :, :]),
                  r=["ast%d" % u], w=["g_src_a"], sem="ast%d" % u)
        RG = [[0, 1, 2, 3], [4, 5, 6, 7]]
        if upto == "swa":
            raise _Stop()
        P.barrier(lambda e: e.memset(bar_t[:, 1:2], 0.0), exclude=["g_dst%d" % k for k in range(4)])
        bigw[0] = 0
        for k in range(0 if os.environ.get("KSIM_NO_CC") != "1" else 4, 4):
            P.dma("gpsimd", lambda e, k=k: e.collective_compute("AllGather", ALU.bypass, replica_groups=RG,
                                                                ins=[g_src[k * 128:(k + 1) * 128, :].opt()],
                                                                outs=[g_dst[k * 512:(k + 1) * 512, :].opt()]),
                  r=["g_src_a"], w=["g_dst%d" % k], sem="cc1_%d" % k, inc=1)
        dnin = [carve(16 * SB, BF16).rearrange("p (a b) -> p a b", a=16) for _ in range(2)]
        zin = [carve(8 * SB, BF16).rearrange("p (a b) -> p a b", a=8) for _ in range(2)]
        ogs1 = carve(8 * SB, BF16).rearrange("p (a b) -> p a b", a=8)
        ogs = [ogs1, ogs1]
        S32 = carve(1024).rearrange("p (a b) -> p a b", a=8)
        Sbf = carve(1024, BF16).rearrange("p (a b) -> p a b", a=8)
        P.op("vector", lambda e: e.memset(S32, 0.0), w=["S32"])
        P.op("vector", lambda e: e.memset(Sbf, 0.0), w=["Sbf"])

        def c3(n, m, dt):
            return carve(n * m, dt)[0:64, :].rearrange("p (a b) -> p a b", a=n)

        def two(f):
            return [f(), f()]

        Ktm = two(lambda: c3(4, 128, BF16)); Vtm = two(lambda: c3(8, 128, BF16)); DmT = two(lambda: c3(8, 64, F32))
        M0 = two(lambda: c3(8, 64, BF16)); N0 = two(lambda: c3(8, 64, BF16)); R0 = two(lambda: c3(8, 64, BF16))
        gcb = c3(8, 64, F32); d1 = c3(8, 64, F32); DmTs = c3(8, 64, F32); dgb = c3(8, 64, BF16)
        kbT = carve(8 * 64, BF16).rearrange("p (a b) -> p a b", a=8)
        Mb = [c3(8, 64, BF16) for _ in range(2)]; Nb = [c3(8, 64, BF16) for _ in range(2)]; Rb = [c3(8, 64, BF16) for _ in range(2)]
        Vb = c3(8, 128, BF16); Kbg = c3(8, 128, BF16)
        u32 = two(lambda: c3(8, 128, F32))
        wT = two(lambda: carve(8 * 64, BF16).rearrange("p (a b) -> p a b", a=8))
        qkT = two(lambda: c3(8, 64, BF16)); kd = two(lambda: c3(8, 128, BF16))
        vnew = c3(8, 128, BF16); o32 = c3(8, 128, F32); p3s = c3(8, 128, F32)
        ssq = carve(8)[0:64, :]; onb = c3(8, 128, BF16)
        HG = ((0, 4), (4, 8))

        def bank(i, part=64, lo=0, n=512):
            return ([psA[0], psA[1], psA[2], psA[3], psB[0], psB[1], psC[0], psC[1]][i])[0:part, lo:lo + n]
        BK = ["psA0", "psA1", "psA2", "psA3", "psB0", "psB1", "psC0", "psC1"]

        def h3(ap, a):
            return ap.rearrange("p (a b) -> p a b", a=a)

        def chunk_views(n):
            sbi, ci = n // 8, n % 8
            u, a0 = sbi % 2, ci * 64
            qTm = lambda m, u=u, a0=a0: dnin[u][:, m, a0:a0 + 64]
            kTm = lambda m, u=u, a0=a0: dnin[u][:, 4 + m, a0:a0 + 64]
            vTh = lambda h, u=u, a0=a0: dnin[u][:, 8 + h, a0:a0 + 64]
            return u, a0, qTm, kTm, vTh

        def emit_A1(n):
            s_ = n % 2
            S_ = str(s_)
            u, a0, qTm, kTm, vTh = chunk_views(n)
            dk_ = "dnin%d" % u
            if n % 8 == 0:
                c0 = (n // 8) * SB
                P.dma("sync", lambda e, u=u, c0=c0: e.dma_start(out=dnin[u][:, :, :], in_=s_dn.rearrange("k p t -> p k t")[:, :, c0:c0 + SB]),
                      r=["s_dn%d" % i for i in range(16)], w=["dnin%d" % u], sem="dnin%d" % u)
                P.dma("sync", lambda e, u=u, c0=c0: e.dma_start(out=zin[u][:, :, :], in_=s_z.rearrange("k p t -> p k t")[:, :, c0:c0 + SB]),
                      r=["s_z%d" % i for i in range(8)], w=["zin%d" % u], sem="zin%d" % u)
            for m in range(4):
                P.op("tensor", lambda e, m=m, kTm=kTm: e.matmul(bank(3, 64, m * 128, 128), lhsT=kTm(m), rhs=I_bf, start=True, stop=True),
                     r=[dk_, "cst_bf"], w=[BK[3]])
            P.op("scalar", lambda e, s_=s_: e.copy(out=Ktm[s_].rearrange("p a b -> p (a b)"), in_=bank(3)), r=[BK[3]], w=["Ktm" + S_])
            for hg, (h0, h1) in enumerate(HG):
                for h in range(h0, h1):
                    P.op("tensor", lambda e, h=h, h0=h0, vTh=vTh: e.matmul(bank(4, 64, (h - h0) * 128, 128), lhsT=vTh(h), rhs=I_bf,
                                                                          start=True, stop=True), r=[dk_, "cst_bf"], w=[BK[4]])
                P.op("scalar", lambda e, h0=h0, h1=h1, s_=s_: e.copy(out=Vtm[s_][:, h0:h1, :].rearrange("p a b -> p (a b)"), in_=bank(4)),
                     r=[BK[4]], w=["Vtm" + S_])
            P.op("vector", lambda e, n=n: e.tensor_copy(out=gcb, in_=gc_t[:, n, :].unsqueeze(2).to_broadcast([64, 8, 64])),
                 r=["gc_t"], w=["gcb"])
            for h in range(8):
                P.op("tensor", lambda e, h=h: e.matmul(bank(3, 64, h * 64, 64), lhsT=gcb[:, h, :], rhs=I64f, start=True, stop=True),
                     r=["gcb", "cst_s"], w=[BK[3]])
            P.op("vector", lambda e, n=n: e.tensor_tensor(out=d1, in0=h3(bank(3), 8), in1=gc_t[:, n, :].unsqueeze(2).to_broadcast([64, 8, 64]),
                                                          op=ALU.subtract), r=[BK[3], "gc_t"], w=["d1"])
            P.op("vector", lambda e: e.tensor_scalar(out=d1, in0=d1, scalar1=0.0, scalar2=None, op0=ALU.min), r=["d1"], w=["d1"])
            P.op("scalar", lambda e: e.activation(out=d1, in_=d1, func=AF.Exp), r=["d1"], w=["d1"])
            P.op("vector", lambda e, s_=s_: e.tensor_tensor(out=DmT[s_], in0=d1, in1=mU.unsqueeze(1).to_broadcast([64, 8, 64]), op=ALU.mult),
                 r=["d1", "cst_s"], w=["DmT" + S_])
            P.op("vector", lambda e: e.tensor_tensor(out=DmTs, in0=d1, in1=mUs.unsqueeze(1).to_broadcast([64, 8, 64]), op=ALU.mult),
                 r=["d1", "cst_s"], w=["DmTs"])
            P.op("vector", lambda e, n=n: e.tensor_tensor(out=dgb, in0=I64f.unsqueeze(1).to_broadcast([64, 8, 64]),
                                                          in1=beta_t[:, n, :].unsqueeze(2).to_broadcast([64, 8, 64]), op=ALU.mult),
                 r=["beta_t", "cst_s"], w=["dgb"])
            for h in range(8):
                P.op("tensor", lambda e, h=h, s_=s_: e.matmul(bank(4, 128, h * 64, 64), lhsT=Ktm[s_][:, h // 2, :], rhs=dgb[:, h, :], start=True, stop=True),
                     r=["Ktm" + S_, "dgb"], w=[BK[4]])
            P.op("scalar", lambda e: e.copy(out=kbT.rearrange("p a b -> p (a b)"), in_=bank(4, 128)), r=[BK[4]], w=["kbT"])
            for h in range(8):
                P.op("tensor", lambda e, h=h, kTm=kTm: e.matmul(bank(3, 64, h * 64, 64), lhsT=kTm(h // 2), rhs=kbT[:, h, :], start=True, stop=True),
                     r=[dk_, "kbT"], w=[BK[3]])
            P.op("vector", lambda e, s_=s_: e.scalar_tensor_tensor(out=M0[s_], in0=h3(bank(3), 8), scalar=-1.0, in1=DmTs, op0=ALU.mult, op1=ALU.mult),
                 r=[BK[3], "DmTs"], w=["M0" + S_])
            for h in range(8):
                P.op("tensor", lambda e, h=h, s_=s_: e.matmul(bank(4, 64, h * 64, 64), lhsT=M0[s_][:, h, :], rhs=cst_bf[0:64, 0:64], start=True, stop=True),
                     r=["M0" + S_, "cst_bf"], w=[BK[4]])
            P.op("scalar", lambda e, s_=s_: e.copy(out=N0[s_].rearrange("p a b -> p (a b)"), in_=bank(4)), r=[BK[4]], w=["N0" + S_])
            P.op("vector", lambda e, s_=s_: e.tensor_tensor(out=R0[s_], in0=M0[s_], in1=cst_bf[0:64, 0:64].unsqueeze(1).to_broadcast([64, 8, 64]), op=ALU.add),
                 r=["M0" + S_, "cst_bf"], w=["R0" + S_])

        def emit_A2(n):
            s_ = n % 2
            S_ = str(s_)
            u, a0, qTm, kTm, vTh = chunk_views(n)
            dk_ = "dnin%d" % u
            Mc, Nc, Rc = M0[s_], N0[s_], R0[s_]
            Mk, Nk, Rk = "M0" + S_, "N0" + S_, "R0" + S_
            for lvl in range(1, 6):
                nx = lvl % 2
                if lvl < 5:
                    for h in range(8):
                        P.op("tensor", lambda e, h=h, Nc=Nc, Mc=Mc: e.matmul(bank(5, 64, h * 64, 64), lhsT=Nc[:, h, :], rhs=Mc[:, h, :],
                                                                            start=True, stop=True), r=[Nk, Mk], w=[BK[5]])
                    P.op("vector", lambda e, nx=nx: e.tensor_copy(out=Mb[nx].rearrange("p a b -> p (a b)"), in_=bank(5)),
                         r=[BK[5]], w=["Mb%d" % nx])
                for h in range(8):
                    P.op("tensor", lambda e, h=h, Nc=Nc, Mc=Mc: e.matmul(bank(6, 64, h * 64, 64), lhsT=Mc[:, h, :], rhs=Nc[:, h, :],
                                                                        start=True, stop=True), r=[Nk, Mk], w=[BK[6]])
                P.op("scalar", lambda e, nx=nx: e.copy(out=Nb[nx].rearrange("p a b -> p (a b)"), in_=bank(6)), r=[BK[6]], w=["Nb%d" % nx])
                for h in range(8):
                    P.op("tensor", lambda e, h=h, nx=nx, Rc=Rc: e.matmul(bank(7, 64, h * 64, 64), lhsT=Nb[nx][:, h, :], rhs=Rc[:, h, :],
                                                                        start=True, stop=True), r=["Nb%d" % nx, Rk], w=[BK[7]])
                P.op("vector", lambda e, nx=nx, Rc=Rc: e.tensor_tensor(out=Rb[nx], in0=h3(bank(7), 8), in1=Rc, op=ALU.add),
                     r=[BK[7], Rk], w=["Rb%d" % nx])
                Mc, Nc, Rc = Mb[nx], Nb[nx], Rb[nx]
                Mk, Nk, Rk = "Mb%d" % nx, "Nb%d" % nx, "Rb%d" % nx
            TT, TTk = Rc, Rk
            P.op("vector", lambda e, n=n, s_=s_: e.tensor_tensor(out=Vb, in0=Vtm[s_], in1=beta_t[:, n, :].unsqueeze(2).to_broadcast([64, 8, 128]), op=ALU.mult),
                 r=["Vtm" + S_, "beta_t"], w=["Vb"])
            K8 = Ktm[s_].unsqueeze(2).to_broadcast([64, 4, 2, 128])
            P.op("vector", lambda e, n=n, K8=K8: e.tensor_tensor(out=Kbg.rearrange("p (m r) d -> p m r d", r=2), in0=K8,
                                                               in1=bge_t[:, n, :].rearrange("p (m r) -> p m r", r=2).unsqueeze(3).to_broadcast([64, 4, 2, 128]),
                                                               op=ALU.mult), r=["Ktm" + S_, "bge_t"], w=["Kbg"])
            P.op("vector", lambda e, n=n, K8=K8, s_=s_: e.tensor_tensor(out=kd[s_].rearrange("p (m r) d -> p m r d", r=2), in0=K8,
                                                                      in1=ekd_t[:, n, :].rearrange("p (m r) -> p m r", r=2).unsqueeze(3).to_broadcast([64, 4, 2, 128]),
                                                                      op=ALU.mult), r=["Ktm" + S_, "ekd_t"], w=["kd" + S_])
            for m in range(4):
                P.op("tensor", lambda e, m=m, kTm=kTm, qTm=qTm: e.matmul(bank(5, 64, m * 64, 64), lhsT=kTm(m), rhs=qTm(m), start=True, stop=True),
                     r=[dk_], w=[BK[5]])
            P.op("vector", lambda e, s_=s_: e.tensor_tensor(out=qkT[s_].rearrange("p (m r) d -> p m r d", r=2),
                                                            in0=h3(bank(5, 64, 0, 256), 4).unsqueeze(2).to_broadcast([64, 4, 2, 64]),
                                                            in1=DmT[s_].rearrange("p (m r) d -> p m r d", r=2), op=ALU.mult),
                 r=[BK[5], "DmT" + S_], w=["qkT" + S_])
            for hg, (h0, h1) in enumerate(HG):
                hs = slice(h0, h1)
                for h in range(h0, h1):
                    P.op("tensor", lambda e, h=h, h0=h0, TT=TT: e.matmul(bank(6, 64, (h - h0) * 128, 128), lhsT=TT[:, h, :], rhs=Vb[:, h, :],
                                                                        start=True, stop=True), r=[TTk, "Vb"], w=[BK[6]])
                P.op("scalar", lambda e, hs=hs, s_=s_: e.copy(out=u32[s_][:, hs, :].rearrange("p a b -> p (a b)"), in_=bank(6)), r=[BK[6]], w=["u32" + S_])
            for h in range(8):
                P.op("tensor", lambda e, h=h, TT=TT: e.matmul(bank(7, 128, h * 64, 64), lhsT=Kbg[:, h, :], rhs=TT[:, h, :],
                                                             start=True, stop=True), r=[TTk, "Kbg"], w=[BK[7]])
            P.op("scalar", lambda e, s_=s_: e.copy(out=wT[s_].rearrange("p a b -> p (a b)"), in_=bank(7, 128)), r=[BK[7]], w=["wT" + S_])

        def emit_B(n):
            s_ = n % 2
            S_ = str(s_)
            u, a0, qTm, kTm, vTh = chunk_views(n)
            dk_ = "dnin%d" % u
            for hg, (h0, h1) in enumerate(HG):
                hs = slice(h0, h1)
                for h in range(h0, h1):
                    P.op("tensor", lambda e, h=h, h0=h0, s_=s_: e.matmul(bank(0, 64, (h - h0) * 128, 128), lhsT=wT[s_][:, h, :], rhs=Sbf[:, h, :],
                                                                        start=True, stop=True), r=["wT" + S_, "Sbf"], w=[BK[0]])
                P.op("vector", lambda e, hs=hs, s_=s_: e.tensor_tensor(out=vnew[:, hs, :], in0=u32[s_][:, hs, :], in1=h3(bank(0), 4), op=ALU.subtract),
                     r=["u32" + S_, BK[0]], w=["vnew"])
                for h in range(h0, h1):
                    P.op("tensor", lambda e, h=h, h0=h0, qTm=qTm: e.matmul(bank(1, 64, (h - h0) * 128, 128), lhsT=qTm(h // 2), rhs=Sbf[:, h, :],
                                                                          start=True, stop=True), r=[dk_, "Sbf"], w=[BK[1]])
                for h in range(h0, h1):
                    P.op("tensor", lambda e, h=h, h0=h0, s_=s_: e.matmul(bank(0, 64, (h - h0) * 128, 128), lhsT=qkT[s_][:, h, :], rhs=vnew[:, h, :],
                                                                        start=True, stop=True), r=["qkT" + S_, "vnew"], w=[BK[0]])
                P.op("scalar", lambda e, hs=hs: e.copy(out=p3s[:, hs, :].rearrange("p a b -> p (a b)"), in_=bank(0)), r=[BK[0]], w=["p3s"])
                P.op("vector", lambda e, hs=hs, n=n: e.tensor_tensor(out=o32[:, hs, :], in0=h3(bank(1), 4),
                                                                     in1=eg_t[:, n, hs].unsqueeze(2).to_broadcast([64, 4, 128]), op=ALU.mult),
                     r=[BK[1], "eg_t"], w=["o32"])
                P.op("vector", lambda e, hs=hs: e.tensor_tensor(out=o32[:, hs, :], in0=o32[:, hs, :], in1=p3s[:, hs, :], op=ALU.add),
                     r=["o32", "p3s"], w=["o32"])
                for h in range(h0, h1):
                    P.op("tensor", lambda e, h=h, h0=h0, s_=s_: e.matmul(bank(2, 128, (h - h0) * 128, 128), lhsT=kd[s_][:, h, :], rhs=vnew[:, h, :],
                                                                        start=True, stop=True), r=["kd" + S_, "vnew"], w=[BK[2]])
                P.op("vector", lambda e, hs=hs, n=n: e.tensor_tensor(out=S32[:, hs, :], in0=S32[:, hs, :],
                                                                     in1=egl_t[:, n, hs].unsqueeze(2).to_broadcast([128, 4, 128]), op=ALU.mult),
                     r=["S32", "egl_t"], w=["S32"])
                P.op("vector", lambda e, hs=hs: e.tensor_tensor(out=S32[:, hs, :], in0=S32[:, hs, :], in1=h3(bank(2, 128), 4), op=ALU.add),
                     r=["S32", BK[2]], w=["S32"])
                P.op("scalar", lambda e, hs=hs: e.copy(out=Sbf[:, hs, :], in_=S32[:, hs, :]), r=["S32"], w=["Sbf"])
                P.op("scalar", lambda e, hs=hs: e.activation(out=p3s[:, hs, :], in_=o32[:, hs, :], func=AF.Square),
                     r=["o32"], w=["p3s"])
                P.op("vector", lambda e, hs=hs: e.tensor_reduce(out=ssq[:, hs], in_=p3s[:, hs, :], axis=AX.X, op=ALU.add), r=["p3s"], w=["ssq"])
                P.op("scalar", lambda e, hs=hs: e.activation(out=ssq[:, hs], in_=ssq[:, hs], func=AF.Sqrt, bias=EPS, scale=1.0 / 128), r=["ssq"], w=["ssq"])
                P.op("vector", lambda e, hs=hs: e.reciprocal(out=ssq[:, hs], in_=ssq[:, hs]), r=["ssq"], w=["ssq"])
                P.op("vector", lambda e, hs=hs: e.tensor_tensor(out=onb[:, hs, :], in0=o32[:, hs, :],
                                                                in1=ssq[:, hs].unsqueeze(2).to_broadcast([64, 4, 128]), op=ALU.mult),
                     r=["o32", "ssq"], w=["onb"])
                for h in range(h0, h1):
                    P.op("tensor", lambda e, h=h, h0=h0: e.matmul(bank(1, 128, (h - h0) * 64, 64), lhsT=onb[:, h, :], rhs=cst_bf[0:64, 0:64],
                                                                 start=True, stop=True), r=["onb", "cst_bf"], w=[BK[1]])
                P.op("vector", lambda e, hs=hs, u=u, a0=a0: e.scalar_tensor_tensor(
                    out=ogs[u][:, hs, a0:a0 + 64], in0=h3(bank(1, 128, 0, 256), 4), scalar=sm_s[:, 6:7], in1=zin[u][:, hs, a0:a0 + 64],
                    op0=ALU.mult, op1=ALU.mult), r=[BK[1], "sm_s", "zin%d" % u], w=["ogs0"])
            if n % 8 == 7:
                c0 = (n // 8) * SB
                P.dma("sync", lambda e, u=u, c0=c0: e.dma_start(out=g_src3[:, 4:12, c0:c0 + SB], in_=ogs[u][:, :, :]),
                      r=["ogs0"], w=["g_src_o"], sem="ogs0")

        def record(fn, n):
            if n < 0 or n >= 64:
                return []
            keep = P.ins
            P.ins = []
            fn(n)
            out = P.ins
            P.ins = keep
            return out

        def merge(lists):
            lists = [l for l in lists if l]
            if not lists:
                return []
            tot = max(len(l) for l in lists)
            pos = [0] * len(lists)
            out = []
            for step in range(1, tot + 1):
                for i, l in enumerate(lists):
                    tgt = (len(l) * step + tot - 1) // tot
                    while pos[i] < tgt:
                        out.append(l[pos[i]])
                        pos[i] += 1
            return out

        NCH = T // 64
        for r_ in range(-2, NCH):
            P.ins.extend(merge([record(emit_B, r_), record(emit_A2, r_ + 1), record(emit_A1, r_ + 2)]))

        if upto == "dn":
            raise _Stop()
        for k in range(4, 12):
            P.dma("gpsimd", lambda e, k=k: e.collective_compute("AllGather", ALU.bypass, replica_groups=RG,
                                                                ins=[g_src[k * 128:(k + 1) * 128, :].opt()],
                                                                outs=[g_dst[k * 512:(k + 1) * 512, :].opt()]),
                  r=["g_src_o"], w=["g_dst%d" % k], sem="cc1_%d" % k, inc=1)

        if upto == "g1":
            raise _Stop()
        P.barrier(lambda e: e.memset(bar_t[:, 2:3], 0.0))
        bigw[0] = 0
        gin = carve(48 * SB, BF16).rearrange("p (a b) -> p a b", a=48)
        wos_s = carve(4 * 16 * 128, BF16).rearrange("p (c k n) -> p c k n", c=4, k=16)
        wod_s = carve(4 * 32 * 128, BF16).rearrange("p (c k n) -> p c k n", c=4, k=32)
        sas = carve(4 * SB, BF16).rearrange("p (a b) -> p a b", a=4)
        sbs = carve(4 * SB, BF16).rearrange("p (a b) -> p a b", a=4)
        yst = carve(4 * SB, BF16).rearrange("p (a b) -> p a b", a=4)
        t1 = carve(SB)
        t2 = carve(SB)
        for c in range(4):
            P.dma("gpsimd", lambda e, c=c: e.dma_start(out=wos_s[:, c, :, :], in_=wos[c]), w=["wos_s"], sem="wos")
            P.dma("gpsimd", lambda e, c=c: e.dma_start(out=wod_s[:, c, :, :], in_=wod[c]), w=["wod_s"], sem="wod")
        g_dst3 = g_dst.rearrange("(k p) t -> p k t", p=128)
        y_src3 = y_src.rearrange("(k p) t -> p k t", p=128)
        for tb in range(T // SB):
            c0 = tb * SB
            for r4 in range(4):
                P.dma("sync", lambda e, c0=c0, r4=r4: e.dma_start(out=gin[:, r4 * 12:r4 * 12 + 12, :], in_=g_dst3[:, r4 * 12:r4 * 12 + 12, c0:c0 + SB]),
                      r=["g_dst%d" % k for k in range(12)], w=["gin"], sem="gin%d" % r4)
            P.dma("sync", lambda e, c0=c0: e.dma_start(out=sas, in_=s_sa.rearrange("k p t -> p k t")[:, :, c0:c0 + SB]),
                  r=["s_sa%d" % i for i in range(4)], w=["sas"], sem="sas")
            P.dma("sync", lambda e, c0=c0: e.dma_start(out=sbs, in_=s_sb.rearrange("k p t -> p k t")[:, :, c0:c0 + SB]),
                  r=["s_sb%d" % i for i in range(4)], w=["sbs"], sem="sbs")
            for c in range(4):
                pa_, pb_ = psA[c % 2], psA[2 + c % 2]
                ka_, kb_ = "psA%d" % (c % 2), "psA%d" % (2 + c % 2)
                for kt in range(16):
                    P.op("tensor", lambda e, c=c, kt=kt, pa_=pa_: e.matmul(pa_[:, :], lhsT=wos_s[:, c, kt, :], rhs=gin[:, (kt % 4) * 4 + kt // 4, :],
                                                                          start=(kt == 0), stop=(kt == 15)), r=["wos_s", "gin"], w=[ka_])
                for kt in range(32):
                    P.op("tensor", lambda e, c=c, kt=kt, pb_=pb_: e.matmul(pb_[:, :], lhsT=wod_s[:, c, kt, :], rhs=gin[:, (4 + kt % 8) * 4 + kt // 8, :],
                                                                          start=(kt == 0), stop=(kt == 31)), r=["wod_s", "gin"], w=[kb_])
                P.op("vector", lambda e, c=c, pa_=pa_: e.tensor_tensor(out=t1, in0=pa_[:, :], in1=sas[:, c, :], op=ALU.mult), r=[ka_, "sas"], w=["t1"])
                P.op("vector", lambda e, c=c, pb_=pb_: e.tensor_tensor(out=t2, in0=pb_[:, :], in1=sbs[:, c, :], op=ALU.mult), r=[kb_, "sbs"], w=["t2"])
                P.op("gpsimd", lambda e, c=c: e.tensor_tensor(out=yst[:, c, :], in0=t1, in1=t2, op=ALU.add), r=["t1", "t2"], w=["yst"])
            P.dma("sync", lambda e, c0=c0: e.dma_start(out=y_src3[:, :, c0:c0 + SB], in_=yst), r=["yst"], w=["y_src"], sem="yst")
        for k in range(4):
            P.dma("gpsimd", lambda e, k=k: e.collective_compute("AllGather", ALU.bypass, replica_groups=RG,
                                                                ins=[y_src[k * 128:(k + 1) * 128, :].opt()],
                                                                outs=[y_dst[k * 512:(k + 1) * 512, :].opt()]),
                  r=["y_src"], w=["y_dst%d" % k], sem="cc2_%d" % k, inc=1)

        if upto == "c2":
            raise _Stop()
        P.barrier(lambda e: e.memset(bar_t[:, 3:4], 0.0))
        bigw[0] = 0
        yin = [carve(16 * SB, BF16).rearrange("p (a b) -> p a b", a=16) for _ in range(2)]
        wout_s = carve(4 * 16 * 128, BF16).rearrange("p (c k n) -> p c k n", c=4, k=16)
        xr = [carve(4 * SB).rearrange("p (a b) -> p a b", a=4) for _ in range(2)]
        ost = [carve(4 * SB).rearrange("p (a b) -> p a b", a=4) for _ in range(2)]
        for c in range(4):
            P.dma("gpsimd", lambda e, c=c: e.dma_start(out=wout_s[:, c, :, :], in_=wout[c]), w=["wout_s"], sem="wout")
        y_dst3 = y_dst.rearrange("(k p) t -> p k t", p=128)
        outT3 = outT.rearrange("(k p) t -> p k t", p=128)
        for tb in range(T // SB):
            c0 = tb * SB
            u = tb % 2
            P.dma("sync", lambda e, c0=c0, u=u: e.dma_start(out=yin[u], in_=y_dst3[:, :, c0:c0 + SB]), r=["y_dst%d" % k for k in range(4)], w=["yin%d" % u], sem="yin%d" % u)
            P.dma("sync", lambda e, c0=c0, u=u: e.dma_start(out=xr[u], in_=xres.rearrange("k p t -> p k t")[:, :, c0:c0 + SB]), w=["xr%d" % u], sem="xr%d" % u)
            for c in range(4):
                pa_ = psA[c]
                ka_ = "psA%d" % c
                for kt in range(16):
                    P.op("tensor", lambda e, c=c, kt=kt, pa_=pa_, u=u: e.matmul(pa_[:, :], lhsT=wout_s[:, c, kt, :], rhs=yin[u][:, (kt % 4) * 4 + kt // 4, :],
                                                                               start=(kt == 0), stop=(kt == 15)), r=["wout_s", "yin%d" % u], w=[ka_])
                P.op("vector", lambda e, c=c, pa_=pa_, u=u: e.scalar_tensor_tensor(out=ost[u][:, c, :], in0=pa_[:, :], scalar=mod[:, 32 + c:33 + c],
                                                                                   in1=xr[u][:, c, :], op0=ALU.mult, op1=ALU.add),
                     r=[ka_, "mod", "xr%d" % u], w=["ost%d" % u])
            P.dma("sync", lambda e, c0=c0, u=u: e.dma_start(out=outT3[:, :, c0:c0 + SB], in_=ost[u]), r=["ost%d" % u], w=["outT"], sem="ost%d" % u)
    except _Stop:
        pass

    allw = set()
    for I in P.ins:
        if I["dma"] is not None:
            allw.update(I["w"])
    endt = sb("endt", [128, 1])
    P.barrier(lambda e: e.memset(endt[:, :], 0.0))
    P.op("sync", lambda e: e.engine_nop() if hasattr(e, "engine_nop") else None, r=sorted(allw), w=["__end"])
    P.finalize()
    P.emit(nc, es)
    es.close()
    return nc


def tile_w(w):
    K, N = w.shape
    assert N % 128 == 0
    return np.ascontiguousarray(w.reshape(K // 128, 128, N // 128, 128).transpose(2, 1, 0, 3))


def prep_inputs(inp, b, j):
    x = inp["x"][b]
    w_in = inp["w_in"][0]
    m = {}
    xT_ = np.ascontiguousarray(x.T)
    m["xT"] = np.ascontiguousarray(xT_.reshape(16, 128, T // 128, 128).transpose(2, 1, 0, 3))
    m["xres"] = np.ascontiguousarray(xT_[512 * j:512 * j + 512].reshape(4, 128, T))
    m["cT"] = np.ascontiguousarray(inp["c"][b].reshape(16, 128).T)
    m["pos"] = np.ascontiguousarray(inp["positions"][b].reshape(1, T).astype(np.int32))
    w_ada = inp["w_ada"][0]
    cols = np.concatenate([np.arange(0, 4096), 4096 + 512 * j + np.arange(512)])
    m["wada"] = tile_w(w_ada[:, cols])
    m["bada"] = np.ascontiguousarray(inp["b_ada"][0][cols].reshape(36, 128).T)
    m["nw"] = np.ascontiguousarray(inp["norm_w"][0].reshape(16, 128).T)
    o_aq, o_ak, o_av, o_ag, o_dn, o_dz, o_db, o_da, o_ma, o_mb = np.cumsum([0, 2048, 256, 256, 2048, 8192, 4096, 32, 32, 2048])
    r = np.arange
    dnq = o_dn + 512 * j + r(512)
    dnk = o_dn + 2048 + 512 * j + r(512)
    dnv = o_dn + 4096 + 1024 * j + r(1024)
    dz = o_dz + 1024 * j + r(1024)
    aq = o_aq + 512 * j + r(512)
    kk = np.concatenate([o_ak + 64 * j + r(64), o_ak + 64 * j + r(64)])
    vv = np.concatenate([o_av + 64 * j + r(64), o_av + 64 * j + r(64)])
    ag = o_ag + 512 * j + r(512)
    ma = o_ma + 512 * j + r(512)
    mb = o_mb + 512 * j + r(512)
    bg = np.concatenate([o_db + 8 * j + r(8), o_da + 8 * j + r(8)])
    cols = np.concatenate([dnq, dnk, dnv, dz, aq, kk, vv, ag, ma, mb])
    wbg = np.zeros((2048, 128), np.float32)
    wbg[:, :16] = w_in[:, bg]
    m["wa"] = tile_w(np.concatenate([w_in[:, cols], wbg], axis=1))
    cwc = inp["conv_w"][0][:, np.concatenate([dnq, dnk, dnv]) - o_dn]
    m["cw"] = np.ascontiguousarray(cwc.reshape(4, 16, 128).transpose(2, 1, 0))
    cs = slice(512 * j, 512 * j + 512)
    m["wos"] = tile_w(inp["w_o_swa"][0][:, cs])
    wod_ = inp["w_o_dn"][0][:, cs]
    m["wod"] = np.ascontiguousarray(wod_.reshape(32, 128, 4, 128).transpose(2, 1, 0, 3))
    m["wout"] = tile_w(inp["w_out"][0][:, cs])
    cst = np.zeros((128, 1536), np.float32)
    cst[:, 0:128] = np.eye(128)
    for q in range(2):
        cst[64 * q:64 * q + 64, 128 + 64 * q:128 + 64 * q + 64] = 1.0 / 64
    for q in range(2):
        for i in range(8):
            cst[64 * q + i + 8, 256 + 64 * q + i] = -1.0
            cst[64 * q + i, 256 + 64 * q + i + 8] = 1.0
    ii = np.arange(64)
    cst[0:64, 384:448] = (ii[:, None] <= ii[None, :])
    cst[63, 448:576] = 1.0
    cst[0:64, 640:704] = (ii[:, None] < ii[None, :])
    kq = np.arange(128)
    cst[:, 704:832] = (kq[None, :] < kq[:, None])
    cst[:, 832:960] = (kq[None, :] >= kq[:, None])
    invf = 500000.0 ** (-np.arange(8, dtype=np.float32) * (2.0 / 16))
    for q in range(2):
        cst[64 * q:64 * q + 8, 960] = invf
        cst[64 * q + 8:64 * q + 16, 960] = invf
    cst[:, 1024:1088] = 1.0
    cst[:, 1152 + 64:1280] = 1.0
    cst[0:64, 961] = 1.0
    cst[64:128, 962] = 1.0
    for k in range(64):
        cst[k, 1280 + k] = 1.0
        cst[k, 1408 + 64 + k] = 1.0
    m["cst"] = cst
    sm = np.zeros((128, 32), np.float32)
    sm[:, 0] = np.tile(inp["q_norm_w"][0], 2)
    sm[:, 1] = np.tile(inp["k_norm_w"][0], 2)
    sk = inp["sinks"][0][8 * j:8 * j + 8]
    for t in range(4):
        sm[0:64, 2 + t] = sk[2 * t]
        sm[64:128, 2 + t] = sk[2 * t + 1]
    sm[:, 6] = inp["dn_norm_w"][0]
    sm[:, 8:16] = inp["a_log"][0][8 * j:8 * j + 8][None, :]
    sm[:, 16:24] = inp["dt_bias"][0][8 * j:8 * j + 8][None, :]
    m["sm"] = sm
    return m


_NC_CACHE = {}


def kernel(**inp):
    inp = {k: np.asarray(v) for k, v in inp.items()}
    if "nc" not in _NC_CACHE:
        _NC_CACHE["nc"] = build()
    nc = _NC_CACHE["nc"]
    in_maps = [prep_inputs(inp, c // 4, c % 4) for c in range(8)]
    res = run_bass_kernel_spmd(nc, in_maps, core_ids=list(range(8)))
    out = np.zeros((2, T, D), np.float32)
    for c in range(8):
        b, j = c // 4, c % 4
        out[b][:, 512 * j:512 * j + 512] = np.asarray(res.results[c]["outT"]).T
    return out
```

```python
import os
import numpy as np
from contextlib import ExitStack
import concourse.bass as bass
import concourse.mybir as mybir
from concourse.bass_utils import run_bass_kernel_spmd

F32 = mybir.dt.float32
BF16 = mybir.dt.bfloat16
I32 = mybir.dt.int32
AF = mybir.ActivationFunctionType
ALU = mybir.AluOpType
AX = mybir.AxisListType

D = 2048
T = 4096
EPS = 1e-6
NCT = 43
SAME_ENGINE_SYNC = True


class Prog:
    def __init__(self):
        self.ins = []
        self.keys = set()
        self.ep = ()

    def op(self, eng, fn, r=(), w=()):
        self.keys.update(r); self.keys.update(w)
        self.ins.append(dict(eng=eng, fn=fn, r=tuple(r) + self.ep, w=tuple(w), dma=None))

    def dma(self, eng, fn, r=(), w=(), sem=None, inc=16):
        assert sem is not None
        self.keys.update(r); self.keys.update(w)
        self.ins.append(dict(eng=eng, fn=fn, r=tuple(r) + self.ep, w=tuple(w), dma=sem, inc=inc))

    def barrier(self, fn, exclude=()):
        ks = [k for k in sorted(self.keys, key=str) if k not in exclude]
        self.ins.append(dict(eng="gpsimd", fn=fn, r=(), w=tuple(ks) + ("__epoch",), dma=None))
        self.ep = ("__epoch",)

    def finalize(self):
        ins = self.ins
        last_w, readers, last_dma = {}, {}, {}
        for idx, I in enumerate(ins):
            deps = set()
            for k in I["r"]:
                if k in last_w:
                    deps.add(last_w[k])
            for k in I["w"]:
                if k in last_w:
                    deps.add(last_w[k])
                deps.update(readers.get(k, ()))
            if I["dma"] is not None:
                if I["dma"] in last_dma:
                    deps.add(last_dma[I["dma"]])
                last_dma[I["dma"]] = idx
            deps.discard(idx)
            I["deps"] = deps
            for k in I["r"]:
                readers.setdefault(k, []).append(idx)
            for k in I["w"]:
                last_w[k] = idx
                readers[k] = []
        need = [False] * len(ins)
        for idx, I in enumerate(ins):
            for d in I["deps"]:
                P = ins[d]
                if P["dma"] is not None:
                    continue
                if P["eng"] != I["eng"]:
                    need[d] = True
                elif SAME_ENGINE_SYNC and I["eng"] != "tensor":
                    need[d] = True
        cnt, dcnt = {}, {}
        for idx, I in enumerate(ins):
            if I["dma"] is not None:
                dcnt[I["dma"]] = dcnt.get(I["dma"], 0) + I["inc"]
                I["sig"] = (("dma", I["dma"]), dcnt[I["dma"]], I["inc"])
            elif need[idx]:
                cnt[I["eng"]] = cnt.get(I["eng"], 0) + 1
                I["sig"] = (("eng", I["eng"]), cnt[I["eng"]], 1)
            else:
                I["sig"] = None
        known = {}
        for idx, I in enumerate(ins):
            waits = {}
            for d in I["deps"]:
                P = ins[d]
                if P["sig"] is None:
                    continue
                if P["dma"] is None and P["eng"] == I["eng"] and (I["eng"] == "tensor" or not SAME_ENGINE_SYNC):
                    continue
                s, v, _ = P["sig"]
                waits[s] = max(waits.get(s, 0), v)
            kn = known.setdefault(I["eng"], {})
            out = []
            for s, v in waits.items():
                if kn.get(s, 0) >= v:
                    continue
                kn[s] = v
                out.append((s, v))
            I["waits"] = out
        self.dma_keys = sorted(dcnt.keys(), key=str)
        return self

    def emit(self, nc, es):
        engs = ["tensor", "vector", "scalar", "gpsimd", "sync"]
        sems = {}
        for e in engs:
            sems[("eng", e)] = es.enter_context(nc.semaphore("se_" + e))
        for i, k in enumerate(self.dma_keys):
            sems[("dma", k)] = es.enter_context(nc.semaphore("sd_%d" % i))
        per = {e: [I for I in self.ins if I["eng"] == e] for e in engs}
        block = es.enter_context(nc.Block())

        def make(name):
            def body(e):
                for I in per[name]:
                    for s, v in I["waits"]:
                        e.wait_ge(sems[s], v)
                    bi = I["fn"](e)
                    if I["sig"] is not None:
                        s, v, inc = I["sig"]
                        bi.then_inc(sems[s], inc)
            return body

        block.tensor(make("tensor"))
        block.vector(make("vector"))
        block.scalar(make("scalar"))
        block.gpsimd(make("gpsimd"))
        block.sync(make("sync"))


class _Stop(Exception):
    pass


def build(debug=False, stop_after=99, upto=None):
    nc = bass.Bass("TRN2", target_bir_lowering=False)
    P = Prog()
    es = ExitStack()

    def din(name, shape, dt=F32):
        return nc.dram_tensor(name, list(shape), dt, kind="ExternalInput").ap()

    def dscr(name, shape, dt=BF16):
        if debug:
            return nc.dram_tensor(name, list(shape), dt, kind="ExternalOutput").ap()
        return nc.dram_tensor(name, list(shape), dt).ap()

    def sb(name, shape, dt=F32):
        return es.enter_context(nc.sbuf_tensor(name, list(shape), dt))

    def ps(name, shape, dt=F32):
        return es.enter_context(nc.psum_tensor(name, list(shape), dt))

    xT = din("xT", [T // 128, 128, 16, 128])
    cT = din("cT", [128, 16])
    wada = din("wada", [36, 128, 16, 128])
    bada = din("bada", [128, 36])
    nw = din("nw", [128, 16])
    wa = din("wa", [NCT, 128, 16, 128])
    cw = din("cw", [128, 16, 4])
    s_dn = dscr("s_dn", [16, 128, T])
    s_z = dscr("s_z", [8, 128, T])
    s_hT = dscr("s_hT", [16, 128, T]) if debug else None

    ones_bf = sb("ones_bf", [128, 128], BF16)
    big = sb("big", [128, 32768])
    hT = big[:, :].bitcast(BF16).rearrange("p (a b) -> p a b", a=16)
    mod = sb("mod", [128, 36])
    gam = sb("gam", [128, 16])
    cT_s = sb("cT_s", [128, 16])
    sc_s = sb("sc_s", [128, 16])
    bada_s = sb("bada_s", [128, 36])
    nw_s = sb("nw_s", [128, 16])
    cw_s = sb("cw_s", [128, 16, 4])
    P.op("gpsimd", lambda e: e.memset(ones_bf[:, :], 1.0), w=["ones_bf"])
    onesD_bf = sb("onesD_bf", [128, 128], BF16)
    P.op("gpsimd", lambda e: e.memset(onesD_bf[:, :], 1.0 / D), w=["onesD_bf"])
    P.dma("sync", lambda e: e.dma_start(out=cT_s[:, :], in_=cT[:, :]), w=["cT_s"], sem="small0")
    P.dma("sync", lambda e: e.dma_start(out=bada_s[:, :], in_=bada[:, :]), w=["bada_s"], sem="small1")
    P.dma("sync", lambda e: e.dma_start(out=nw_s[:, :], in_=nw[:, :]), w=["nw_s"], sem="small2")
    P.dma("sync", lambda e: e.dma_start(out=cw_s[:, :, :], in_=cw[:, :, :]), w=["cw_s"], sem="small3")

    psA = [ps("psA%d" % i, [128, 512]) for i in range(4)]
    psB = [ps("psB%d" % i, [128, 512]) for i in range(2)]
    psC = [ps("psC%d" % i, [128, 512]) for i in range(2)]

    P.op("scalar", lambda e: e.activation(out=sc_s[:, :], in_=cT_s[:, :], func=AF.Silu), r=["cT_s"], w=["sc_s"])
    wad = [sb("wad%d" % i, [128, 16, 128]) for i in range(2)]
    for t in range(36):
        buf = wad[t % 2]
        key = "wad%d" % (t % 2)
        P.dma("sync", lambda e, buf=buf, t=t: e.dma_start(out=buf[:, :, :], in_=wada[t]), w=[key], sem=key)
        for kt in range(16):
            P.op("tensor", lambda e, buf=buf, t=t, kt=kt: e.matmul(
                psA[0][:, t:t + 1], lhsT=buf[:, kt, :], rhs=sc_s[:, kt:kt + 1], start=(kt == 0), stop=(kt == 15)),
                r=[key, "sc_s"], w=["psA0"])
    P.op("vector", lambda e: e.tensor_tensor(out=mod[:, :], in0=psA[0][:, 0:36], in1=bada_s[:, :], op=ALU.add),
         r=["psA0", "bada_s"], w=["mod"])
    P.op("vector", lambda e: e.scalar_tensor_tensor(out=gam[:, :], in0=mod[:, 16:32], scalar=1.0, in1=nw_s[:, :],
                                                     op0=ALU.add, op1=ALU.mult), r=["mod", "nw_s"], w=["gam"])

    TB1 = 128
    xb = [sb("xb%d" % i, [128, 16, TB1]) for i in range(2)]
    sq = sb("sq", [128, 16, TB1], BF16)
    rstd = sb("rstd", [128, TB1])
    for tb in range(T // TB1):
        xbuf = xb[tb % 2]
        xk = "xb%d" % (tb % 2)
        t0 = tb * TB1
        P.dma("sync", lambda e, xbuf=xbuf, t0=t0: e.dma_start(out=xbuf[:, :, :], in_=xT[t0 // TB1]),
              w=[xk], sem=xk)
        P.op("scalar", lambda e, xbuf=xbuf: e.activation(out=sq[:, :, :], in_=xbuf[:, :, :], func=AF.Square),
             r=[xk], w=["sq"])
        pb = psB[tb % 2]
        pk = "psB%d" % (tb % 2)
        for dt in range(16):
            P.op("tensor", lambda e, pb=pb, dt=dt: e.matmul(pb[:, 0:TB1], lhsT=onesD_bf[:, :], rhs=sq[:, dt, :],
                                                           start=(dt == 0), stop=(dt == 15)),
                 r=["sq", "onesD_bf"], w=[pk])
        P.op("scalar", lambda e, pb=pb: e.activation(out=rstd[:, :], in_=pb[:, 0:TB1], func=AF.Sqrt, bias=EPS, scale=1.0),
             r=[pk], w=["rstd"])
        P.op("vector", lambda e: e.reciprocal(out=rstd[:, :], in_=rstd[:, :]), r=["rstd"], w=["rstd"])
        P.op("vector", lambda e, xbuf=xbuf: e.tensor_tensor(
            out=xbuf[:, :, :], in0=xbuf[:, :, :], in1=rstd[:, :].unsqueeze(1).to_broadcast([128, 16, TB1]), op=ALU.mult),
            r=[xk, "rstd"], w=[xk])
        for dt in range(16):
            P.op("scalar", lambda e, t0=t0, xbuf=xbuf, dt=dt: e.activation(
                out=hT[:, dt, t0:t0 + TB1], in_=xbuf[:, dt, :], func=AF.Identity, bias=mod[:, dt:dt + 1], scale=gam[:, dt:dt + 1]),
                r=[xk, "mod", "gam"], w=["hT%d_%d" % (tb, dt)])
    hT_keys = ["hT%d_%d" % (tb, dt) for tb in range(T // TB1) for dt in range(16)]
    if debug:
        P.dma("sync", lambda e: e.dma_start(out=s_hT.rearrange("k p t -> p k t"), in_=hT[:, :, :]),
              r=hT_keys, w=["s_hT"], sem="dbg")

    pos = din("pos", [1, T], I32)
    cst = din("cst", [128, 1536])
    sm = din("sm", [128, 32])
    wos = din("wos", [4, 128, 16, 128])
    wod = din("wod", [4, 128, 32, 128])
    wout = din("wout", [4, 128, 16, 128])
    xres = din("xres", [4, 128, T])
    outT = nc.dram_tensor("outT", [512, T], F32, kind="ExternalOutput").ap()
    s_aq = dscr("s_aq", [4, 128, T])
    s_kk = dscr("s_kk", [128, T])
    s_vv = dscr("s_vv", [128, T])
    s_ag = dscr("s_ag", [4, 128, T])
    s_sa = dscr("s_sa", [4, 128, T])
    s_sb = dscr("s_sb", [4, 128, T])
    g_src = dscr("g_src", [12 * 128, T])
    g_dst = nc.dram_tensor("g_dst", [48 * 128, T], BF16).ap()
    y_src = dscr("y_src", [4 * 128, T])
    y_dst = nc.dram_tensor("y_dst", [16 * 128, T], BF16).ap()
    cst_s = sb("cst_s", [128, 1536])
    cst_bf = sb("cst_bf", [128, 1536], BF16)
    sm_s = sb("sm_s", [128, 32])
    P.dma("sync", lambda e: e.dma_start(out=cst_s[:, :], in_=cst[:, :]), w=["cst_s"], sem="small0")
    P.dma("sync", lambda e: e.dma_start(out=sm_s[:, :], in_=sm[:, :]), w=["sm_s"], sem="small1")
    P.op("vector", lambda e: e.tensor_copy(out=cst_bf[:, :], in_=cst_s[:, :]), r=["cst_s"], w=["cst_bf"])
    I_bf = cst_bf[:, 0:128]
    blk64_bf = cst_bf[:, 128:256]
    Pm_bf = cst_bf[:, 256:384]
    tri64 = cst_s[0:64, 384:448]
    sel63 = cst_s[0:64, 448:576]
    mU = cst_s[0:64, 384:448]
    mUs = cst_s[0:64, 640:704]
    maskCP = cst_bf[:, 704:960]
    invf = cst_s[:, 960:961]
    onesE_bf = cst_bf[:, 1024:1152]
    onesO_bf = cst_bf[:, 1152:1280]
    I64f = cst_s[0:64, 0:64]
    PI = float(np.pi)

    try:
        wb = [sb("wb%d" % i, [128, 16, 128], BF16) for i in range(2)]
        QT = 1024
        Y = [sb("Y%d" % i, [128, 3 + QT]) for i in range(2)]
        acc = sb("acc", [128, QT])
        sil = sb("sil", [128, QT])
        sqb = sb("sqb", [128, QT], BF16)
        rs2 = sb("rs2", [128, QT])
        sqf = sq[:, :, :].rearrange("p a b -> p (a b)")
        ob = [sqf[:, i * QT:(i + 1) * QT] for i in range(2)]
        Cf = wad[0][:, :, :].rearrange("p a b -> p (a b)").bitcast(BF16)
        Sf = wad[1][:, :, :].rearrange("p a b -> p (a b)").bitcast(BF16)
        rs2i = rs2[:, :].bitcast(I32)
        for qi in range(4):
            q0 = qi * QT
            P.dma("sync", lambda e, q0=q0: e.dma_start(out=rs2i, in_=pos[0:1, q0:q0 + QT].partition_broadcast(128)),
                  w=["rs2"], sem="posld")
            P.op("vector", lambda e: e.tensor_copy(out=acc[:, :], in_=rs2i), r=["rs2"], w=["acc"])
            P.op("vector", lambda e: e.tensor_scalar(out=acc[:, :], in0=acc[:, :], scalar1=invf, scalar2=None, op0=ALU.mult),
                 r=["acc", "cst_s"], w=["acc"])
            for tab, tk, offs in ((Sf, "wad1", 0.0), (Cf, "wad0", PI / 2)):
                if offs != 0.0:
                    P.op("vector", lambda e, offs=offs: e.tensor_scalar(out=acc[:, :], in0=acc[:, :], scalar1=offs, scalar2=None,
                                                                        op0=ALU.add), r=["acc"], w=["acc"])
                P.op("vector", lambda e: e.tensor_scalar(out=sil[:, :], in0=acc[:, :], scalar1=1.0 / (2 * PI), scalar2=None,
                                                         op0=ALU.mult), r=["acc"], w=["sil"])
                P.op("vector", lambda e: e.tensor_copy(out=rs2i, in_=sil[:, :]), r=["sil"], w=["rs2"])
                P.op("vector", lambda e: e.tensor_copy(out=sil[:, :], in_=rs2i), r=["rs2"], w=["sil"])
                P.op("vector", lambda e: e.scalar_tensor_tensor(out=sil[:, :], in0=sil[:, :], scalar=-2 * PI, in1=acc[:, :],
                                                                op0=ALU.mult, op1=ALU.add), r=["sil", "acc"], w=["sil"])
                P.op("vector", lambda e: e.tensor_scalar(out=rs2[:, :], in0=sil[:, :], scalar1=PI, scalar2=None, op0=ALU.is_gt),
                     r=["sil"], w=["rs2"])
                P.op("vector", lambda e: e.scalar_tensor_tensor(out=sil[:, :], in0=rs2[:, :], scalar=-2 * PI, in1=sil[:, :],
                                                                op0=ALU.mult, op1=ALU.add), r=["sil", "rs2"], w=["sil"])
                P.op("vector", lambda e: e.tensor_scalar(out=rs2[:, :], in0=sil[:, :], scalar1=-PI, scalar2=None, op0=ALU.is_lt),
                     r=["sil"], w=["rs2"])
                P.op("vector", lambda e: e.scalar_tensor_tensor(out=sil[:, :], in0=rs2[:, :], scalar=2 * PI, in1=sil[:, :],
                                                                op0=ALU.mult, op1=ALU.add), r=["sil", "rs2"], w=["sil"])
                P.op("scalar", lambda e, tab=tab, q0=q0: e.activation(out=tab[:, q0:q0 + QT], in_=sil[:, :], func=AF.Sin),
                     r=["sil"], w=[tk])

        if upto == "tab":
            raise _Stop()
        yc = 0
        oc = 0
        pa = 0
        n_ct = min(NCT - 1, stop_after)
        P.dma("gpsimd", lambda e: e.dma_start(out=wb[0][:, :, :], in_=wa[0]), w=["wb0"], sem="wb0")
        for ct in range(n_ct):
            wbuf = wb[ct % 2]
            wk = "wb%d" % (ct % 2)
            if ct + 1 < NCT:
                P.dma("gpsimd", lambda e, ct=ct: e.dma_start(out=wb[(ct + 1) % 2][:, :, :], in_=wa[ct + 1]),
                      w=["wb%d" % ((ct + 1) % 2)], sem="wb%d" % ((ct + 1) % 2))
            if ct < 16:
                kind = "conv"
            elif ct < 24:
                kind, fn, dst, dkey = "act", AF.Silu, s_z[ct - 16], "s_z%d" % (ct - 16)
            elif ct < 28:
                kind, dst, dkey = "rope", s_aq[ct - 24], "s_aq%d" % (ct - 24)
            elif ct == 28:
                kind, dst, dkey = "rope", s_kk, "s_kk"
            elif ct == 29:
                kind, fn, dst, dkey = "act", AF.Copy, s_vv, "s_vv"
            elif ct < 34:
                kind, fn, dst, dkey = "act", AF.Silu, s_ag[ct - 30], "s_ag%d" % (ct - 30)
            elif ct < 38:
                kind, fn, dst, dkey = "act", AF.Sigmoid, s_sa[ct - 34], "s_sa%d" % (ct - 34)
            else:
                kind, fn, dst, dkey = "act", AF.Sigmoid, s_sb[ct - 38], "s_sb%d" % (ct - 38)
            for qi in range(T // QT):
                q0 = qi * QT
                if kind in ("conv", "rope"):
                    Yb = Y[yc % 2]
                    Yk = "Y%d" % (yc % 2)
                    Yp = Y[(yc + 1) % 2]
                    Ypk = "Y%d" % ((yc + 1) % 2)
                    yc += 1
                    if kind == "conv":
                        if qi == 0:
                            P.op("gpsimd", lambda e, Yb=Yb: e.memset(Yb[:, 0:3], 0.0), w=[Yk])
                        else:
                            P.op("gpsimd", lambda e, Yb=Yb, Yp=Yp: e.tensor_copy(out=Yb[:, 0:3], in_=Yp[:, QT:QT + 3]),
                                 r=[Ypk], w=[Yk])
                obuf = ob[oc % 2]
                okey = "ob%d" % (oc % 2)
                oc += 1
                for bi in range(QT // 512):
                    t0 = q0 + bi * 512
                    pbank = psA[pa % 4]
                    pkey = "psA%d" % (pa % 4)
                    pa += 1
                    for kt in range(16):
                        P.op("tensor", lambda e, pbank=pbank, wbuf=wbuf, kt=kt, t0=t0: e.matmul(
                            pbank[:, :], lhsT=wbuf[:, kt, :], rhs=hT[:, kt, t0:t0 + 512], start=(kt == 0), stop=(kt == 15)),
                            r=[wk] + ["hT%d_%d" % (t0 // TB1 + i, kt) for i in range(512 // TB1)], w=[pkey])
                    if kind in ("conv", "rope"):
                        P.op("scalar", lambda e, pbank=pbank, Yb=Yb, bi=bi: e.copy(out=Yb[:, 3 + bi * 512:3 + bi * 512 + 512],
                                                                                 in_=pbank[:, :]), r=[pkey], w=[Yk])
                    else:
                        P.op("scalar", lambda e, pbank=pbank, obuf=obuf, bi=bi, fn=fn: e.activation(
                            out=obuf[:, bi * 512:bi * 512 + 512], in_=pbank[:, :], func=fn), r=[pkey], w=[okey])
                if kind == "conv":
                    P.op("vector", lambda e, Yb=Yb, ct=ct: e.tensor_scalar(out=acc[:, :], in0=Yb[:, 0:QT], scalar1=cw_s[:, ct, 0:1],
                                                                           scalar2=None, op0=ALU.mult),
                         r=[Yk, "cw_s"], w=["acc"])
                    for j in range(1, 4):
                        P.op("vector", lambda e, Yb=Yb, ct=ct, j=j: e.scalar_tensor_tensor(
                            out=acc[:, :], in0=Yb[:, j:j + QT], scalar=cw_s[:, ct, j:j + 1], in1=acc[:, :],
                            op0=ALU.mult, op1=ALU.add), r=[Yk, "cw_s", "acc"], w=["acc"])
                    if ct >= 8:
                        P.op("scalar", lambda e, obuf=obuf: e.activation(out=obuf[:, :], in_=acc[:, :], func=AF.Silu),
                             r=["acc"], w=[okey])
                    else:
                        P.op("scalar", lambda e: e.activation(out=sil[:, :], in_=acc[:, :], func=AF.Silu), r=["acc"], w=["sil"])
                        P.op("scalar", lambda e: e.activation(out=sqb[:, :], in_=sil[:, :], func=AF.Square), r=["sil"], w=["sqb"])
                        for bi in range(QT // 512):
                            pb = psB[bi]
                            pk = "psB%d" % bi
                            P.op("tensor", lambda e, pb=pb, bi=bi: e.matmul(pb[:, :], lhsT=ones_bf[:, :],
                                                                           rhs=sqb[:, bi * 512:bi * 512 + 512], start=True, stop=True),
                                 r=["sqb", "ones_bf"], w=[pk])
                            P.op("scalar", lambda e, pb=pb, bi=bi: e.activation(
                                out=rs2[:, bi * 512:bi * 512 + 512], in_=pb[:, :], func=AF.Sqrt, bias=EPS, scale=1.0),
                                r=[pk], w=["rs2"])
                        P.op("vector", lambda e: e.reciprocal(out=rs2[:, :], in_=rs2[:, :]), r=["rs2"], w=["rs2"])
                        qscale = (128.0 ** -0.5) if ct < 4 else 1.0
                        P.op("vector", lambda e, obuf=obuf, qscale=qscale: e.scalar_tensor_tensor(
                            out=obuf[:, :], in0=sil[:, :], scalar=qscale, in1=rs2[:, :], op0=ALU.mult, op1=ALU.mult),
                            r=["sil", "rs2"], w=[okey])
                    dst = s_dn[ct]
                    dkey = "s_dn%d" % ct
                elif kind == "rope":
                    isq = ct < 28
                    P.op("scalar", lambda e, Yb=Yb: e.activation(out=sqb[:, :], in_=Yb[:, 3:3 + QT], func=AF.Square), r=[Yk], w=["sqb"])
                    for bi in range(QT // 512):
                        pb = psB[bi]
                        pk = "psB%d" % bi
                        P.op("tensor", lambda e, pb=pb, bi=bi: e.matmul(pb[:, :], lhsT=blk64_bf, rhs=sqb[:, bi * 512:bi * 512 + 512],
                                                                       start=True, stop=True), r=["sqb", "cst_bf"], w=[pk])
                        sc_ = 64.0 if isq else 1.0
                        P.op("scalar", lambda e, pb=pb, bi=bi, sc_=sc_: e.activation(
                            out=rs2[:, bi * 512:bi * 512 + 512], in_=pb[:, :], func=AF.Sqrt, bias=EPS * sc_, scale=sc_),
                            r=[pk], w=["rs2"])
                    P.op("vector", lambda e: e.reciprocal(out=rs2[:, :], in_=rs2[:, :]), r=["rs2"], w=["rs2"])
                    nwc = sm_s[:, 0:1] if isq else sm_s[:, 1:2]
                    P.op("vector", lambda e, Yb=Yb, nwc=nwc: e.scalar_tensor_tensor(
                        out=sqb[:, :], in0=Yb[:, 3:3 + QT], scalar=nwc, in1=rs2[:, :], op0=ALU.mult, op1=ALU.mult),
                        r=[Yk, "rs2", "sm_s"], w=["sqb"])
                    P.op("vector", lambda e, q0=q0: e.tensor_tensor(out=acc[:, :], in0=sqb[:, :], in1=Cf[:, q0:q0 + QT], op=ALU.mult),
                         r=["sqb", "wad0"], w=["acc"])
                    for bi in range(QT // 512):
                        pb = psC[bi]
                        pk = "psC%d" % bi
                        P.op("tensor", lambda e, pb=pb, bi=bi: e.matmul(pb[:, :], lhsT=Pm_bf, rhs=sqb[:, bi * 512:bi * 512 + 512],
                                                                       start=True, stop=True), r=["sqb", "cst_bf"], w=[pk])
                        P.op("vector", lambda e, pb=pb, bi=bi, q0=q0: e.tensor_tensor(
                            out=sil[:, bi * 512:bi * 512 + 512], in0=pb[:, :], in1=Sf[:, q0 + bi * 512:q0 + bi * 512 + 512], op=ALU.mult),
                            r=[pk, "wad1"], w=["sil"])
                    P.op("gpsimd", lambda e, obuf=obuf: e.tensor_tensor(out=obuf[:, :], in0=acc[:, :], in1=sil[:, :], op=ALU.add),
                         r=["acc", "sil"], w=[okey])
                P.dma("sync", lambda e, dst=dst, obuf=obuf, q0=q0: e.dma_start(out=dst[:, q0:q0 + QT], in_=obuf[:, :]),
                      r=[okey], w=[dkey], sem="st_" + okey)

        if upto == "ct":
            raise _Stop()
        xf0 = xb[0][:, :, :].rearrange("p a b -> p (a b)")
        xf1 = xb[1][:, :, :].rearrange("p a b -> p (a b)")
        v3 = lambda ap: ap.rearrange("p (n c) -> p n c", c=8)
        beta_t = v3(xf0[0:64, 0:512])
        gc_t = v3(xf0[0:64, 512:1024])
        eg_t = v3(xf0[0:64, 1024:1536])
        ekd_t = v3(xf0[0:64, 1536:2048])
        bge_t = v3(xf1[0:64, 0:512])
        egl_t = v3(xf1[:, 512:1024])
        negA = xf1[0:64, 1024:1032]
        wbuf = wb[(NCT - 1) % 2]
        wk = "wb%d" % ((NCT - 1) % 2)
        for n in range(64):
            pbank = psA[n // 32]
            pkey = "psA%d" % (n // 32)
            for kt in range(16):
                P.op("tensor", lambda e, pbank=pbank, wbuf=wbuf, kt=kt, n=n: e.matmul(
                    pbank[0:64, (n % 32) * 16:(n % 32) * 16 + 16], lhsT=hT[:, kt, n * 64:n * 64 + 64], rhs=wbuf[:, kt, 0:16],
                    start=(kt == 0), stop=(kt == 15)), r=[wk, "hT%d_%d" % (n * 64 // TB1, kt)], w=[pkey])
        bgraw = acc[0:64, :].rearrange("p (n c) -> p n c", c=16)
        tmpb = sil[0:64, 0:512].rearrange("p (n c) -> p n c", c=8)
        for half in range(2):
            P.op("scalar", lambda e, half=half: e.copy(out=acc[0:64, half * 512:half * 512 + 512], in_=psA[half][0:64, :]),
                 r=["psA%d" % half], w=["acc"])
        if upto == "bg1":
            raise _Stop()
        P.op("scalar", lambda e: e.activation(out=beta_t, in_=bgraw[:, :, 0:8], func=AF.Sigmoid), r=["acc"], w=["beta_t"])
        P.op("vector", lambda e: e.tensor_tensor(out=tmpb, in0=bgraw[:, :, 8:16],
                                                 in1=sm_s[0:64, 16:24].unsqueeze(1).to_broadcast([64, 64, 8]), op=ALU.add),
             r=["acc", "sm_s"], w=["sil"])
        P.op("scalar", lambda e: e.activation(out=tmpb, in_=tmpb, func=AF.Exp), r=["sil"], w=["sil"])
        P.op("scalar", lambda e: e.activation(out=tmpb, in_=tmpb, func=AF.Ln, bias=1.0, scale=1.0), r=["sil"], w=["sil"])
        P.op("scalar", lambda e: e.activation(out=negA, in_=sm_s[0:64, 8:16], func=AF.Exp), r=["sm_s"], w=["negA"])
        P.op("vector", lambda e: e.tensor_scalar(out=negA, in0=negA, scalar1=-1.0, scalar2=None, op0=ALU.mult),
             r=["negA"], w=["negA"])
        P.op("vector", lambda e: e.tensor_tensor(out=tmpb, in0=tmpb, in1=negA.unsqueeze(1).to_broadcast([64, 64, 8]), op=ALU.mult),
             r=["sil", "negA"], w=["sil"])
        if upto == "bg2":
            raise _Stop()
        P.op("tensor", lambda e: e.matmul(psB[0][0:64, :], lhsT=tri64, rhs=sil[0:64, 0:512], start=True, stop=True),
             r=["sil", "cst_s"], w=["psB0"])
        gcf = xf0[0:64, 512:1024]
        P.op("vector", lambda e: e.tensor_copy(out=gcf, in_=psB[0][0:64, :]), r=["psB0"], w=["gc_t"])
        if upto == "bg3":
            raise _Stop()
        P.op("tensor", lambda e: e.matmul(psB[1][:, :], lhsT=sel63, rhs=gcf, start=True, stop=True), r=["gc_t", "cst_s"], w=["psB1"])
        P.op("scalar", lambda e: e.activation(out=xf1[:, 512:1024], in_=psB[1][:, :], func=AF.Exp),
             r=["psB1"], w=["egl_t"])
        if upto == "bg4":
            raise _Stop()
        P.op("vector", lambda e: e.tensor_tensor(out=sil[0:64, 512:1024], in0=psB[1][0:64, :], in1=gcf, op=ALU.subtract),
             r=["egl_t", "gc_t"], w=["sil"])
        if upto == "bg4a":
            raise _Stop()
        P.op("scalar", lambda e: e.activation(out=xf0[0:64, 1536:2048], in_=sil[0:64, 512:1024], func=AF.Exp),
             r=["sil"], w=["ekd_t"])
        if upto == "bg4b":
            raise _Stop()
        P.op("scalar", lambda e: e.activation(out=xf0[0:64, 1024:1536], in_=gcf, func=AF.Exp),
             r=["gc_t"], w=["eg_t"])
        if upto == "bg4c":
            raise _Stop()
        P.op("vector", lambda e: e.tensor_tensor(out=bge_t, in0=beta_t, in1=eg_t, op=ALU.mult),
             r=["beta_t", "eg_t"], w=["bge_t"])
        if upto == "bg5":
            raise _Stop()
        esink = xf1[:, 1040:1044]
        P.op("scalar", lambda e: e.activation(out=esink, in_=sm_s[:, 2:6], func=AF.Exp), r=["sm_s"], w=["esink"])

        if debug:
            dbg_bg = nc.dram_tensor("dbg_bg", [6, 128, 512], F32, kind="ExternalOutput").ap()
            for i_, (ap_, k_) in enumerate(((xf0[0:64, 0:512], "beta_t"), (xf0[0:64, 512:1024], "gc_t"), (xf0[0:64, 1024:1536], "eg_t"),
                                            (xf0[0:64, 1536:2048], "ekd_t"), (xf1[0:64, 0:512], "bge_t"), (xf1[:, 512:1024], "egl_t"))):
                P.dma("sync", lambda e, i_=i_, ap_=ap_: e.dma_start(out=dbg_bg[i_, 0:ap_.shape[0], :], in_=ap_), r=[k_], w=["dbg_bg%d" % i_], sem="dbg")
        if upto == "p2":
            raise _Stop()
        bar_t = xf1[:, 1048:1052]
        P.barrier(lambda e: e.memset(bar_t[:, 0:1], 0.0))
        bigw = [0]

        def carve(words, dt=F32, shape=None):
            n = words if dt == F32 else (words + 1) // 2
            ap = big[:, bigw[0]:bigw[0] + n]
            bigw[0] += n
            assert bigw[0] <= 32768, bigw[0]
            if dt != F32:
                ap = ap.bitcast(dt)
            return ap
        SB = 512
        swq = [carve(4 * SB, BF16).rearrange("p (a b) -> p a b", a=4) for _ in range(2)]
        swk = [carve(SB, BF16) for _ in range(2)]
        swv = [carve(SB, BF16) for _ in range(2)]
        swg = [carve(4 * SB, BF16).rearrange("p (a b) -> p a b", a=4) for _ in range(2)]
        Vp = [carve(256, BF16).rearrange("p (a b) -> p a b", a=2) for _ in range(2)]
        swqe = carve(4 * SB, BF16).rearrange("p (a b) -> p a b", a=4)
        swqo = carve(4 * SB, BF16).rearrange("p (a b) -> p a b", a=4)
        PT = carve(2048, BF16)
        rden = carve(512)
        aout = carve(512)
        ast = [carve(4 * SB, BF16).rearrange("p (a b) -> p a b", a=4) for _ in range(2)]
        psS = [psA[0], psA[1], psA[2], psA[3]]
        g_src3 = g_src.rearrange("(k p) t -> p k t", p=128)
        for sbi in range(T // SB):
            u = sbi % 2
            c0 = sbi * SB
            P.dma("sync", lambda e, u=u, c0=c0: e.dma_start(out=swq[u][:, :, :], in_=s_aq.rearrange("k p t -> p k t")[:, :, c0:c0 + SB]),
                  r=["s_aq%d" % i for i in range(4)], w=["swq%d" % u], sem="swq%d" % u)
            P.dma("sync", lambda e, u=u, c0=c0: e.dma_start(out=swk[u], in_=s_kk[:, c0:c0 + SB]), r=["s_kk"], w=["swk%d" % u], sem="swk%d" % u)
            P.dma("sync", lambda e, u=u, c0=c0: e.dma_start(out=swv[u], in_=s_vv[:, c0:c0 + SB]), r=["s_vv"], w=["swv%d" % u], sem="swv%d" % u)
            P.dma("sync", lambda e, u=u, c0=c0: e.dma_start(out=swg[u][:, :, :], in_=s_ag.rearrange("k p t -> p k t")[:, :, c0:c0 + SB]),
                  r=["s_ag%d" % i for i in range(4)], w=["swg%d" % u], sem="swg%d" % u)
            P.op("gpsimd", lambda e, u=u: e.tensor_scalar(out=swqe, in0=swq[u], scalar1=cst_s[:, 961:962], scalar2=None, op0=ALU.mult),
                 r=["swq%d" % u, "cst_s"], w=["swqm"])
            P.op("vector", lambda e, u=u: e.tensor_scalar(out=swqo, in0=swq[u], scalar1=cst_s[:, 962:963], scalar2=None, op0=ALU.mult),
                 r=["swq%d" % u, "cst_s"], w=["swqm"])
            for bi in range(SB // 128):
                nb = sbi * 4 + bi
                b0 = bi * 128
                vcur = nb % 2
                if upto == "swa0":
                    raise _Stop()
                P.op("tensor", lambda e, u=u, b0=b0: e.matmul(psB[0][:, 0:128], lhsT=swv[u][:, b0:b0 + 128], rhs=cst_bf[:, 1280:1408],
                                                             start=True, stop=True), r=["swv%d" % u, "cst_bf"], w=["psB0"])
                P.op("tensor", lambda e, u=u, b0=b0: e.matmul(psB[0][:, 128:256], lhsT=swv[u][:, b0:b0 + 128], rhs=cst_bf[:, 1408:1536],
                                                             start=True, stop=True), r=["swv%d" % u, "cst_bf"], w=["psB0"])
                P.op("scalar", lambda e, vcur=vcur: e.copy(out=Vp[vcur][:, :, :].rearrange("p a b -> p (a b)"), in_=psB[0][:, 0:256]),
                     r=["psB0"], w=["Vp%d" % vcur])
                if upto == "swa1":
                    raise _Stop()
                kbs = []
                if nb > 0:
                    if bi == 0:
                        kbs.append((0, swk[1 - u], "swk%d" % (1 - u), SB - 128, 1 - vcur))
                    else:
                        kbs.append((0, swk[u], "swk%d" % u, b0 - 128, 1 - vcur))
                kbs.append((1, swk[u], "swk%d" % u, b0, vcur))
                for kbi, kt_, kk_, ko, _v in kbs:
                    for h in range(8):
                        t, par = h // 2, h % 2
                        col = (kbi * 8 + h) * 128
                        pst = psS[col // 512]
                        qm_ = swqe if par == 0 else swqo
                        P.op("tensor", lambda e, kt_=kt_, ko=ko, qm_=qm_, t=t, b0=b0, pst=pst, col=col: e.matmul(
                            pst[:, col % 512:col % 512 + 128], lhsT=kt_[:, ko:ko + 128],
                            rhs=qm_[:, t, b0:b0 + 128], start=True, stop=True),
                            r=[kk_, "swqm"], w=["psA%d" % (col // 512)])
                if upto == "swa2":
                    raise _Stop()
                lo = 0 if nb > 0 else 2
                for q4 in range(lo, 4):
                    P.op("scalar", lambda e, q4=q4: e.activation(out=PT[:, q4 * 512:q4 * 512 + 512], in_=psS[q4][:, :], func=AF.Exp),
                         r=["psA%d" % q4], w=["PT"])
                for kbi in range(lo // 2, 2):
                    P.op("vector", lambda e, kbi=kbi: e.tensor_tensor(
                        out=PT[:, kbi * 1024:kbi * 1024 + 1024].rearrange("p (h q) -> p h q", h=8),
                        in0=PT[:, kbi * 1024:kbi * 1024 + 1024].rearrange("p (h q) -> p h q", h=8),
                        in1=maskCP[:, kbi * 128:kbi * 128 + 128].unsqueeze(1).to_broadcast([128, 8, 128]), op=ALU.mult),
                        r=["PT", "cst_bf"], w=["PT"])
                if upto == "swa3":
                    raise _Stop()
                for t in range(4):
                    mm = [(kbi, par, _v) for (kbi, _a, _b, _c, _v) in kbs for par in range(2)]
                    for i, (kbi, par, vv_) in enumerate(mm):
                        col = (kbi * 8 + 2 * t + par) * 128
                        P.op("tensor", lambda e, vv_=vv_, par=par, col=col, t=t, i=i, n=len(mm): e.matmul(
                            psC[0][:, t * 128:t * 128 + 128], lhsT=Vp[vv_][:, par, :], rhs=PT[:, col:col + 128],
                            start=(i == 0), stop=(i == n - 1)), r=["Vp%d" % vv_, "PT"], w=["psC0"])
                    for i, (kbi, par, vv_) in enumerate(mm):
                        col = (kbi * 8 + 2 * t + par) * 128
                        oo = onesE_bf if par == 0 else onesO_bf
                        P.op("tensor", lambda e, oo=oo, col=col, t=t, i=i, n=len(mm): e.matmul(
                            psC[1][:, t * 128:t * 128 + 128], lhsT=oo, rhs=PT[:, col:col + 128],
                            start=(i == 0), stop=(i == n - 1)), r=["cst_bf", "PT"], w=["psC1"])
                if upto == "swa4":
                    raise _Stop()
                P.op("scalar", lambda e: e.copy(out=rden, in_=psC[1][:, :]), r=["psC1"], w=["rden"])
                P.op("vector", lambda e: e.tensor_tensor(out=rden.rearrange("p (a b) -> p a b", a=4),
                                                         in0=rden.rearrange("p (a b) -> p a b", a=4),
                                                         in1=esink.unsqueeze(2).to_broadcast([128, 4, 128]), op=ALU.add),
                     r=["rden", "esink"], w=["rden"])
                P.op("vector", lambda e: e.reciprocal(out=rden, in_=rden), r=["rden"], w=["rden"])
                P.op("scalar", lambda e: e.copy(out=aout, in_=psC[0][:, :]), r=["psC0"], w=["aout"])
                P.op("vector", lambda e: e.tensor_tensor(out=aout, in0=aout, in1=rden, op=ALU.mult), r=["aout", "rden"], w=["aout"])
                P.op("gpsimd", lambda e, u=u, b0=b0: e.tensor_tensor(out=ast[u][:, :, b0:b0 + 128],
                                                                     in0=aout.rearrange("p (a b) -> p a b", a=4),
                                                                     in1=swg[u][:, :, b0:b0 + 128], op=ALU.mult),
                     r=["aout", "swg%d" % u], w=["ast%d" % u])
            P.dma("sync", lambda e, u=u, c0=c0: e.dma_start(out=g_src3[:, 0:4, c0:c0 + SB], in_=ast[u][:, :, :]),
                  r=["ast%d" % u], w=["g_src_a"], sem="ast%d" % u)
        RG = [[0, 1, 2, 3], [4, 5, 6, 7]]
        if upto == "swa":
            raise _Stop()
        P.barrier(lambda e: e.memset(bar_t[:, 1:2], 0.0), exclude=["g_dst%d" % k for k in range(4)])
        bigw[0] = 0
        for k in range(0 if os.environ.get("KSIM_NO_CC") != "1" else 4, 4):
            P.dma("gpsimd", lambda e, k=k: e.collective_compute("AllGather", ALU.bypass, replica_groups=RG,
                                                                ins=[g_src[k * 128:(k + 1) * 128, :].opt()],
                                                                outs=[g_dst[k * 512:(k + 1) * 512, :].opt()]),
                  r=["g_src_a"], w=["g_dst%d" % k], sem="cc1_%d" % k, inc=1)
        dnin = [carve(16 * SB, BF16).rearrange("p (a b) -> p a b", a=16) for _ in range(2)]
        zin = [carve(8 * SB, BF16).rearrange("p (a b) -> p a b", a=8) for _ in range(2)]
        ogs1 = carve(8 * SB, BF16).rearrange("p (a b) -> p a b", a=8)
        ogs = [ogs1, ogs1]
        S32 = carve(1024).rearrange("p (a b) -> p a b", a=8)
        Sbf = carve(1024, BF16).rearrange("p (a b) -> p a b", a=8)
        P.op("vector", lambda e: e.memset(S32, 0.0), w=["S32"])
        P.op("vector", lambda e: e.memset(Sbf, 0.0), w=["Sbf"])

        def c3(n, m, dt):
            return carve(n * m, dt)[0:64, :].rearrange("p (a b) -> p a b", a=n)

        def two(f):
            return [f(), f()]

        Ktm = two(lambda: c3(4, 128, BF16)); Vtm = two(lambda: c3(8, 128, BF16)); DmT = two(lambda: c3(8, 64, F32))
        M0 = two(lambda: c3(8, 64, BF16)); N0 = two(lambda: c3(8, 64, BF16)); R0 = two(lambda: c3(8, 64, BF16))
        gcb = c3(8, 64, F32); d1 = c3(8, 64, F32); DmTs = c3(8, 64, F32); dgb = c3(8, 64, BF16)
        kbT = carve(8 * 64, BF16).rearrange("p (a b) -> p a b", a=8)
        Mb = [c3(8, 64, BF16) for _ in range(2)]; Nb = [c3(8, 64, BF16) for _ in range(2)]; Rb = [c3(8, 64, BF16) for _ in range(2)]
        Vb = c3(8, 128, BF16); Kbg = c3(8, 128, BF16)
        u32 = two(lambda: c3(8, 128, F32))
        wT = two(lambda: carve(8 * 64, BF16).rearrange("p (a b) -> p a b", a=8))
        qkT = two(lambda: c3(8, 64, BF16)); kd = two(lambda: c3(8, 128, BF16))
        vnew = c3(8, 128, BF16); o32 = c3(8, 128, F32); p3s = c3(8, 128, F32)
        ssq = carve(8)[0:64, :]; onb = c3(8, 128, BF16)
        HG = ((0, 4), (4, 8))

        def bank(i, part=64, lo=0, n=512):
            return ([psA[0], psA[1], psA[2], psA[3], psB[0], psB[1], psC[0], psC[1]][i])[0:part, lo:lo + n]
        BK = ["psA0", "psA1", "psA2", "psA3", "psB0", "psB1", "psC0", "psC1"]

        def h3(ap, a):
            return ap.rearrange("p (a b) -> p a b", a=a)

        def chunk_views(n):
            sbi, ci = n // 8, n % 8
            u, a0 = sbi % 2, ci * 64
            qTm = lambda m, u=u, a0=a0: dnin[u][:, m, a0:a0 + 64]
            kTm = lambda m, u=u, a0=a0: dnin[u][:, 4 + m, a0:a0 + 64]
            vTh = lambda h, u=u, a0=a0: dnin[u][:, 8 + h, a0:a0 + 64]
            return u, a0, qTm, kTm, vTh

        def emit_A1(n):
            s_ = n % 2
            S_ = str(s_)
            u, a0, qTm, kTm, vTh = chunk_views(n)
            dk_ = "dnin%d" % u
            if n % 8 == 0:
                c0 = (n // 8) * SB
                P.dma("sync", lambda e, u=u, c0=c0: e.dma_start(out=dnin[u][:, :, :], in_=s_dn.rearrange("k p t -> p k t")[:, :, c0:c0 + SB]),
                      r=["s_dn%d" % i for i in range(16)], w=["dnin%d" % u], sem="dnin%d" % u)
                P.dma("sync", lambda e, u=u, c0=c0: e.dma_start(out=zin[u][:, :, :], in_=s_z.rearrange("k p t -> p k t")[:, :, c0:c0 + SB]),
                      r=["s_z%d" % i for i in range(8)], w=["zin%d" % u], sem="zin%d" % u)
            for m in range(4):
                P.op("tensor", lambda e, m=m, kTm=kTm: e.matmul(bank(3, 64, m * 128, 128), lhsT=kTm(m), rhs=I_bf, start=True, stop=True),
                     r=[dk_, "cst_bf"], w=[BK[3]])
            P.op("scalar", lambda e, s_=s_: e.copy(out=Ktm[s_].rearrange("p a b -> p (a b)"), in_=bank(3)), r=[BK[3]], w=["Ktm" + S_])
            for hg, (h0, h1) in enumerate(HG):
                for h in range(h0, h1):
                    P.op("tensor", lambda e, h=h, h0=h0, vTh=vTh: e.matmul(bank(4, 64, (h - h0) * 128, 128), lhsT=vTh(h), rhs=I_bf,
                                                                          start=True, stop=True), r=[dk_, "cst_bf"], w=[BK[4]])
                P.op("scalar", lambda e, h0=h0, h1=h1, s_=s_: e.copy(out=Vtm[s_][:, h0:h1, :].rearrange("p a b -> p (a b)"), in_=bank(4)),
                     r=[BK[4]], w=["Vtm" + S_])
            P.op("vector", lambda e, n=n: e.tensor_copy(out=gcb, in_=gc_t[:, n, :].unsqueeze(2).to_broadcast([64, 8, 64])),
                 r=["gc_t"], w=["gcb"])
            for h in range(8):
                P.op("tensor", lambda e, h=h: e.matmul(bank(3, 64, h * 64, 64), lhsT=gcb[:, h, :], rhs=I64f, start=True, stop=True),
                     r=["gcb", "cst_s"], w=[BK[3]])
            P.op("vector", lambda e, n=n: e.tensor_tensor(out=d1, in0=h3(bank(3), 8), in1=gc_t[:, n, :].unsqueeze(2).to_broadcast([64, 8, 64]),
                                                          op=ALU.subtract), r=[BK[3], "gc_t"], w=["d1"])
            P.op("vector", lambda e: e.tensor_scalar(out=d1, in0=d1, scalar1=0.0, scalar2=None, op0=ALU.min), r=["d1"], w=["d1"])
            P.op("scalar", lambda e: e.activation(out=d1, in_=d1, func=AF.Exp), r=["d1"], w=["d1"])
            P.op("vector", lambda e, s_=s_: e.tensor_tensor(out=DmT[s_], in0=d1, in1=mU.unsqueeze(1).to_broadcast([64, 8, 64]), op=ALU.mult),
                 r=["d1", "cst_s"], w=["DmT" + S_])
            P.op("vector", lambda e: e.tensor_tensor(out=DmTs, in0=d1, in1=mUs.unsqueeze(1).to_broadcast([64, 8, 64]), op=ALU.mult),
                 r=["d1", "cst_s"], w=["DmTs"])
            P.op("vector", lambda e, n=n: e.tensor_tensor(out=dgb, in0=I64f.unsqueeze(1).to_broadcast([64, 8, 64]),
                                                          in1=beta_t[:, n, :].unsqueeze(2).to_broadcast([64, 8, 64]), op=ALU.mult),
                 r=["beta_t", "cst_s"], w=["dgb"])
            for h in range(8):
                P.op("tensor", lambda e, h=h, s_=s_: e.matmul(bank(4, 128, h * 64, 64), lhsT=Ktm[s_][:, h // 2, :], rhs=dgb[:, h, :], start=True, stop=True),
                     r=["Ktm" + S_, "dgb"], w=[BK[4]])
            P.op("scalar", lambda e: e.copy(out=kbT.rearrange("p a b -> p (a b)"), in_=bank(4, 128)), r=[BK[4]], w=["kbT"])
            for h in range(8):
                P.op("tensor", lambda e, h=h, kTm=kTm: e.matmul(bank(3, 64, h * 64, 64), lhsT=kTm(h // 2), rhs=kbT[:, h, :], start=True, stop=True),
                     r=[dk_, "kbT"], w=[BK[3]])
            P.op("vector", lambda e, s_=s_: e.scalar_tensor_tensor(out=M0[s_], in0=h3(bank(3), 8), scalar=-1.0, in1=DmTs, op0=ALU.mult, op1=ALU.mult),
                 r=[BK[3], "DmTs"], w=["M0" + S_])
            for h in range(8):
                P.op("tensor", lambda e, h=h, s_=s_: e.matmul(bank(4, 64, h * 64, 64), lhsT=M0[s_][:, h, :], rhs=cst_bf[0:64, 0:64], start=True, stop=True),
                     r=["M0" + S_, "cst_bf"], w=[BK[4]])
            P.op("scalar", lambda e, s_=s_: e.copy(out=N0[s_].rearrange("p a b -> p (a b)"), in_=bank(4)), r=[BK[4]], w=["N0" + S_])
            P.op("vector", lambda e, s_=s_: e.tensor_tensor(out=R0[s_], in0=M0[s_], in1=cst_bf[0:64, 0:64].unsqueeze(1).to_broadcast([64, 8, 64]), op=ALU.add),
                 r=["M0" + S_, "cst_bf"], w=["R0" + S_])

        def emit_A2(n):
            s_ = n % 2
            S_ = str(s_)
            u, a0, qTm, kTm, vTh = chunk_views(n)
            dk_ = "dnin%d" % u
            Mc, Nc, Rc = M0[s_], N0[s_], R0[s_]
            Mk, Nk, Rk = "M0" + S_, "N0" + S_, "R0" + S_
            for lvl in range(1, 6):
                nx = lvl % 2
                if lvl < 5:
                    for h in range(8):
                        P.op("tensor", lambda e, h=h, Nc=Nc, Mc=Mc: e.matmul(bank(5, 64, h * 64, 64), lhsT=Nc[:, h, :], rhs=Mc[:, h, :],
                                                                            start=True, stop=True), r=[Nk, Mk], w=[BK[5]])
                    P.op("vector", lambda e, nx=nx: e.tensor_copy(out=Mb[nx].rearrange("p a b -> p (a b)"), in_=bank(5)),
                         r=[BK[5]], w=["Mb%d" % nx])
                for h in range(8):
                    P.op("tensor", lambda e, h=h, Nc=Nc, Mc=Mc: e.matmul(bank(6, 64, h * 64, 64), lhsT=Mc[:, h, :], rhs=Nc[:, h, :],
                                                                        start=True, stop=True), r=[Nk, Mk], w=[BK[6]])
                P.op("scalar", lambda e, nx=nx: e.copy(out=Nb[nx].rearrange("p a b -> p (a b)"), in_=bank(6)), r=[BK[6]], w=["Nb%d" % nx])
                for h in range(8):
                    P.op("tensor", lambda e, h=h, nx=nx, Rc=Rc: e.matmul(bank(7, 64, h * 64, 64), lhsT=Nb[nx][:, h, :], rhs=Rc[:, h, :],
                                                                        start=True, stop=True), r=["Nb%d" % nx, Rk], w=[BK[7]])
                P.op("vector", lambda e, nx=nx, Rc=Rc: e.tensor_tensor(out=Rb[nx], in0=h3(bank(7), 8), in1=Rc, op=ALU.add),
                     r=[BK[7], Rk], w=["Rb%d" % nx])
                Mc, Nc, Rc = Mb[nx], Nb[nx], Rb[nx]
                Mk, Nk, Rk = "Mb%d" % nx, "Nb%d" % nx, "Rb%d" % nx
            TT, TTk = Rc, Rk
            P.op("vector", lambda e, n=n, s_=s_: e.tensor_tensor(out=Vb, in0=Vtm[s_], in1=beta_t[:, n, :].unsqueeze(2).to_broadcast([64, 8, 128]), op=ALU.mult),
                 r=["Vtm" + S_, "beta_t"], w=["Vb"])
            K8 = Ktm[s_].unsqueeze(2).to_broadcast([64, 4, 2, 128])
            P.op("vector", lambda e, n=n, K8=K8: e.tensor_tensor(out=Kbg.rearrange("p (m r) d -> p m r d", r=2), in0=K8,
                                                               in1=bge_t[:, n, :].rearrange("p (m r) -> p m r", r=2).unsqueeze(3).to_broadcast([64, 4, 2, 128]),
                                                               op=ALU.mult), r=["Ktm" + S_, "bge_t"], w=["Kbg"])
            P.op("vector", lambda e, n=n, K8=K8, s_=s_: e.tensor_tensor(out=kd[s_].rearrange("p (m r) d -> p m r d", r=2), in0=K8,
                                                                      in1=ekd_t[:, n, :].rearrange("p (m r) -> p m r", r=2).unsqueeze(3).to_broadcast([64, 4, 2, 128]),
                                                                      op=ALU.mult), r=["Ktm" + S_, "ekd_t"], w=["kd" + S_])
            for m in range(4):
                P.op("tensor", lambda e, m=m, kTm=kTm, qTm=qTm: e.matmul(bank(5, 64, m * 64, 64), lhsT=kTm(m), rhs=qTm(m), start=True, stop=True),
                     r=[dk_], w=[BK[5]])
            P.op("vector", lambda e, s_=s_: e.tensor_tensor(out=qkT[s_].rearrange("p (m r) d -> p m r d", r=2),
                                                            in0=h3(bank(5, 64, 0, 256), 4).unsqueeze(2).to_broadcast([64, 4, 2, 64]),
                                                            in1=DmT[s_].rearrange("p (m r) d -> p m r d", r=2), op=ALU.mult),
                 r=[BK[5], "DmT" + S_], w=["qkT" + S_])
            for hg, (h0, h1) in enumerate(HG):
                hs = slice(h0, h1)
                for h in range(h0, h1):
                    P.op("tensor", lambda e, h=h, h0=h0, TT=TT: e.matmul(bank(6, 64, (h - h0) * 128, 128), lhsT=TT[:, h, :], rhs=Vb[:, h, :],
                                                                        start=True, stop=True), r=[TTk, "Vb"], w=[BK[6]])
                P.op("scalar", lambda e, hs=hs, s_=s_: e.copy(out=u32[s_][:, hs, :].rearrange("p a b -> p (a b)"), in_=bank(6)), r=[BK[6]], w=["u32" + S_])
            for h in range(8):
                P.op("tensor", lambda e, h=h, TT=TT: e.matmul(bank(7, 128, h * 64, 64), lhsT=Kbg[:, h, :], rhs=TT[:, h, :],
                                                             start=True, stop=True), r=[TTk, "Kbg"], w=[BK[7]])
            P.op("scalar", lambda e, s_=s_: e.copy(out=wT[s_].rearrange("p a b -> p (a b)"), in_=bank(7, 128)), r=[BK[7]], w=["wT" + S_])

        def emit_B(n):
            s_ = n % 2
            S_ = str(s_)
            u, a0, qTm, kTm, vTh = chunk_views(n)
            dk_ = "dnin%d" % u
            for hg, (h0, h1) in enumerate(HG):
                hs = slice(h0, h1)
                for h in range(h0, h1):
                    P.op("tensor", lambda e, h=h, h0=h0, s_=s_: e.matmul(bank(0, 64, (h - h0) * 128, 128), lhsT=wT[s_][:, h, :], rhs=Sbf[:, h, :],
                                                                        start=True, stop=True), r=["wT" + S_, "Sbf"], w=[BK[0]])
                P.op("vector", lambda e, hs=hs, s_=s_: e.tensor_tensor(out=vnew[:, hs, :], in0=u32[s_][:, hs, :], in1=h3(bank(0), 4), op=ALU.subtract),
                     r=["u32" + S_, BK[0]], w=["vnew"])
                for h in range(h0, h1):
                    P.op("tensor", lambda e, h=h, h0=h0, qTm=qTm: e.matmul(bank(1, 64, (h - h0) * 128, 128), lhsT=qTm(h // 2), rhs=Sbf[:, h, :],
                                                                          start=True, stop=True), r=[dk_, "Sbf"], w=[BK[1]])
                for h in range(h0, h1):
                    P.op("tensor", lambda e, h=h, h0=h0, s_=s_: e.matmul(bank(0, 64, (h - h0) * 128, 128), lhsT=qkT[s_][:, h, :], rhs=vnew[:, h, :],
                                                                        start=True, stop=True), r=["qkT" + S_, "vnew"], w=[BK[0]])
                P.op("scalar", lambda e, hs=hs: e.copy(out=p3s[:, hs, :].rearrange("p a b -> p (a b)"), in_=bank(0)), r=[BK[0]], w=["p3s"])
                P.op("vector", lambda e, hs=hs, n=n: e.tensor_tensor(out=o32[:, hs, :], in0=h3(bank(1), 4),
                                                                     in1=eg_t[:, n, hs].unsqueeze(2).to_broadcast([64, 4, 128]), op=ALU.mult),
                     r=[BK[1], "eg_t"], w=["o32"])
                P.op("vector", lambda e, hs=hs: e.tensor_tensor(out=o32[:, hs, :], in0=o32[:, hs, :], in1=p3s[:, hs, :], op=ALU.add),
                     r=["o32", "p3s"], w=["o32"])
                for h in range(h0, h1):
                    P.op("tensor", lambda e, h=h, h0=h0, s_=s_: e.matmul(bank(2, 128, (h - h0) * 128, 128), lhsT=kd[s_][:, h, :], rhs=vnew[:, h, :],
                                                                        start=True, stop=True), r=["kd" + S_, "vnew"], w=[BK[2]])
                P.op("vector", lambda e, hs=hs, n=n: e.tensor_tensor(out=S32[:, hs, :], in0=S32[:, hs, :],
                                                                     in1=egl_t[:, n, hs].unsqueeze(2).to_broadcast([128, 4, 128]), op=ALU.mult),
                     r=["S32", "egl_t"], w=["S32"])
                P.op("vector", lambda e, hs=hs: e.tensor_tensor(out=S32[:, hs, :], in0=S32[:, hs, :], in1=h3(bank(2, 128), 4), op=ALU.add),
                     r=["S32", BK[2]], w=["S32"])
                P.op("scalar", lambda e, hs=hs: e.copy(out=Sbf[:, hs, :], in_=S32[:, hs, :]), r=["S32"], w=["Sbf"])
                P.op("scalar", lambda e, hs=hs: e.activation(out=p3s[:, hs, :], in_=o32[:, hs, :], func=AF.Square),
                     r=["o32"], w=["p3s"])
                P.op("vector", lambda e, hs=hs: e.tensor_reduce(out=ssq[:, hs], in_=p3s[:, hs, :], axis=AX.X, op=ALU.add), r=["p3s"], w=["ssq"])
                P.op("scalar", lambda e, hs=hs: e.activation(out=ssq[:, hs], in_=ssq[:, hs], func=AF.Sqrt, bias=EPS, scale=1.0 / 128), r=["ssq"], w=["ssq"])
                P.op("vector", lambda e, hs=hs: e.reciprocal(out=ssq[:, hs], in_=ssq[:, hs]), r=["ssq"], w=["ssq"])
                P.op("vector", lambda e, hs=hs: e.tensor_tensor(out=onb[:, hs, :], in0=o32[:, hs, :],
                                                                in1=ssq[:, hs].unsqueeze(2).to_broadcast([64, 4, 128]), op=ALU.mult),
                     r=["o32", "ssq"], w=["onb"])
                for h in range(h0, h1):
                    P.op("tensor", lambda e, h=h, h0=h0: e.matmul(bank(1, 128, (h - h0) * 64, 64), lhsT=onb[:, h, :], rhs=cst_bf[0:64, 0:64],
                                                                 start=True, stop=True), r=["onb", "cst_bf"], w=[BK[1]])
                P.op("vector", lambda e, hs=hs, u=u, a0=a0: e.scalar_tensor_tensor(
                    out=ogs[u][:, hs, a0:a0 + 64], in0=h3(bank(1, 128, 0, 256), 4), scalar=sm_s[:, 6:7], in1=zin[u][:, hs, a0:a0 + 64],
                    op0=ALU.mult, op1=ALU.mult), r=[BK[1], "sm_s", "zin%d" % u], w=["ogs0"])
            if n % 8 == 7:
                c0 = (n // 8) * SB
                P.dma("sync", lambda e, u=u, c0=c0: e.dma_start(out=g_src3[:, 4:12, c0:c0 + SB], in_=ogs[u][:, :, :]),
                      r=["ogs0"], w=["g_src_o"], sem="ogs0")

        def record(fn, n):
            if n < 0 or n >= 64:
                return []
            keep = P.ins
            P.ins = []
            fn(n)
            out = P.ins
            P.ins = keep
            return out

        def merge(lists):
            lists = [l for l in lists if l]
            if not lists:
                return []
            tot = max(len(l) for l in lists)
            pos = [0] * len(lists)
            out = []
            for step in range(1, tot + 1):
                for i, l in enumerate(lists):
                    tgt = (len(l) * step + tot - 1) // tot
                    while pos[i] < tgt:
                        out.append(l[pos[i]])
                        pos[i] += 1
            return out

        NCH = T // 64
        for r_ in range(-2, NCH):
            P.ins.extend(merge([record(emit_B, r_), record(emit_A2, r_ + 1), record(emit_A1, r_ + 2)]))

        if upto == "dn":
            raise _Stop()
        for k in range(4, 12):
            P.dma("gpsimd", lambda e, k=k: e.collective_compute("AllGather", ALU.bypass, replica_groups=RG,
                                                                ins=[g_src[k * 128:(k + 1) * 128, :].opt()],
                                                                outs=[g_dst[k * 512:(k + 1) * 512, :].opt()]),
                  r=["g_src_o"], w=["g_dst%d" % k], sem="cc1_%d" % k, inc=1)

        if upto == "g1":
            raise _Stop()
        P.barrier(lambda e: e.memset(bar_t[:, 2:3], 0.0))
        bigw[0] = 0
        gin = carve(48 * SB, BF16).rearrange("p (a b) -> p a b", a=48)
        wos_s = carve(4 * 16 * 128, BF16).rearrange("p (c k n) -> p c k n", c=4, k=16)
        wod_s = carve(4 * 32 * 128, BF16).rearrange("p (c k n) -> p c k n", c=4, k=32)
        sas = [carve(4 * SB, BF16).rearrange("p (a b) -> p a b", a=4) for _ in range(2)]
        sbs = [carve(4 * SB, BF16).rearrange("p (a b) -> p a b", a=4) for _ in range(2)]
        yst = carve(4 * SB, BF16).rearrange("p (a b) -> p a b", a=4)
        t1 = carve(SB)
        t2 = carve(SB)
        for c in range(4):
            P.dma("gpsimd", lambda e, c=c: e.dma_start(out=wos_s[:, c, :, :], in_=wos[c]), w=["wos_s"], sem="wos")
            P.dma("gpsimd", lambda e, c=c: e.dma_start(out=wod_s[:, c, :, :], in_=wod[c]), w=["wod_s"], sem="wod")
        g_dst3 = g_dst.rearrange("(k p) t -> p k t", p=128)
        y_src3 = y_src.rearrange("(k p) t -> p k t", p=128)

        def c2_loads(tb):
            c0 = tb * SB
            u = tb % 2
            for r4 in range(4):
                P.dma("sync", lambda e, c0=c0, r4=r4: e.dma_start(out=gin[:, r4 * 12:r4 * 12 + 12, :], in_=g_dst3[:, r4 * 12:r4 * 12 + 12, c0:c0 + SB]),
                      r=["g_dst%d" % k for k in range(12)], w=["gin_g%d" % r4], sem="gin%d" % r4)
            P.dma("sync", lambda e, c0=c0, u=u: e.dma_start(out=sas[u], in_=s_sa.rearrange("k p t -> p k t")[:, :, c0:c0 + SB]),
                  r=["s_sa%d" % i for i in range(4)], w=["sas%d" % u], sem="sas%d" % u)
            P.dma("sync", lambda e, c0=c0, u=u: e.dma_start(out=sbs[u], in_=s_sb.rearrange("k p t -> p k t")[:, :, c0:c0 + SB]),
                  r=["s_sb%d" % i for i in range(4)], w=["sbs%d" % u], sem="sbs%d" % u)

        accA = [psA[0], psA[1], psA[2], psA[3]]
        accB = [psB[0], psB[1], psC[0], psC[1]]
        kA = ["psA0", "psA1", "psA2", "psA3"]
        kB = ["psB0", "psB1", "psC0", "psC1"]
        items = []
        for kt in range(16):
            items.append(((kt % 4) * 4 + kt // 4, kt, False))
        for kt in range(32):
            items.append(((4 + kt % 8) * 4 + kt // 8, kt, True))
        items.sort()
        firstA, lastA = min(i for i, it in enumerate(items) if not it[2]), max(i for i, it in enumerate(items) if not it[2])
        firstB, lastB = min(i for i, it in enumerate(items) if it[2]), max(i for i, it in enumerate(items) if it[2])
        c2_loads(0)
        for tb in range(T // SB):
            c0 = tb * SB
            u = tb % 2
            for i_, (gt, kt, is_o) in enumerate(items):
                gk = "gin_g%d" % (gt // 12)
                for c in range(4):
                    if not is_o:
                        P.op("tensor", lambda e, c=c, kt=kt, gt=gt, st=(i_ == firstA), sp=(i_ == lastA): e.matmul(
                            accA[c][:, :], lhsT=wos_s[:, c, kt, :], rhs=gin[:, gt, :], start=st, stop=sp), r=["wos_s", gk], w=[kA[c]])
                    else:
                        P.op("tensor", lambda e, c=c, kt=kt, gt=gt, st=(i_ == firstB), sp=(i_ == lastB): e.matmul(
                            accB[c][:, :], lhsT=wod_s[:, c, kt, :], rhs=gin[:, gt, :], start=st, stop=sp), r=["wod_s", gk], w=[kB[c]])
            if tb + 1 < T // SB:
                c2_loads(tb + 1)
            for c in range(4):
                P.op("vector", lambda e, c=c, u=u: e.tensor_tensor(out=t1, in0=accA[c][:, :], in1=sas[u][:, c, :], op=ALU.mult), r=[kA[c], "sas%d" % u], w=["t1"])
                P.op("vector", lambda e, c=c, u=u: e.tensor_tensor(out=t2, in0=accB[c][:, :], in1=sbs[u][:, c, :], op=ALU.mult), r=[kB[c], "sbs%d" % u], w=["t2"])
                P.op("vector", lambda e, c=c: e.tensor_tensor(out=yst[:, c, :], in0=t1, in1=t2, op=ALU.add), r=["t1", "t2"], w=["yst"])
            P.dma("sync", lambda e, c0=c0: e.dma_start(out=y_src3[:, :, c0:c0 + SB], in_=yst), r=["yst"], w=["y_src"], sem="yst")
        for k in range(4):
            P.dma("gpsimd", lambda e, k=k: e.collective_compute("AllGather", ALU.bypass, replica_groups=RG,
                                                                ins=[y_src[k * 128:(k + 1) * 128, :].opt()],
                                                                outs=[y_dst[k * 512:(k + 1) * 512, :].opt()]),
                  r=["y_src"], w=["y_dst%d" % k], sem="cc2_%d" % k, inc=1)

        if upto == "c2":
            raise _Stop()
        P.barrier(lambda e: e.memset(bar_t[:, 3:4], 0.0))
        bigw[0] = 0
        yin = [carve(16 * SB, BF16).rearrange("p (a b) -> p a b", a=16) for _ in range(2)]
        wout_s = carve(4 * 16 * 128, BF16).rearrange("p (c k n) -> p c k n", c=4, k=16)
        xr = [carve(4 * SB).rearrange("p (a b) -> p a b", a=4) for _ in range(2)]
        ost = [carve(4 * SB).rearrange("p (a b) -> p a b", a=4) for _ in range(2)]
        for c in range(4):
            P.dma("gpsimd", lambda e, c=c: e.dma_start(out=wout_s[:, c, :, :], in_=wout[c]), w=["wout_s"], sem="wout")
        y_dst3 = y_dst.rearrange("(k p) t -> p k t", p=128)
        outT3 = outT.rearrange("(k p) t -> p k t", p=128)
        def c3_loads(tb):
            c0 = tb * SB
            u = tb % 2
            P.dma("sync", lambda e, c0=c0, u=u: e.dma_start(out=yin[u], in_=y_dst3[:, :, c0:c0 + SB]), r=["y_dst%d" % k for k in range(4)], w=["yin%d" % u], sem="yin%d" % u)
            P.dma("sync", lambda e, c0=c0, u=u: e.dma_start(out=xr[u], in_=xres.rearrange("k p t -> p k t")[:, :, c0:c0 + SB]), w=["xr%d" % u], sem="xr%d" % u)

        c3_loads(0)
        for tb in range(T // SB):
            c0 = tb * SB
            u = tb % 2
            if tb + 1 < T // SB:
                c3_loads(tb + 1)
            for c in range(4):
                pa_ = psA[c]
                ka_ = "psA%d" % c
                for kt in range(16):
                    P.op("tensor", lambda e, c=c, kt=kt, pa_=pa_, u=u: e.matmul(pa_[:, :], lhsT=wout_s[:, c, kt, :], rhs=yin[u][:, (kt % 4) * 4 + kt // 4, :],
                                                                               start=(kt == 0), stop=(kt == 15)), r=["wout_s", "yin%d" % u], w=[ka_])
                P.op("vector", lambda e, c=c, pa_=pa_, u=u: e.scalar_tensor_tensor(out=ost[u][:, c, :], in0=pa_[:, :], scalar=mod[:, 32 + c:33 + c],
                                                                                   in1=xr[u][:, c, :], op0=ALU.mult, op1=ALU.add),
                     r=[ka_, "mod", "xr%d" % u], w=["ost%d" % u])
            P.dma("sync", lambda e, c0=c0, u=u: e.dma_start(out=outT3[:, :, c0:c0 + SB], in_=ost[u]), r=["ost%d" % u], w=["outT"], sem="ost%d" % u)
    except _Stop:
        pass

    allw = set()
    for I in P.ins:
        if I["dma"] is not None:
            allw.update(I["w"])
    endt = sb("endt", [128, 1])
    P.barrier(lambda e: e.memset(endt[:, :], 0.0))
    P.op("sync", lambda e: e.engine_nop() if hasattr(e, "engine_nop") else None, r=sorted(allw), w=["__end"])
    P.finalize()
    P.emit(nc, es)
    es.close()
    return nc


def tile_w(w):
    K, N = w.shape
    assert N % 128 == 0
    return np.ascontiguousarray(w.reshape(K // 128, 128, N // 128, 128).transpose(2, 1, 0, 3))


def prep_inputs(inp, b, j):
    x = inp["x"][b]
    w_in = inp["w_in"][0]
    m = {}
    xT_ = np.ascontiguousarray(x.T)
    m["xT"] = np.ascontiguousarray(xT_.reshape(16, 128, T // 128, 128).transpose(2, 1, 0, 3))
    m["xres"] = np.ascontiguousarray(xT_[512 * j:512 * j + 512].reshape(4, 128, T))
    m["cT"] = np.ascontiguousarray(inp["c"][b].reshape(16, 128).T)
    m["pos"] = np.ascontiguousarray(inp["positions"][b].reshape(1, T).astype(np.int32))
    w_ada = inp["w_ada"][0]
    cols = np.concatenate([np.arange(0, 4096), 4096 + 512 * j + np.arange(512)])
    m["wada"] = tile_w(w_ada[:, cols])
    m["bada"] = np.ascontiguousarray(inp["b_ada"][0][cols].reshape(36, 128).T)
    m["nw"] = np.ascontiguousarray(inp["norm_w"][0].reshape(16, 128).T)
    o_aq, o_ak, o_av, o_ag, o_dn, o_dz, o_db, o_da, o_ma, o_mb = np.cumsum([0, 2048, 256, 256, 2048, 8192, 4096, 32, 32, 2048])
    r = np.arange
    dnq = o_dn + 512 * j + r(512)
    dnk = o_dn + 2048 + 512 * j + r(512)
    dnv = o_dn + 4096 + 1024 * j + r(1024)
    dz = o_dz + 1024 * j + r(1024)
    aq = o_aq + 512 * j + r(512)
    kk = np.concatenate([o_ak + 64 * j + r(64), o_ak + 64 * j + r(64)])
    vv = np.concatenate([o_av + 64 * j + r(64), o_av + 64 * j + r(64)])
    ag = o_ag + 512 * j + r(512)
    ma = o_ma + 512 * j + r(512)
    mb = o_mb + 512 * j + r(512)
    bg = np.concatenate([o_db + 8 * j + r(8), o_da + 8 * j + r(8)])
    cols = np.concatenate([dnq, dnk, dnv, dz, aq, kk, vv, ag, ma, mb])
    wbg = np.zeros((2048, 128), np.float32)
    wbg[:, :16] = w_in[:, bg]
    m["wa"] = tile_w(np.concatenate([w_in[:, cols], wbg], axis=1))
    cwc = inp["conv_w"][0][:, np.concatenate([dnq, dnk, dnv]) - o_dn]
    m["cw"] = np.ascontiguousarray(cwc.reshape(4, 16, 128).transpose(2, 1, 0))
    cs = slice(512 * j, 512 * j + 512)
    m["wos"] = tile_w(inp["w_o_swa"][0][:, cs])
    wod_ = inp["w_o_dn"][0][:, cs]
    m["wod"] = np.ascontiguousarray(wod_.reshape(32, 128, 4, 128).transpose(2, 1, 0, 3))
    m["wout"] = tile_w(inp["w_out"][0][:, cs])
    cst = np.zeros((128, 1536), np.float32)
    cst[:, 0:128] = np.eye(128)
    for q in range(2):
        cst[64 * q:64 * q + 64, 128 + 64 * q:128 + 64 * q + 64] = 1.0 / 64
    for q in range(2):
        for i in range(8):
            cst[64 * q + i + 8, 256 + 64 * q + i] = -1.0
            cst[64 * q + i, 256 + 64 * q + i + 8] = 1.0
    ii = np.arange(64)
    cst[0:64, 384:448] = (ii[:, None] <= ii[None, :])
    cst[63, 448:576] = 1.0
    cst[0:64, 640:704] = (ii[:, None] < ii[None, :])
    kq = np.arange(128)
    cst[:, 704:832] = (kq[None, :] < kq[:, None])
    cst[:, 832:960] = (kq[None, :] >= kq[:, None])
    invf = 500000.0 ** (-np.arange(8, dtype=np.float32) * (2.0 / 16))
    for q in range(2):
        cst[64 * q:64 * q + 8, 960] = invf
        cst[64 * q + 8:64 * q + 16, 960] = invf
    cst[:, 1024:1088] = 1.0
    cst[:, 1152 + 64:1280] = 1.0
    cst[0:64, 961] = 1.0
    cst[64:128, 962] = 1.0
    for k in range(64):
        cst[k, 1280 + k] = 1.0
        cst[k, 1408 + 64 + k] = 1.0
    m["cst"] = cst
    sm = np.zeros((128, 32), np.float32)
    sm[:, 0] = np.tile(inp["q_norm_w"][0], 2)
    sm[:, 1] = np.tile(inp["k_norm_w"][0], 2)
    sk = inp["sinks"][0][8 * j:8 * j + 8]
    for t in range(4):
        sm[0:64, 2 + t] = sk[2 * t]
        sm[64:128, 2 + t] = sk[2 * t + 1]
    sm[:, 6] = inp["dn_norm_w"][0]
    sm[:, 8:16] = inp["a_log"][0][8 * j:8 * j + 8][None, :]
    sm[:, 16:24] = inp["dt_bias"][0][8 * j:8 * j + 8][None, :]
    m["sm"] = sm
    return m


_NC_CACHE = {}


def kernel(**inp):
    inp = {k: np.asarray(v) for k, v in inp.items()}
    if "nc" not in _NC_CACHE:
        _NC_CACHE["nc"] = build()
    nc = _NC_CACHE["nc"]
    in_maps = [prep_inputs(inp, c // 4, c % 4) for c in range(8)]
    res = run_bass_kernel_spmd(nc, in_maps, core_ids=list(range(8)))
    out = np.zeros((2, T, D), np.float32)
    for c in range(8):
        b, j = c // 4, c % 4
        out[b][:, 512 * j:512 * j + 512] = np.asarray(res.results[c]["outT"]).T
    return out
```

```python
import os
import numpy as np
from contextlib import ExitStack
import concourse.bass as bass
import concourse.mybir as mybir
from concourse.bass_utils import run_bass_kernel_spmd

F32 = mybir.dt.float32
BF16 = mybir.dt.bfloat16
I32 = mybir.dt.int32
AF = mybir.ActivationFunctionType
ALU = mybir.AluOpType
AX = mybir.AxisListType

D = 2048
T = 4096
EPS = 1e-6
NCT = 43
SAME_ENGINE_SYNC = True


class Prog:
    def __init__(self):
        self.ins = []
        self.keys = set()
        self.ep = ()

    def op(self, eng, fn, r=(), w=()):
        self.keys.update(r); self.keys.update(w)
        self.ins.append(dict(eng=eng, fn=fn, r=tuple(r) + self.ep, w=tuple(w), dma=None))

    def dma(self, eng, fn, r=(), w=(), sem=None, inc=16):
        assert sem is not None
        self.keys.update(r); self.keys.update(w)
        self.ins.append(dict(eng=eng, fn=fn, r=tuple(r) + self.ep, w=tuple(w), dma=sem, inc=inc))

    def barrier(self, fn, exclude=()):
        ks = [k for k in sorted(self.keys, key=str) if k not in exclude]
        self.ins.append(dict(eng="gpsimd", fn=fn, r=(), w=tuple(ks) + ("__epoch",), dma=None))
        self.ep = ("__epoch",)

    def finalize(self):
        ins = self.ins
        last_w, readers, last_dma = {}, {}, {}
        for idx, I in enumerate(ins):
            deps = set()
            for k in I["r"]:
                if k in last_w:
                    deps.add(last_w[k])
            for k in I["w"]:
                if k in last_w:
                    deps.add(last_w[k])
                deps.update(readers.get(k, ()))
            if I["dma"] is not None:
                if I["dma"] in last_dma:
                    deps.add(last_dma[I["dma"]])
                last_dma[I["dma"]] = idx
            deps.discard(idx)
            I["deps"] = deps
            for k in I["r"]:
                readers.setdefault(k, []).append(idx)
            for k in I["w"]:
                last_w[k] = idx
                readers[k] = []
        need = [False] * len(ins)
        for idx, I in enumerate(ins):
            for d in I["deps"]:
                P = ins[d]
                if P["dma"] is not None:
                    continue
                if P["eng"] != I["eng"]:
                    need[d] = True
                elif SAME_ENGINE_SYNC and I["eng"] != "tensor":
                    need[d] = True
        cnt, dcnt = {}, {}
        for idx, I in enumerate(ins):
            if I["dma"] is not None:
                dcnt[I["dma"]] = dcnt.get(I["dma"], 0) + I["inc"]
                I["sig"] = (("dma", I["dma"]), dcnt[I["dma"]], I["inc"])
            elif need[idx]:
                cnt[I["eng"]] = cnt.get(I["eng"], 0) + 1
                I["sig"] = (("eng", I["eng"]), cnt[I["eng"]], 1)
            else:
                I["sig"] = None
        known = {}
        for idx, I in enumerate(ins):
            waits = {}
            for d in I["deps"]:
                P = ins[d]
                if P["sig"] is None:
                    continue
                if P["dma"] is None and P["eng"] == I["eng"] and (I["eng"] == "tensor" or not SAME_ENGINE_SYNC):
                    continue
                s, v, _ = P["sig"]
                waits[s] = max(waits.get(s, 0), v)
            kn = known.setdefault(I["eng"], {})
            out = []
            for s, v in waits.items():
                if kn.get(s, 0) >= v:
                    continue
                kn[s] = v
                out.append((s, v))
            I["waits"] = out
        self.dma_keys = sorted(dcnt.keys(), key=str)
        return self

    def emit(self, nc, es):
        engs = ["tensor", "vector", "scalar", "gpsimd", "sync"]
        sems = {}
        for e in engs:
            sems[("eng", e)] = es.enter_context(nc.semaphore("se_" + e))
        for i, k in enumerate(self.dma_keys):
            sems[("dma", k)] = es.enter_context(nc.semaphore("sd_%d" % i))
        per = {e: [I for I in self.ins if I["eng"] == e] for e in engs}
        block = es.enter_context(nc.Block())

        def make(name):
            def body(e):
                for I in per[name]:
                    for s, v in I["waits"]:
                        e.wait_ge(sems[s], v)
                    bi = I["fn"](e)
                    if I["sig"] is not None:
                        s, v, inc = I["sig"]
                        bi.then_inc(sems[s], inc)
            return body

        block.tensor(make("tensor"))
        block.vector(make("vector"))
        block.scalar(make("scalar"))
        block.gpsimd(make("gpsimd"))
        block.sync(make("sync"))


class _Stop(Exception):
    pass


def build(debug=False, stop_after=99, upto=None):
    nc = bass.Bass("TRN2", target_bir_lowering=False)
    P = Prog()
    es = ExitStack()

    def din(name, shape, dt=F32):
        return nc.dram_tensor(name, list(shape), dt, kind="ExternalInput").ap()

    def dscr(name, shape, dt=BF16):
        if debug:
            return nc.dram_tensor(name, list(shape), dt, kind="ExternalOutput").ap()
        return nc.dram_tensor(name, list(shape), dt).ap()

    def sb(name, shape, dt=F32):
        return es.enter_context(nc.sbuf_tensor(name, list(shape), dt))

    def ps(name, shape, dt=F32):
        return es.enter_context(nc.psum_tensor(name, list(shape), dt))

    xT = din("xT", [T // 128, 128, 16, 128])
    cT = din("cT", [128, 16])
    wada = din("wada", [36, 128, 16, 128])
    bada = din("bada", [128, 36])
    nw = din("nw", [128, 16])
    wa = din("wa", [NCT, 128, 16, 128])
    cw = din("cw", [128, 16, 4])
    s_dn = dscr("s_dn", [16, 128, T])
    s_z = dscr("s_z", [8, 128, T])
    s_hT = dscr("s_hT", [16, 128, T]) if debug else None

    ones_bf = sb("ones_bf", [128, 128], BF16)
    big = sb("big", [128, 32768])
    hT = big[:, :].bitcast(BF16).rearrange("p (a b) -> p a b", a=16)
    mod = sb("mod", [128, 36])
    gam = sb("gam", [128, 16])
    cT_s = sb("cT_s", [128, 16])
    sc_s = sb("sc_s", [128, 16])
    bada_s = sb("bada_s", [128, 36])
    nw_s = sb("nw_s", [128, 16])
    cw_s = sb("cw_s", [128, 16, 4])
    P.op("gpsimd", lambda e: e.memset(ones_bf[:, :], 1.0), w=["ones_bf"])
    onesD_bf = sb("onesD_bf", [128, 128], BF16)
    P.op("gpsimd", lambda e: e.memset(onesD_bf[:, :], 1.0 / D), w=["onesD_bf"])
    P.dma("sync", lambda e: e.dma_start(out=cT_s[:, :], in_=cT[:, :]), w=["cT_s"], sem="small0")
    P.dma("sync", lambda e: e.dma_start(out=bada_s[:, :], in_=bada[:, :]), w=["bada_s"], sem="small1")
    P.dma("sync", lambda e: e.dma_start(out=nw_s[:, :], in_=nw[:, :]), w=["nw_s"], sem="small2")
    P.dma("sync", lambda e: e.dma_start(out=cw_s[:, :, :], in_=cw[:, :, :]), w=["cw_s"], sem="small3")

    psA = [ps("psA%d" % i, [128, 512]) for i in range(4)]
    psB = [ps("psB%d" % i, [128, 512]) for i in range(2)]
    psC = [ps("psC%d" % i, [128, 512]) for i in range(2)]

    P.op("scalar", lambda e: e.activation(out=sc_s[:, :], in_=cT_s[:, :], func=AF.Silu), r=["cT_s"], w=["sc_s"])
    wad = [sb("wad%d" % i, [128, 16, 128]) for i in range(2)]
    for t in range(36):
        buf = wad[t % 2]
        key = "wad%d" % (t % 2)
        P.dma("sync", lambda e, buf=buf, t=t: e.dma_start(out=buf[:, :, :], in_=wada[t]), w=[key], sem=key)
        for kt in range(16):
            P.op("tensor", lambda e, buf=buf, t=t, kt=kt: e.matmul(
                psA[0][:, t:t + 1], lhsT=buf[:, kt, :], rhs=sc_s[:, kt:kt + 1], start=(kt == 0), stop=(kt == 15)),
                r=[key, "sc_s"], w=["psA0"])
    P.op("vector", lambda e: e.tensor_tensor(out=mod[:, :], in0=psA[0][:, 0:36], in1=bada_s[:, :], op=ALU.add),
         r=["psA0", "bada_s"], w=["mod"])
    P.op("vector", lambda e: e.scalar_tensor_tensor(out=gam[:, :], in0=mod[:, 16:32], scalar=1.0, in1=nw_s[:, :],
                                                     op0=ALU.add, op1=ALU.mult), r=["mod", "nw_s"], w=["gam"])

    TB1 = 128
    xb = [sb("xb%d" % i, [128, 16, TB1]) for i in range(2)]
    sq = sb("sq", [128, 16, TB1], BF16)
    rstd = sb("rstd", [128, TB1])
    for tb in range(T // TB1):
        xbuf = xb[tb % 2]
        xk = "xb%d" % (tb % 2)
        t0 = tb * TB1
        P.dma("sync", lambda e, xbuf=xbuf, t0=t0: e.dma_start(out=xbuf[:, :, :], in_=xT[t0 // TB1]),
              w=[xk], sem=xk)
        P.op("scalar", lambda e, xbuf=xbuf: e.activation(out=sq[:, :, :], in_=xbuf[:, :, :], func=AF.Square),
             r=[xk], w=["sq"])
        pb = psB[tb % 2]
        pk = "psB%d" % (tb % 2)
        for dt in range(16):
            P.op("tensor", lambda e, pb=pb, dt=dt: e.matmul(pb[:, 0:TB1], lhsT=onesD_bf[:, :], rhs=sq[:, dt, :],
                                                           start=(dt == 0), stop=(dt == 15)),
                 r=["sq", "onesD_bf"], w=[pk])
        P.op("scalar", lambda e, pb=pb: e.activation(out=rstd[:, :], in_=pb[:, 0:TB1], func=AF.Sqrt, bias=EPS, scale=1.0),
             r=[pk], w=["rstd"])
        P.op("vector", lambda e: e.reciprocal(out=rstd[:, :], in_=rstd[:, :]), r=["rstd"], w=["rstd"])
        P.op("vector", lambda e, xbuf=xbuf: e.tensor_tensor(
            out=xbuf[:, :, :], in0=xbuf[:, :, :], in1=rstd[:, :].unsqueeze(1).to_broadcast([128, 16, TB1]), op=ALU.mult),
            r=[xk, "rstd"], w=[xk])
        for dt in range(16):
            P.op("scalar", lambda e, t0=t0, xbuf=xbuf, dt=dt: e.activation(
                out=hT[:, dt, t0:t0 + TB1], in_=xbuf[:, dt, :], func=AF.Identity, bias=mod[:, dt:dt + 1], scale=gam[:, dt:dt + 1]),
                r=[xk, "mod", "gam"], w=["hT%d_%d" % (tb, dt)])
    hT_keys = ["hT%d_%d" % (tb, dt) for tb in range(T // TB1) for dt in range(16)]
    if debug:
        P.dma("sync", lambda e: e.dma_start(out=s_hT.rearrange("k p t -> p k t"), in_=hT[:, :, :]),
              r=hT_keys, w=["s_hT"], sem="dbg")

    pos = din("pos", [1, T], I32)
    cst = din("cst", [128, 1536])
    sm = din("sm", [128, 32])
    wos = din("wos", [4, 128, 16, 128])
    wod = din("wod", [4, 128, 32, 128])
    wout = din("wout", [4, 128, 16, 128])
    xres = din("xres", [4, 128, T])
    outT = nc.dram_tensor("outT", [512, T], F32, kind="ExternalOutput").ap()
    s_aq = dscr("s_aq", [4, 128, T])
    s_kk = dscr("s_kk", [128, T])
    s_vv = dscr("s_vv", [128, T])
    s_ag = dscr("s_ag", [4, 128, T])
    s_sa = dscr("s_sa", [4, 128, T])
    s_sb = dscr("s_sb", [4, 128, T])
    g_src = dscr("g_src", [12 * 128, T])
    g_dst = nc.dram_tensor("g_dst", [48 * 128, T], BF16).ap()
    y_src = dscr("y_src", [4 * 128, T])
    y_dst = nc.dram_tensor("y_dst", [16 * 128, T], BF16).ap()
    cst_s = sb("cst_s", [128, 1536])
    cst_bf = sb("cst_bf", [128, 1536], BF16)
    sm_s = sb("sm_s", [128, 32])
    P.dma("sync", lambda e: e.dma_start(out=cst_s[:, :], in_=cst[:, :]), w=["cst_s"], sem="small0")
    P.dma("sync", lambda e: e.dma_start(out=sm_s[:, :], in_=sm[:, :]), w=["sm_s"], sem="small1")
    P.op("vector", lambda e: e.tensor_copy(out=cst_bf[:, :], in_=cst_s[:, :]), r=["cst_s"], w=["cst_bf"])
    I_bf = cst_bf[:, 0:128]
    blk64_bf = cst_bf[:, 128:256]
    Pm_bf = cst_bf[:, 256:384]
    tri64 = cst_s[0:64, 384:448]
    sel63 = cst_s[0:64, 448:576]
    mU = cst_s[0:64, 384:448]
    mUs = cst_s[0:64, 640:704]
    maskCP = cst_bf[:, 704:960]
    invf = cst_s[:, 960:961]
    onesE_bf = cst_bf[:, 1024:1152]
    onesO_bf = cst_bf[:, 1152:1280]
    I64f = cst_s[0:64, 0:64]
    PI = float(np.pi)

    try:
        wb = [sb("wb%d" % i, [128, 16, 128], BF16) for i in range(2)]
        QT = 1024
        Y = [sb("Y%d" % i, [128, 3 + QT]) for i in range(2)]
        acc = sb("acc", [128, QT])
        sil = sb("sil", [128, QT])
        sqb = sb("sqb", [128, QT], BF16)
        rs2 = sb("rs2", [128, QT])
        sqf = sq[:, :, :].rearrange("p a b -> p (a b)")
        ob = [sqf[:, i * QT:(i + 1) * QT] for i in range(2)]
        Cf = wad[0][:, :, :].rearrange("p a b -> p (a b)").bitcast(BF16)
        Sf = wad[1][:, :, :].rearrange("p a b -> p (a b)").bitcast(BF16)
        rs2i = rs2[:, :].bitcast(I32)
        for qi in range(4):
            q0 = qi * QT
            P.dma("sync", lambda e, q0=q0: e.dma_start(out=rs2i, in_=pos[0:1, q0:q0 + QT].partition_broadcast(128)),
                  w=["rs2"], sem="posld")
            P.op("vector", lambda e: e.tensor_copy(out=acc[:, :], in_=rs2i), r=["rs2"], w=["acc"])
            P.op("vector", lambda e: e.tensor_scalar(out=acc[:, :], in0=acc[:, :], scalar1=invf, scalar2=None, op0=ALU.mult),
                 r=["acc", "cst_s"], w=["acc"])
            for tab, tk, offs in ((Sf, "wad1", 0.0), (Cf, "wad0", PI / 2)):
                if offs != 0.0:
                    P.op("vector", lambda e, offs=offs: e.tensor_scalar(out=acc[:, :], in0=acc[:, :], scalar1=offs, scalar2=None,
                                                                        op0=ALU.add), r=["acc"], w=["acc"])
                P.op("vector", lambda e: e.tensor_scalar(out=sil[:, :], in0=acc[:, :], scalar1=1.0 / (2 * PI), scalar2=None,
                                                         op0=ALU.mult), r=["acc"], w=["sil"])
                P.op("vector", lambda e: e.tensor_copy(out=rs2i, in_=sil[:, :]), r=["sil"], w=["rs2"])
                P.op("vector", lambda e: e.tensor_copy(out=sil[:, :], in_=rs2i), r=["rs2"], w=["sil"])
                P.op("vector", lambda e: e.scalar_tensor_tensor(out=sil[:, :], in0=sil[:, :], scalar=-2 * PI, in1=acc[:, :],
                                                                op0=ALU.mult, op1=ALU.add), r=["sil", "acc"], w=["sil"])
                P.op("vector", lambda e: e.tensor_scalar(out=rs2[:, :], in0=sil[:, :], scalar1=PI, scalar2=None, op0=ALU.is_gt),
                     r=["sil"], w=["rs2"])
                P.op("vector", lambda e: e.scalar_tensor_tensor(out=sil[:, :], in0=rs2[:, :], scalar=-2 * PI, in1=sil[:, :],
                                                                op0=ALU.mult, op1=ALU.add), r=["sil", "rs2"], w=["sil"])
                P.op("vector", lambda e: e.tensor_scalar(out=rs2[:, :], in0=sil[:, :], scalar1=-PI, scalar2=None, op0=ALU.is_lt),
                     r=["sil"], w=["rs2"])
                P.op("vector", lambda e: e.scalar_tensor_tensor(out=sil[:, :], in0=rs2[:, :], scalar=2 * PI, in1=sil[:, :],
                                                                op0=ALU.mult, op1=ALU.add), r=["sil", "rs2"], w=["sil"])
                P.op("scalar", lambda e, tab=tab, q0=q0: e.activation(out=tab[:, q0:q0 + QT], in_=sil[:, :], func=AF.Sin),
                     r=["sil"], w=[tk])

        if upto == "tab":
            raise _Stop()
        yc = 0
        oc = 0
        pa = 0
        n_ct = min(NCT - 1, stop_after)
        P.dma("gpsimd", lambda e: e.dma_start(out=wb[0][:, :, :], in_=wa[0]), w=["wb0"], sem="wb0")
        for ct in range(n_ct):
            wbuf = wb[ct % 2]
            wk = "wb%d" % (ct % 2)
            if ct + 1 < NCT:
                P.dma("gpsimd", lambda e, ct=ct: e.dma_start(out=wb[(ct + 1) % 2][:, :, :], in_=wa[ct + 1]),
                      w=["wb%d" % ((ct + 1) % 2)], sem="wb%d" % ((ct + 1) % 2))
            if ct < 16:
                kind = "conv"
            elif ct < 24:
                kind, fn, dst, dkey = "act", AF.Silu, s_z[ct - 16], "s_z%d" % (ct - 16)
            elif ct < 28:
                kind, dst, dkey = "rope", s_aq[ct - 24], "s_aq%d" % (ct - 24)
            elif ct == 28:
                kind, dst, dkey = "rope", s_kk, "s_kk"
            elif ct == 29:
                kind, fn, dst, dkey = "act", AF.Copy, s_vv, "s_vv"
            elif ct < 34:
                kind, fn, dst, dkey = "act", AF.Silu, s_ag[ct - 30], "s_ag%d" % (ct - 30)
            elif ct < 38:
                kind, fn, dst, dkey = "act", AF.Sigmoid, s_sa[ct - 34], "s_sa%d" % (ct - 34)
            else:
                kind, fn, dst, dkey = "act", AF.Sigmoid, s_sb[ct - 38], "s_sb%d" % (ct - 38)
            for qi in range(T // QT):
                q0 = qi * QT
                if kind in ("conv", "rope"):
                    Yb = Y[yc % 2]
                    Yk = "Y%d" % (yc % 2)
                    Yp = Y[(yc + 1) % 2]
                    Ypk = "Y%d" % ((yc + 1) % 2)
                    yc += 1
                    if kind == "conv":
                        if qi == 0:
                            P.op("gpsimd", lambda e, Yb=Yb: e.memset(Yb[:, 0:3], 0.0), w=[Yk])
                        else:
                            P.op("gpsimd", lambda e, Yb=Yb, Yp=Yp: e.tensor_copy(out=Yb[:, 0:3], in_=Yp[:, QT:QT + 3]),
                                 r=[Ypk], w=[Yk])
                obuf = ob[oc % 2]
                okey = "ob%d" % (oc % 2)
                oc += 1
                for bi in range(QT // 512):
                    t0 = q0 + bi * 512
                    pbank = psA[pa % 4]
                    pkey = "psA%d" % (pa % 4)
                    pa += 1
                    for kt in range(16):
                        P.op("tensor", lambda e, pbank=pbank, wbuf=wbuf, kt=kt, t0=t0: e.matmul(
                            pbank[:, :], lhsT=wbuf[:, kt, :], rhs=hT[:, kt, t0:t0 + 512], start=(kt == 0), stop=(kt == 15)),
                            r=[wk] + ["hT%d_%d" % (t0 // TB1 + i, kt) for i in range(512 // TB1)], w=[pkey])
                    if kind in ("conv", "rope"):
                        P.op("scalar", lambda e, pbank=pbank, Yb=Yb, bi=bi: e.copy(out=Yb[:, 3 + bi * 512:3 + bi * 512 + 512],
                                                                                 in_=pbank[:, :]), r=[pkey], w=[Yk])
                    else:
                        P.op("scalar", lambda e, pbank=pbank, obuf=obuf, bi=bi, fn=fn: e.activation(
                            out=obuf[:, bi * 512:bi * 512 + 512], in_=pbank[:, :], func=fn), r=[pkey], w=[okey])
                if kind == "conv":
                    P.op("vector", lambda e, Yb=Yb, ct=ct: e.tensor_scalar(out=acc[:, :], in0=Yb[:, 0:QT], scalar1=cw_s[:, ct, 0:1],
                                                                           scalar2=None, op0=ALU.mult),
                         r=[Yk, "cw_s"], w=["acc"])
                    for j in range(1, 4):
                        P.op("vector", lambda e, Yb=Yb, ct=ct, j=j: e.scalar_tensor_tensor(
                            out=acc[:, :], in0=Yb[:, j:j + QT], scalar=cw_s[:, ct, j:j + 1], in1=acc[:, :],
                            op0=ALU.mult, op1=ALU.add), r=[Yk, "cw_s", "acc"], w=["acc"])
                    if ct >= 8:
                        P.op("scalar", lambda e, obuf=obuf: e.activation(out=obuf[:, :], in_=acc[:, :], func=AF.Silu),
                             r=["acc"], w=[okey])
                    else:
                        P.op("scalar", lambda e: e.activation(out=sil[:, :], in_=acc[:, :], func=AF.Silu), r=["acc"], w=["sil"])
                        P.op("scalar", lambda e: e.activation(out=sqb[:, :], in_=sil[:, :], func=AF.Square), r=["sil"], w=["sqb"])
                        for bi in range(QT // 512):
                            pb = psB[bi]
                            pk = "psB%d" % bi
                            P.op("tensor", lambda e, pb=pb, bi=bi: e.matmul(pb[:, :], lhsT=ones_bf[:, :],
                                                                           rhs=sqb[:, bi * 512:bi * 512 + 512], start=True, stop=True),
                                 r=["sqb", "ones_bf"], w=[pk])
                            P.op("scalar", lambda e, pb=pb, bi=bi: e.activation(
                                out=rs2[:, bi * 512:bi * 512 + 512], in_=pb[:, :], func=AF.Sqrt, bias=EPS, scale=1.0),
                                r=[pk], w=["rs2"])
                        P.op("vector", lambda e: e.reciprocal(out=rs2[:, :], in_=rs2[:, :]), r=["rs2"], w=["rs2"])
                        qscale = (128.0 ** -0.5) if ct < 4 else 1.0
                        P.op("vector", lambda e, obuf=obuf, qscale=qscale: e.scalar_tensor_tensor(
                            out=obuf[:, :], in0=sil[:, :], scalar=qscale, in1=rs2[:, :], op0=ALU.mult, op1=ALU.mult),
                            r=["sil", "rs2"], w=[okey])
                    dst = s_dn[ct]
                    dkey = "s_dn%d" % ct
                elif kind == "rope":
                    isq = ct < 28
                    P.op("scalar", lambda e, Yb=Yb: e.activation(out=sqb[:, :], in_=Yb[:, 3:3 + QT], func=AF.Square), r=[Yk], w=["sqb"])
                    for bi in range(QT // 512):
                        pb = psB[bi]
                        pk = "psB%d" % bi
                        P.op("tensor", lambda e, pb=pb, bi=bi: e.matmul(pb[:, :], lhsT=blk64_bf, rhs=sqb[:, bi * 512:bi * 512 + 512],
                                                                       start=True, stop=True), r=["sqb", "cst_bf"], w=[pk])
                        sc_ = 64.0 if isq else 1.0
                        P.op("scalar", lambda e, pb=pb, bi=bi, sc_=sc_: e.activation(
                            out=rs2[:, bi * 512:bi * 512 + 512], in_=pb[:, :], func=AF.Sqrt, bias=EPS * sc_, scale=sc_),
                            r=[pk], w=["rs2"])
                    P.op("vector", lambda e: e.reciprocal(out=rs2[:, :], in_=rs2[:, :]), r=["rs2"], w=["rs2"])
                    nwc = sm_s[:, 0:1] if isq else sm_s[:, 1:2]
                    P.op("vector", lambda e, Yb=Yb, nwc=nwc: e.scalar_tensor_tensor(
                        out=sqb[:, :], in0=Yb[:, 3:3 + QT], scalar=nwc, in1=rs2[:, :], op0=ALU.mult, op1=ALU.mult),
                        r=[Yk, "rs2", "sm_s"], w=["sqb"])
                    P.op("vector", lambda e, q0=q0: e.tensor_tensor(out=acc[:, :], in0=sqb[:, :], in1=Cf[:, q0:q0 + QT], op=ALU.mult),
                         r=["sqb", "wad0"], w=["acc"])
                    for bi in range(QT // 512):
                        pb = psC[bi]
                        pk = "psC%d" % bi
                        P.op("tensor", lambda e, pb=pb, bi=bi: e.matmul(pb[:, :], lhsT=Pm_bf, rhs=sqb[:, bi * 512:bi * 512 + 512],
                                                                       start=True, stop=True), r=["sqb", "cst_bf"], w=[pk])
                        P.op("vector", lambda e, pb=pb, bi=bi, q0=q0: e.tensor_tensor(
                            out=sil[:, bi * 512:bi * 512 + 512], in0=pb[:, :], in1=Sf[:, q0 + bi * 512:q0 + bi * 512 + 512], op=ALU.mult),
                            r=[pk, "wad1"], w=["sil"])
                    P.op("gpsimd", lambda e, obuf=obuf: e.tensor_tensor(out=obuf[:, :], in0=acc[:, :], in1=sil[:, :], op=ALU.add),
                         r=["acc", "sil"], w=[okey])
                P.dma("sync", lambda e, dst=dst, obuf=obuf, q0=q0: e.dma_start(out=dst[:, q0:q0 + QT], in_=obuf[:, :]),
                      r=[okey], w=[dkey], sem="st_" + okey)

        if upto == "ct":
            raise _Stop()
        xf0 = xb[0][:, :, :].rearrange("p a b -> p (a b)")
        xf1 = xb[1][:, :, :].rearrange("p a b -> p (a b)")
        v3 = lambda ap: ap.rearrange("p (n c) -> p n c", c=8)
        beta_t = v3(xf0[0:64, 0:512])
        gc_t = v3(xf0[0:64, 512:1024])
        eg_t = v3(xf0[0:64, 1024:1536])
        ekd_t = v3(xf0[0:64, 1536:2048])
        bge_t = v3(xf1[0:64, 0:512])
        egl_t = v3(xf1[:, 512:1024])
        negA = xf1[0:64, 1024:1032]
        wbuf = wb[(NCT - 1) % 2]
        wk = "wb%d" % ((NCT - 1) % 2)
        for n in range(64):
            pbank = psA[n // 32]
            pkey = "psA%d" % (n // 32)
            for kt in range(16):
                P.op("tensor", lambda e, pbank=pbank, wbuf=wbuf, kt=kt, n=n: e.matmul(
                    pbank[0:64, (n % 32) * 16:(n % 32) * 16 + 16], lhsT=hT[:, kt, n * 64:n * 64 + 64], rhs=wbuf[:, kt, 0:16],
                    start=(kt == 0), stop=(kt == 15)), r=[wk, "hT%d_%d" % (n * 64 // TB1, kt)], w=[pkey])
        bgraw = acc[0:64, :].rearrange("p (n c) -> p n c", c=16)
        tmpb = sil[0:64, 0:512].rearrange("p (n c) -> p n c", c=8)
        for half in range(2):
            P.op("scalar", lambda e, half=half: e.copy(out=acc[0:64, half * 512:half * 512 + 512], in_=psA[half][0:64, :]),
                 r=["psA%d" % half], w=["acc"])
        if upto == "bg1":
            raise _Stop()
        P.op("scalar", lambda e: e.activation(out=beta_t, in_=bgraw[:, :, 0:8], func=AF.Sigmoid), r=["acc"], w=["beta_t"])
        P.op("vector", lambda e: e.tensor_tensor(out=tmpb, in0=bgraw[:, :, 8:16],
                                                 in1=sm_s[0:64, 16:24].unsqueeze(1).to_broadcast([64, 64, 8]), op=ALU.add),
             r=["acc", "sm_s"], w=["sil"])
        P.op("scalar", lambda e: e.activation(out=tmpb, in_=tmpb, func=AF.Exp), r=["sil"], w=["sil"])
        P.op("scalar", lambda e: e.activation(out=tmpb, in_=tmpb, func=AF.Ln, bias=1.0, scale=1.0), r=["sil"], w=["sil"])
        P.op("scalar", lambda e: e.activation(out=negA, in_=sm_s[0:64, 8:16], func=AF.Exp), r=["sm_s"], w=["negA"])
        P.op("vector", lambda e: e.tensor_scalar(out=negA, in0=negA, scalar1=-1.0, scalar2=None, op0=ALU.mult),
             r=["negA"], w=["negA"])
        P.op("vector", lambda e: e.tensor_tensor(out=tmpb, in0=tmpb, in1=negA.unsqueeze(1).to_broadcast([64, 64, 8]), op=ALU.mult),
             r=["sil", "negA"], w=["sil"])
        if upto == "bg2":
            raise _Stop()
        P.op("tensor", lambda e: e.matmul(psB[0][0:64, :], lhsT=tri64, rhs=sil[0:64, 0:512], start=True, stop=True),
             r=["sil", "cst_s"], w=["psB0"])
        gcf = xf0[0:64, 512:1024]
        P.op("vector", lambda e: e.tensor_copy(out=gcf, in_=psB[0][0:64, :]), r=["psB0"], w=["gc_t"])
        if upto == "bg3":
            raise _Stop()
        P.op("tensor", lambda e: e.matmul(psB[1][:, :], lhsT=sel63, rhs=gcf, start=True, stop=True), r=["gc_t", "cst_s"], w=["psB1"])
        P.op("scalar", lambda e: e.activation(out=xf1[:, 512:1024], in_=psB[1][:, :], func=AF.Exp),
             r=["psB1"], w=["egl_t"])
        if upto == "bg4":
            raise _Stop()
        P.op("vector", lambda e: e.tensor_tensor(out=sil[0:64, 512:1024], in0=psB[1][0:64, :], in1=gcf, op=ALU.subtract),
             r=["egl_t", "gc_t"], w=["sil"])
        if upto == "bg4a":
            raise _Stop()
        P.op("scalar", lambda e: e.activation(out=xf0[0:64, 1536:2048], in_=sil[0:64, 512:1024], func=AF.Exp),
             r=["sil"], w=["ekd_t"])
        if upto == "bg4b":
            raise _Stop()
        P.op("scalar", lambda e: e.activation(out=xf0[0:64, 1024:1536], in_=gcf, func=AF.Exp),
             r=["gc_t"], w=["eg_t"])
        if upto == "bg4c":
            raise _Stop()
        P.op("vector", lambda e: e.tensor_tensor(out=bge_t, in0=beta_t, in1=eg_t, op=ALU.mult),
             r=["beta_t", "eg_t"], w=["bge_t"])
        if upto == "bg5":
            raise _Stop()
        esink = xf1[:, 1040:1044]
        P.op("scalar", lambda e: e.activation(out=esink, in_=sm_s[:, 2:6], func=AF.Exp), r=["sm_s"], w=["esink"])

        if debug:
            dbg_bg = nc.dram_tensor("dbg_bg", [6, 128, 512], F32, kind="ExternalOutput").ap()
            for i_, (ap_, k_) in enumerate(((xf0[0:64, 0:512], "beta_t"), (xf0[0:64, 512:1024], "gc_t"), (xf0[0:64, 1024:1536], "eg_t"),
                                            (xf0[0:64, 1536:2048], "ekd_t"), (xf1[0:64, 0:512], "bge_t"), (xf1[:, 512:1024], "egl_t"))):
                P.dma("sync", lambda e, i_=i_, ap_=ap_: e.dma_start(out=dbg_bg[i_, 0:ap_.shape[0], :], in_=ap_), r=[k_], w=["dbg_bg%d" % i_], sem="dbg")
        if upto == "p2":
            raise _Stop()
        bar_t = xf1[:, 1048:1052]
        P.barrier(lambda e: e.memset(bar_t[:, 0:1], 0.0))
        bigw = [0]

        def carve(words, dt=F32, shape=None):
            n = words if dt == F32 else (words + 1) // 2
            ap = big[:, bigw[0]:bigw[0] + n]
            bigw[0] += n
            assert bigw[0] <= 32768, bigw[0]
            if dt != F32:
                ap = ap.bitcast(dt)
            return ap
        SB = 512
        swq = [carve(4 * SB, BF16).rearrange("p (a b) -> p a b", a=4) for _ in range(2)]
        swk = [carve(SB, BF16) for _ in range(2)]
        swv = [carve(SB, BF16) for _ in range(2)]
        swg = [carve(4 * SB, BF16).rearrange("p (a b) -> p a b", a=4) for _ in range(2)]
        Vp = [carve(256, BF16).rearrange("p (a b) -> p a b", a=2) for _ in range(2)]
        swqe = carve(4 * SB, BF16).rearrange("p (a b) -> p a b", a=4)
        swqo = carve(4 * SB, BF16).rearrange("p (a b) -> p a b", a=4)
        PT = carve(2048, BF16)
        rden = carve(512)
        aout = carve(512)
        ast = [carve(4 * SB, BF16).rearrange("p (a b) -> p a b", a=4) for _ in range(2)]
        psS = [psA[0], psA[1], psA[2], psA[3]]
        g_src3 = g_src.rearrange("(k p) t -> p k t", p=128)
        def sw_loads(sbi):
            u = sbi % 2
            c0 = sbi * SB
            P.dma("sync", lambda e, u=u, c0=c0: e.dma_start(out=swq[u][:, :, :], in_=s_aq.rearrange("k p t -> p k t")[:, :, c0:c0 + SB]),
                  r=["s_aq%d" % i for i in range(4)], w=["swq%d" % u], sem="swq%d" % u)
            P.dma("sync", lambda e, u=u, c0=c0: e.dma_start(out=swk[u], in_=s_kk[:, c0:c0 + SB]), r=["s_kk"], w=["swk%d" % u], sem="swk%d" % u)
            P.dma("sync", lambda e, u=u, c0=c0: e.dma_start(out=swv[u], in_=s_vv[:, c0:c0 + SB]), r=["s_vv"], w=["swv%d" % u], sem="swv%d" % u)
            P.dma("sync", lambda e, u=u, c0=c0: e.dma_start(out=swg[u][:, :, :], in_=s_ag.rearrange("k p t -> p k t")[:, :, c0:c0 + SB]),
                  r=["s_ag%d" % i for i in range(4)], w=["swg%d" % u], sem="swg%d" % u)

        sw_loads(0)
        for sbi in range(T // SB):
            u = sbi % 2
            c0 = sbi * SB
            P.op("gpsimd", lambda e, u=u: e.tensor_scalar(out=swqe, in0=swq[u], scalar1=cst_s[:, 961:962], scalar2=None, op0=ALU.mult),
                 r=["swq%d" % u, "cst_s"], w=["swqm"])
            P.op("vector", lambda e, u=u: e.tensor_scalar(out=swqo, in0=swq[u], scalar1=cst_s[:, 962:963], scalar2=None, op0=ALU.mult),
                 r=["swq%d" % u, "cst_s"], w=["swqm"])
            for bi in range(SB // 128):
                if bi == 1 and sbi + 1 < T // SB:
                    sw_loads(sbi + 1)
                nb = sbi * 4 + bi
                b0 = bi * 128
                vcur = nb % 2
                if upto == "swa0":
                    raise _Stop()
                P.op("tensor", lambda e, u=u, b0=b0: e.matmul(psB[0][:, 0:128], lhsT=swv[u][:, b0:b0 + 128], rhs=cst_bf[:, 1280:1408],
                                                             start=True, stop=True), r=["swv%d" % u, "cst_bf"], w=["psB0"])
                P.op("tensor", lambda e, u=u, b0=b0: e.matmul(psB[0][:, 128:256], lhsT=swv[u][:, b0:b0 + 128], rhs=cst_bf[:, 1408:1536],
                                                             start=True, stop=True), r=["swv%d" % u, "cst_bf"], w=["psB0"])
                P.op("scalar", lambda e, vcur=vcur: e.copy(out=Vp[vcur][:, :, :].rearrange("p a b -> p (a b)"), in_=psB[0][:, 0:256]),
                     r=["psB0"], w=["Vp%d" % vcur])
                if upto == "swa1":
                    raise _Stop()
                kbs = []
                if nb > 0:
                    if bi == 0:
                        kbs.append((0, swk[1 - u], "swk%d" % (1 - u), SB - 128, 1 - vcur))
                    else:
                        kbs.append((0, swk[u], "swk%d" % u, b0 - 128, 1 - vcur))
                kbs.append((1, swk[u], "swk%d" % u, b0, vcur))
                for kbi, kt_, kk_, ko, _v in kbs:
                    for h in range(8):
                        t, par = h // 2, h % 2
                        col = (kbi * 8 + h) * 128
                        pst = psS[col // 512]
                        qm_ = swqe if par == 0 else swqo
                        P.op("tensor", lambda e, kt_=kt_, ko=ko, qm_=qm_, t=t, b0=b0, pst=pst, col=col: e.matmul(
                            pst[:, col % 512:col % 512 + 128], lhsT=kt_[:, ko:ko + 128],
                            rhs=qm_[:, t, b0:b0 + 128], start=True, stop=True),
                            r=[kk_, "swqm"], w=["psA%d" % (col // 512)])
                if upto == "swa2":
                    raise _Stop()
                lo = 0 if nb > 0 else 2
                for q4 in range(lo, 4):
                    P.op("scalar", lambda e, q4=q4: e.activation(out=PT[:, q4 * 512:q4 * 512 + 512], in_=psS[q4][:, :], func=AF.Exp),
                         r=["psA%d" % q4], w=["PT"])
                for kbi in range(lo // 2, 2):
                    P.op("vector", lambda e, kbi=kbi: e.tensor_tensor(
                        out=PT[:, kbi * 1024:kbi * 1024 + 1024].rearrange("p (h q) -> p h q", h=8),
                        in0=PT[:, kbi * 1024:kbi * 1024 + 1024].rearrange("p (h q) -> p h q", h=8),
                        in1=maskCP[:, kbi * 128:kbi * 128 + 128].unsqueeze(1).to_broadcast([128, 8, 128]), op=ALU.mult),
                        r=["PT", "cst_bf"], w=["PT"])
                if upto == "swa3":
                    raise _Stop()
                for t in range(4):
                    mm = [(kbi, par, _v) for (kbi, _a, _b, _c, _v) in kbs for par in range(2)]
                    for i, (kbi, par, vv_) in enumerate(mm):
                        col = (kbi * 8 + 2 * t + par) * 128
                        P.op("tensor", lambda e, vv_=vv_, par=par, col=col, t=t, i=i, n=len(mm): e.matmul(
                            psC[0][:, t * 128:t * 128 + 128], lhsT=Vp[vv_][:, par, :], rhs=PT[:, col:col + 128],
                            start=(i == 0), stop=(i == n - 1)), r=["Vp%d" % vv_, "PT"], w=["psC0"])
                    for i, (kbi, par, vv_) in enumerate(mm):
                        col = (kbi * 8 + 2 * t + par) * 128
                        oo = onesE_bf if par == 0 else onesO_bf
                        P.op("tensor", lambda e, oo=oo, col=col, t=t, i=i, n=len(mm): e.matmul(
                            psC[1][:, t * 128:t * 128 + 128], lhsT=oo, rhs=PT[:, col:col + 128],
                            start=(i == 0), stop=(i == n - 1)), r=["cst_bf", "PT"], w=["psC1"])
                if upto == "swa4":
                    raise _Stop()
                P.op("scalar", lambda e: e.copy(out=rden, in_=psC[1][:, :]), r=["psC1"], w=["rden"])
                P.op("vector", lambda e: e.tensor_tensor(out=rden.rearrange("p (a b) -> p a b", a=4),
                                                         in0=rden.rearrange("p (a b) -> p a b", a=4),
                                                         in1=esink.unsqueeze(2).to_broadcast([128, 4, 128]), op=ALU.add),
                     r=["rden", "esink"], w=["rden"])
                P.op("vector", lambda e: e.reciprocal(out=rden, in_=rden), r=["rden"], w=["rden"])
                P.op("scalar", lambda e: e.copy(out=aout, in_=psC[0][:, :]), r=["psC0"], w=["aout"])
                P.op("vector", lambda e: e.tensor_tensor(out=aout, in0=aout, in1=rden, op=ALU.mult), r=["aout", "rden"], w=["aout"])
                P.op("gpsimd", lambda e, u=u, b0=b0: e.tensor_tensor(out=ast[u][:, :, b0:b0 + 128],
                                                                     in0=aout.rearrange("p (a b) -> p a b", a=4),
                                                                     in1=swg[u][:, :, b0:b0 + 128], op=ALU.mult),
                     r=["aout", "swg%d" % u], w=["ast%d" % u])
            P.dma("sync", lambda e, u=u, c0=c0: e.dma_start(out=g_src3[:, 0:4, c0:c0 + SB], in_=ast[u][:, :, :]),
                  r=["ast%d" % u], w=["g_src_a"], sem="ast%d" % u)
        RG = [[0, 1, 2, 3], [4, 5, 6, 7]]
        if upto == "swa":
            raise _Stop()
        P.barrier(lambda e: e.memset(bar_t[:, 1:2], 0.0), exclude=["g_dst%d" % k for k in range(4)])
        bigw[0] = 0
        for k in range(0 if os.environ.get("KSIM_NO_CC") != "1" else 4, 4):
            P.dma("gpsimd", lambda e, k=k: e.collective_compute("AllGather", ALU.bypass, replica_groups=RG,
                                                                ins=[g_src[k * 128:(k + 1) * 128, :].opt()],
                                                                outs=[g_dst[k * 512:(k + 1) * 512, :].opt()]),
                  r=["g_src_a"], w=["g_dst%d" % k], sem="cc1_%d" % k, inc=1)
        dnin = [carve(16 * SB, BF16).rearrange("p (a b) -> p a b", a=16) for _ in range(2)]
        zin = [carve(8 * SB, BF16).rearrange("p (a b) -> p a b", a=8) for _ in range(2)]
        ogs1 = carve(8 * SB, BF16).rearrange("p (a b) -> p a b", a=8)
        ogs = [ogs1, ogs1]
        S32 = carve(1024).rearrange("p (a b) -> p a b", a=8)
        Sbf = carve(1024, BF16).rearrange("p (a b) -> p a b", a=8)
        P.op("vector", lambda e: e.memset(S32, 0.0), w=["S32"])
        P.op("vector", lambda e: e.memset(Sbf, 0.0), w=["Sbf"])

        def c3(n, m, dt):
            return carve(n * m, dt)[0:64, :].rearrange("p (a b) -> p a b", a=n)

        def two(f):
            return [f(), f()]

        Ktm = two(lambda: c3(4, 128, BF16)); Vtm = two(lambda: c3(8, 128, BF16)); DmT = two(lambda: c3(8, 64, F32))
        M0 = two(lambda: c3(8, 64, BF16)); N0 = two(lambda: c3(8, 64, BF16)); R0 = two(lambda: c3(8, 64, BF16))
        gcb = c3(8, 64, F32); d1 = c3(8, 64, F32); DmTs = c3(8, 64, F32); dgb = c3(8, 64, BF16)
        kbT = carve(8 * 64, BF16).rearrange("p (a b) -> p a b", a=8)
        Mb = [c3(8, 64, BF16) for _ in range(2)]; Nb = [c3(8, 64, BF16) for _ in range(2)]; Rb = [c3(8, 64, BF16) for _ in range(2)]
        Vb = c3(8, 128, BF16); Kbg = c3(8, 128, BF16)
        u32 = two(lambda: c3(8, 128, F32))
        wT = two(lambda: carve(8 * 64, BF16).rearrange("p (a b) -> p a b", a=8))
        qkT = two(lambda: c3(8, 64, BF16)); kd = two(lambda: c3(8, 128, BF16))
        vnew = c3(8, 128, BF16); o32 = c3(8, 128, F32); p3s = c3(8, 128, F32)
        ssq = carve(8)[0:64, :]; onb = c3(8, 128, BF16)
        HG = ((0, 4), (4, 8))

        def bank(i, part=64, lo=0, n=512):
            return ([psA[0], psA[1], psA[2], psA[3], psB[0], psB[1], psC[0], psC[1]][i])[0:part, lo:lo + n]
        BK = ["psA0", "psA1", "psA2", "psA3", "psB0", "psB1", "psC0", "psC1"]

        def h3(ap, a):
            return ap.rearrange("p (a b) -> p a b", a=a)

        def chunk_views(n):
            sbi, ci = n // 8, n % 8
            u, a0 = sbi % 2, ci * 64
            qTm = lambda m, u=u, a0=a0: dnin[u][:, m, a0:a0 + 64]
            kTm = lambda m, u=u, a0=a0: dnin[u][:, 4 + m, a0:a0 + 64]
            vTh = lambda h, u=u, a0=a0: dnin[u][:, 8 + h, a0:a0 + 64]
            return u, a0, qTm, kTm, vTh

        def emit_A1(n):
            s_ = n % 2
            S_ = str(s_)
            u, a0, qTm, kTm, vTh = chunk_views(n)
            dk_ = "dnin%d" % u
            if n % 8 == 0:
                c0 = (n // 8) * SB
                P.dma("sync", lambda e, u=u, c0=c0: e.dma_start(out=dnin[u][:, :, :], in_=s_dn.rearrange("k p t -> p k t")[:, :, c0:c0 + SB]),
                      r=["s_dn%d" % i for i in range(16)], w=["dnin%d" % u], sem="dnin%d" % u)
                P.dma("sync", lambda e, u=u, c0=c0: e.dma_start(out=zin[u][:, :, :], in_=s_z.rearrange("k p t -> p k t")[:, :, c0:c0 + SB]),
                      r=["s_z%d" % i for i in range(8)], w=["zin%d" % u], sem="zin%d" % u)
            for m in range(4):
                P.op("tensor", lambda e, m=m, kTm=kTm: e.matmul(bank(3, 64, m * 128, 128), lhsT=kTm(m), rhs=I_bf, start=True, stop=True),
                     r=[dk_, "cst_bf"], w=[BK[3]])
            P.op("scalar", lambda e, s_=s_: e.copy(out=Ktm[s_].rearrange("p a b -> p (a b)"), in_=bank(3)), r=[BK[3]], w=["Ktm" + S_])
            for hg, (h0, h1) in enumerate(HG):
                for h in range(h0, h1):
                    P.op("tensor", lambda e, h=h, h0=h0, vTh=vTh: e.matmul(bank(4, 64, (h - h0) * 128, 128), lhsT=vTh(h), rhs=I_bf,
                                                                          start=True, stop=True), r=[dk_, "cst_bf"], w=[BK[4]])
                P.op("scalar", lambda e, h0=h0, h1=h1, s_=s_: e.copy(out=Vtm[s_][:, h0:h1, :].rearrange("p a b -> p (a b)"), in_=bank(4)),
                     r=[BK[4]], w=["Vtm" + S_])
            P.op("vector", lambda e, n=n: e.tensor_copy(out=gcb, in_=gc_t[:, n, :].unsqueeze(2).to_broadcast([64, 8, 64])),
                 r=["gc_t"], w=["gcb"])
            for h in range(8):
                P.op("tensor", lambda e, h=h: e.matmul(bank(3, 64, h * 64, 64), lhsT=gcb[:, h, :], rhs=I64f, start=True, stop=True),
                     r=["gcb", "cst_s"], w=[BK[3]])
            P.op("vector", lambda e, n=n: e.tensor_tensor(out=d1, in0=h3(bank(3), 8), in1=gc_t[:, n, :].unsqueeze(2).to_broadcast([64, 8, 64]),
                                                          op=ALU.subtract), r=[BK[3], "gc_t"], w=["d1"])
            P.op("vector", lambda e: e.tensor_scalar(out=d1, in0=d1, scalar1=0.0, scalar2=None, op0=ALU.min), r=["d1"], w=["d1"])
            P.op("scalar", lambda e: e.activation(out=d1, in_=d1, func=AF.Exp), r=["d1"], w=["d1"])
            P.op("vector", lambda e, s_=s_: e.tensor_tensor(out=DmT[s_], in0=d1, in1=mU.unsqueeze(1).to_broadcast([64, 8, 64]), op=ALU.mult),
                 r=["d1", "cst_s"], w=["DmT" + S_])
            P.op("vector", lambda e: e.tensor_tensor(out=DmTs, in0=d1, in1=mUs.unsqueeze(1).to_broadcast([64, 8, 64]), op=ALU.mult),
                 r=["d1", "cst_s"], w=["DmTs"])
            P.op("vector", lambda e, n=n: e.tensor_tensor(out=dgb, in0=I64f.unsqueeze(1).to_broadcast([64, 8, 64]),
                                                          in1=beta_t[:, n, :].unsqueeze(2).to_broadcast([64, 8, 64]), op=ALU.mult),
                 r=["beta_t", "cst_s"], w=["dgb"])
            for h in range(8):
                P.op("tensor", lambda e, h=h, s_=s_: e.matmul(bank(4, 128, h * 64, 64), lhsT=Ktm[s_][:, h // 2, :], rhs=dgb[:, h, :], start=True, stop=True),
                     r=["Ktm" + S_, "dgb"], w=[BK[4]])
            P.op("scalar", lambda e: e.copy(out=kbT.rearrange("p a b -> p (a b)"), in_=bank(4, 128)), r=[BK[4]], w=["kbT"])
            for h in range(8):
                P.op("tensor", lambda e, h=h, kTm=kTm: e.matmul(bank(3, 64, h * 64, 64), lhsT=kTm(h // 2), rhs=kbT[:, h, :], start=True, stop=True),
                     r=[dk_, "kbT"], w=[BK[3]])
            P.op("vector", lambda e, s_=s_: e.scalar_tensor_tensor(out=M0[s_], in0=h3(bank(3), 8), scalar=-1.0, in1=DmTs, op0=ALU.mult, op1=ALU.mult),
                 r=[BK[3], "DmTs"], w=["M0" + S_])
            for h in range(8):
                P.op("tensor", lambda e, h=h, s_=s_: e.matmul(bank(4, 64, h * 64, 64), lhsT=M0[s_][:, h, :], rhs=cst_bf[0:64, 0:64], start=True, stop=True),
                     r=["M0" + S_, "cst_bf"], w=[BK[4]])
            P.op("scalar", lambda e, s_=s_: e.copy(out=N0[s_].rearrange("p a b -> p (a b)"), in_=bank(4)), r=[BK[4]], w=["N0" + S_])
            P.op("vector", lambda e, s_=s_: e.tensor_tensor(out=R0[s_], in0=M0[s_], in1=cst_bf[0:64, 0:64].unsqueeze(1).to_broadcast([64, 8, 64]), op=ALU.add),
                 r=["M0" + S_, "cst_bf"], w=["R0" + S_])

        def emit_A2(n):
            s_ = n % 2
            S_ = str(s_)
            u, a0, qTm, kTm, vTh = chunk_views(n)
            dk_ = "dnin%d" % u
            Mc, Nc, Rc = M0[s_], N0[s_], R0[s_]
            Mk, Nk, Rk = "M0" + S_, "N0" + S_, "R0" + S_
            for lvl in range(1, 6):
                nx = lvl % 2
                if lvl < 5:
                    for h in range(8):
                        P.op("tensor", lambda e, h=h, Nc=Nc, Mc=Mc: e.matmul(bank(5, 64, h * 64, 64), lhsT=Nc[:, h, :], rhs=Mc[:, h, :],
                                                                            start=True, stop=True), r=[Nk, Mk], w=[BK[5]])
                    P.op("vector", lambda e, nx=nx: e.tensor_copy(out=Mb[nx].rearrange("p a b -> p (a b)"), in_=bank(5)),
                         r=[BK[5]], w=["Mb%d" % nx])
                for h in range(8):
                    P.op("tensor", lambda e, h=h, Nc=Nc, Mc=Mc: e.matmul(bank(6, 64, h * 64, 64), lhsT=Mc[:, h, :], rhs=Nc[:, h, :],
                                                                        start=True, stop=True), r=[Nk, Mk], w=[BK[6]])
                P.op("scalar", lambda e, nx=nx: e.copy(out=Nb[nx].rearrange("p a b -> p (a b)"), in_=bank(6)), r=[BK[6]], w=["Nb%d" % nx])
                for h in range(8):
                    P.op("tensor", lambda e, h=h, nx=nx, Rc=Rc: e.matmul(bank(7, 64, h * 64, 64), lhsT=Nb[nx][:, h, :], rhs=Rc[:, h, :],
                                                                        start=True, stop=True), r=["Nb%d" % nx, Rk], w=[BK[7]])
                P.op("vector", lambda e, nx=nx, Rc=Rc: e.tensor_tensor(out=Rb[nx], in0=h3(bank(7), 8), in1=Rc, op=ALU.add),
                     r=[BK[7], Rk], w=["Rb%d" % nx])
                Mc, Nc, Rc = Mb[nx], Nb[nx], Rb[nx]
                Mk, Nk, Rk = "Mb%d" % nx, "Nb%d" % nx, "Rb%d" % nx
            TT, TTk = Rc, Rk
            P.op("vector", lambda e, n=n, s_=s_: e.tensor_tensor(out=Vb, in0=Vtm[s_], in1=beta_t[:, n, :].unsqueeze(2).to_broadcast([64, 8, 128]), op=ALU.mult),
                 r=["Vtm" + S_, "beta_t"], w=["Vb"])
            K8 = Ktm[s_].unsqueeze(2).to_broadcast([64, 4, 2, 128])
            P.op("vector", lambda e, n=n, K8=K8: e.tensor_tensor(out=Kbg.rearrange("p (m r) d -> p m r d", r=2), in0=K8,
                                                               in1=bge_t[:, n, :].rearrange("p (m r) -> p m r", r=2).unsqueeze(3).to_broadcast([64, 4, 2, 128]),
                                                               op=ALU.mult), r=["Ktm" + S_, "bge_t"], w=["Kbg"])
            P.op("vector", lambda e, n=n, K8=K8, s_=s_: e.tensor_tensor(out=kd[s_].rearrange("p (m r) d -> p m r d", r=2), in0=K8,
                                                                      in1=ekd_t[:, n, :].rearrange("p (m r) -> p m r", r=2).unsqueeze(3).to_broadcast([64, 4, 2, 128]),
                                                                      op=ALU.mult), r=["Ktm" + S_, "ekd_t"], w=["kd" + S_])
            for m in range(4):
                P.op("tensor", lambda e, m=m, kTm=kTm, qTm=qTm: e.matmul(bank(5, 64, m * 64, 64), lhsT=kTm(m), rhs=qTm(m), start=True, stop=True),
                     r=[dk_], w=[BK[5]])
            P.op("vector", lambda e, s_=s_: e.tensor_tensor(out=qkT[s_].rearrange("p (m r) d -> p m r d", r=2),
                                                            in0=h3(bank(5, 64, 0, 256), 4).unsqueeze(2).to_broadcast([64, 4, 2, 64]),
                                                            in1=DmT[s_].rearrange("p (m r) d -> p m r d", r=2), op=ALU.mult),
                 r=[BK[5], "DmT" + S_], w=["qkT" + S_])
            for hg, (h0, h1) in enumerate(HG):
                hs = slice(h0, h1)
                for h in range(h0, h1):
                    P.op("tensor", lambda e, h=h, h0=h0, TT=TT: e.matmul(bank(6, 64, (h - h0) * 128, 128), lhsT=TT[:, h, :], rhs=Vb[:, h, :],
                                                                        start=True, stop=True), r=[TTk, "Vb"], w=[BK[6]])
                P.op("scalar", lambda e, hs=hs, s_=s_: e.copy(out=u32[s_][:, hs, :].rearrange("p a b -> p (a b)"), in_=bank(6)), r=[BK[6]], w=["u32" + S_])
            for h in range(8):
                P.op("tensor", lambda e, h=h, TT=TT: e.matmul(bank(7, 128, h * 64, 64), lhsT=Kbg[:, h, :], rhs=TT[:, h, :],
                                                             start=True, stop=True), r=[TTk, "Kbg"], w=[BK[7]])
            P.op("scalar", lambda e, s_=s_: e.copy(out=wT[s_].rearrange("p a b -> p (a b)"), in_=bank(7, 128)), r=[BK[7]], w=["wT" + S_])

        def emit_B(n):
            s_ = n % 2
            S_ = str(s_)
            u, a0, qTm, kTm, vTh = chunk_views(n)
            dk_ = "dnin%d" % u
            for hg, (h0, h1) in enumerate(HG):
                hs = slice(h0, h1)
                for h in range(h0, h1):
                    P.op("tensor", lambda e, h=h, h0=h0, s_=s_: e.matmul(bank(0, 64, (h - h0) * 128, 128), lhsT=wT[s_][:, h, :], rhs=Sbf[:, h, :],
                                                                        start=True, stop=True), r=["wT" + S_, "Sbf"], w=[BK[0]])
                P.op("vector", lambda e, hs=hs, s_=s_: e.tensor_tensor(out=vnew[:, hs, :], in0=u32[s_][:, hs, :], in1=h3(bank(0), 4), op=ALU.subtract),
                     r=["u32" + S_, BK[0]], w=["vnew"])
                for h in range(h0, h1):
                    P.op("tensor", lambda e, h=h, h0=h0, qTm=qTm: e.matmul(bank(1, 64, (h - h0) * 128, 128), lhsT=qTm(h // 2), rhs=Sbf[:, h, :],
                                                                          start=True, stop=True), r=[dk_, "Sbf"], w=[BK[1]])
                for h in range(h0, h1):
                    P.op("tensor", lambda e, h=h, h0=h0, s_=s_: e.matmul(bank(0, 64, (h - h0) * 128, 128), lhsT=qkT[s_][:, h, :], rhs=vnew[:, h, :],
                                                                        start=True, stop=True), r=["qkT" + S_, "vnew"], w=[BK[0]])
                P.op("scalar", lambda e, hs=hs: e.copy(out=p3s[:, hs, :].rearrange("p a b -> p (a b)"), in_=bank(0)), r=[BK[0]], w=["p3s"])
                P.op("vector", lambda e, hs=hs, n=n: e.tensor_tensor(out=o32[:, hs, :], in0=h3(bank(1), 4),
                                                                     in1=eg_t[:, n, hs].unsqueeze(2).to_broadcast([64, 4, 128]), op=ALU.mult),
                     r=[BK[1], "eg_t"], w=["o32"])
                P.op("vector", lambda e, hs=hs: e.tensor_tensor(out=o32[:, hs, :], in0=o32[:, hs, :], in1=p3s[:, hs, :], op=ALU.add),
                     r=["o32", "p3s"], w=["o32"])
                for h in range(h0, h1):
                    P.op("tensor", lambda e, h=h, h0=h0, s_=s_: e.matmul(bank(2, 128, (h - h0) * 128, 128), lhsT=kd[s_][:, h, :], rhs=vnew[:, h, :],
                                                                        start=True, stop=True), r=["kd" + S_, "vnew"], w=[BK[2]])
                P.op("vector", lambda e, hs=hs, n=n: e.tensor_tensor(out=S32[:, hs, :], in0=S32[:, hs, :],
                                                                     in1=egl_t[:, n, hs].unsqueeze(2).to_broadcast([128, 4, 128]), op=ALU.mult),
                     r=["S32", "egl_t"], w=["S32"])
                P.op("vector", lambda e, hs=hs: e.tensor_tensor(out=S32[:, hs, :], in0=S32[:, hs, :], in1=h3(bank(2, 128), 4), op=ALU.add),
                     r=["S32", BK[2]], w=["S32"])
                P.op("scalar", lambda e, hs=hs: e.copy(out=Sbf[:, hs, :], in_=S32[:, hs, :]), r=["S32"], w=["Sbf"])
                P.op("scalar", lambda e, hs=hs: e.activation(out=p3s[:, hs, :], in_=o32[:, hs, :], func=AF.Square),
                     r=["o32"], w=["p3s"])
                P.op("vector", lambda e, hs=hs: e.tensor_reduce(out=ssq[:, hs], in_=p3s[:, hs, :], axis=AX.X, op=ALU.add), r=["p3s"], w=["ssq"])
                P.op("scalar", lambda e, hs=hs: e.activation(out=ssq[:, hs], in_=ssq[:, hs], func=AF.Sqrt, bias=EPS, scale=1.0 / 128), r=["ssq"], w=["ssq"])
                P.op("vector", lambda e, hs=hs: e.reciprocal(out=ssq[:, hs], in_=ssq[:, hs]), r=["ssq"], w=["ssq"])
                P.op("vector", lambda e, hs=hs: e.tensor_tensor(out=onb[:, hs, :], in0=o32[:, hs, :],
                                                                in1=ssq[:, hs].unsqueeze(2).to_broadcast([64, 4, 128]), op=ALU.mult),
                     r=["o32", "ssq"], w=["onb"])
                for h in range(h0, h1):
                    P.op("tensor", lambda e, h=h, h0=h0: e.matmul(bank(1, 128, (h - h0) * 64, 64), lhsT=onb[:, h, :], rhs=cst_bf[0:64, 0:64],
                                                                 start=True, stop=True), r=["onb", "cst_bf"], w=[BK[1]])
                P.op("vector", lambda e, hs=hs, u=u, a0=a0: e.scalar_tensor_tensor(
                    out=ogs[u][:, hs, a0:a0 + 64], in0=h3(bank(1, 128, 0, 256), 4), scalar=sm_s[:, 6:7], in1=zin[u][:, hs, a0:a0 + 64],
                    op0=ALU.mult, op1=ALU.mult), r=[BK[1], "sm_s", "zin%d" % u], w=["ogs0"])
            if n % 8 == 7:
                c0 = (n // 8) * SB
                P.dma("sync", lambda e, u=u, c0=c0: e.dma_start(out=g_src3[:, 4:12, c0:c0 + SB], in_=ogs[u][:, :, :]),
                      r=["ogs0"], w=["g_src_o"], sem="ogs0")

        def record(fn, n):
            if n < 0 or n >= 64:
                return []
            keep = P.ins
            P.ins = []
            fn(n)
            out = P.ins
            P.ins = keep
            return out

        def merge(lists):
            lists = [l for l in lists if l]
            if not lists:
                return []
            tot = max(len(l) for l in lists)
            pos = [0] * len(lists)
            out = []
            for step in range(1, tot + 1):
                for i, l in enumerate(lists):
                    tgt = (len(l) * step + tot - 1) // tot
                    while pos[i] < tgt:
                        out.append(l[pos[i]])
                        pos[i] += 1
            return out

        NCH = T // 64
        for r_ in range(-2, NCH):
            P.ins.extend(merge([record(emit_B, r_), record(emit_A2, r_ + 1), record(emit_A1, r_ + 2)]))

        if upto == "dn":
            raise _Stop()

        if upto == "g1":
            raise _Stop()
        P.barrier(lambda e: e.memset(bar_t[:, 2:3], 0.0))
        bigw[0] = 0
        gin = carve(48 * SB, BF16).rearrange("p (a b) -> p a b", a=48)
        wos_s = carve(4 * 16 * 128, BF16).rearrange("p (c k n) -> p c k n", c=4, k=16)
        wod_s = carve(4 * 32 * 128, BF16).rearrange("p (c k n) -> p c k n", c=4, k=32)
        sas = [carve(4 * SB, BF16).rearrange("p (a b) -> p a b", a=4) for _ in range(2)]
        sbs = [carve(4 * SB, BF16).rearrange("p (a b) -> p a b", a=4) for _ in range(2)]
        yst = carve(4 * SB, BF16).rearrange("p (a b) -> p a b", a=4)
        t1 = carve(SB)
        t2 = carve(SB)
        for c in range(4):
            P.dma("gpsimd", lambda e, c=c: e.dma_start(out=wos_s[:, c, :, :], in_=wos[c]), w=["wos_s"], sem="wos")
            P.dma("gpsimd", lambda e, c=c: e.dma_start(out=wod_s[:, c, :, :], in_=wod[c]), w=["wod_s"], sem="wod")
        for k in range(4, 12):
            P.dma("gpsimd", lambda e, k=k: e.collective_compute("AllGather", ALU.bypass, replica_groups=RG,
                                                                ins=[g_src[k * 128:(k + 1) * 128, :].opt()],
                                                                outs=[g_dst[k * 512:(k + 1) * 512, :].opt()]),
                  r=["g_src_o"], w=["g_dst%d" % k], sem="cc1_%d" % k, inc=1)
        g_dst3 = g_dst.rearrange("(k p) t -> p k t", p=128)
        y_src3 = y_src.rearrange("(k p) t -> p k t", p=128)

        def c2_loads(tb):
            c0 = tb * SB
            u = tb % 2
            for r4 in range(4):
                P.dma("sync", lambda e, c0=c0, r4=r4: e.dma_start(out=gin[:, r4 * 12:r4 * 12 + 12, :], in_=g_dst3[:, r4 * 12:r4 * 12 + 12, c0:c0 + SB]),
                      r=["g_dst%d" % k for k in range(12)], w=["gin_g%d" % r4], sem="gin%d" % r4)
            P.dma("sync", lambda e, c0=c0, u=u: e.dma_start(out=sas[u], in_=s_sa.rearrange("k p t -> p k t")[:, :, c0:c0 + SB]),
                  r=["s_sa%d" % i for i in range(4)], w=["sas%d" % u], sem="sas%d" % u)
            P.dma("sync", lambda e, c0=c0, u=u: e.dma_start(out=sbs[u], in_=s_sb.rearrange("k p t -> p k t")[:, :, c0:c0 + SB]),
                  r=["s_sb%d" % i for i in range(4)], w=["sbs%d" % u], sem="sbs%d" % u)

        accA = [psA[0], psA[1], psA[2], psA[3]]
        accB = [psB[0], psB[1], psC[0], psC[1]]
        kA = ["psA0", "psA1", "psA2", "psA3"]
        kB = ["psB0", "psB1", "psC0", "psC1"]
        items = []
        for kt in range(16):
            items.append(((kt % 4) * 4 + kt // 4, kt, False))
        for kt in range(32):
            items.append(((4 + kt % 8) * 4 + kt // 8, kt, True))
        items.sort()
        firstA, lastA = min(i for i, it in enumerate(items) if not it[2]), max(i for i, it in enumerate(items) if not it[2])
        firstB, lastB = min(i for i, it in enumerate(items) if it[2]), max(i for i, it in enumerate(items) if it[2])
        c2_loads(0)
        for tb in range(T // SB):
            c0 = tb * SB
            u = tb % 2
            for i_, (gt, kt, is_o) in enumerate(items):
                gk = "gin_g%d" % (gt // 12)
                for c in range(4):
                    if not is_o:
                        P.op("tensor", lambda e, c=c, kt=kt, gt=gt, st=(i_ == firstA), sp=(i_ == lastA): e.matmul(
                            accA[c][:, :], lhsT=wos_s[:, c, kt, :], rhs=gin[:, gt, :], start=st, stop=sp), r=["wos_s", gk], w=[kA[c]])
                    else:
                        P.op("tensor", lambda e, c=c, kt=kt, gt=gt, st=(i_ == firstB), sp=(i_ == lastB): e.matmul(
                            accB[c][:, :], lhsT=wod_s[:, c, kt, :], rhs=gin[:, gt, :], start=st, stop=sp), r=["wod_s", gk], w=[kB[c]])
            if tb + 1 < T // SB:
                c2_loads(tb + 1)
            for c in range(4):
                P.op("vector", lambda e, c=c, u=u: e.tensor_tensor(out=t1, in0=accA[c][:, :], in1=sas[u][:, c, :], op=ALU.mult), r=[kA[c], "sas%d" % u], w=["t1"])
                P.op("vector", lambda e, c=c, u=u: e.tensor_tensor(out=t2, in0=accB[c][:, :], in1=sbs[u][:, c, :], op=ALU.mult), r=[kB[c], "sbs%d" % u], w=["t2"])
                P.op("vector", lambda e, c=c: e.tensor_tensor(out=yst[:, c, :], in0=t1, in1=t2, op=ALU.add), r=["t1", "t2"], w=["yst"])
            P.dma("sync", lambda e, c0=c0: e.dma_start(out=y_src3[:, :, c0:c0 + SB], in_=yst), r=["yst"], w=["y_src"], sem="yst")
        for k in range(4):
            P.dma("gpsimd", lambda e, k=k: e.collective_compute("AllGather", ALU.bypass, replica_groups=RG,
                                                                ins=[y_src[k * 128:(k + 1) * 128, :].opt()],
                                                                outs=[y_dst[k * 512:(k + 1) * 512, :].opt()]),
                  r=["y_src"], w=["y_dst%d" % k], sem="cc2_%d" % k, inc=1)

        if upto == "c2":
            raise _Stop()
        P.barrier(lambda e: e.memset(bar_t[:, 3:4], 0.0))
        bigw[0] = 0
        yin = [carve(16 * SB, BF16).rearrange("p (a b) -> p a b", a=16) for _ in range(2)]
        wout_s = carve(4 * 16 * 128, BF16).rearrange("p (c k n) -> p c k n", c=4, k=16)
        xr = [carve(4 * SB).rearrange("p (a b) -> p a b", a=4) for _ in range(2)]
        ost = [carve(4 * SB).rearrange("p (a b) -> p a b", a=4) for _ in range(2)]
        for c in range(4):
            P.dma("gpsimd", lambda e, c=c: e.dma_start(out=wout_s[:, c, :, :], in_=wout[c]), w=["wout_s"], sem="wout")
        y_dst3 = y_dst.rearrange("(k p) t -> p k t", p=128)
        outT3 = outT.rearrange("(k p) t -> p k t", p=128)
        def c3_loads(tb):
            c0 = tb * SB
            u = tb % 2
            P.dma("sync", lambda e, c0=c0, u=u: e.dma_start(out=yin[u], in_=y_dst3[:, :, c0:c0 + SB]), r=["y_dst%d" % k for k in range(4)], w=["yin%d" % u], sem="yin%d" % u)
            P.dma("sync", lambda e, c0=c0, u=u: e.dma_start(out=xr[u], in_=xres.rearrange("k p t -> p k t")[:, :, c0:c0 + SB]), w=["xr%d" % u], sem="xr%d" % u)

        c3_loads(0)
        for tb in range(T // SB):
            c0 = tb * SB
            u = tb % 2
            if tb + 1 < T // SB:
                c3_loads(tb + 1)
            for c in range(4):
                pa_ = psA[c]
                ka_ = "psA%d" % c
                for kt in range(16):
                    P.op("tensor", lambda e, c=c, kt=kt, pa_=pa_, u=u: e.matmul(pa_[:, :], lhsT=wout_s[:, c, kt, :], rhs=yin[u][:, (kt % 4) * 4 + kt // 4, :],
                                                                               start=(kt == 0), stop=(kt == 15)), r=["wout_s", "yin%d" % u], w=[ka_])
                P.op("vector", lambda e, c=c, pa_=pa_, u=u: e.scalar_tensor_tensor(out=ost[u][:, c, :], in0=pa_[:, :], scalar=mod[:, 32 + c:33 + c],
                                                                                   in1=xr[u][:, c, :], op0=ALU.mult, op1=ALU.add),
                     r=[ka_, "mod", "xr%d" % u], w=["ost%d" % u])
            P.dma("sync", lambda e, c0=c0, u=u: e.dma_start(out=outT3[:, :, c0:c0 + SB], in_=ost[u]), r=["ost%d" % u], w=["outT"], sem="ost%d" % u)
    except _Stop:
        pass

    allw = set()
    for I in P.ins:
        if I["dma"] is not None:
            allw.update(I["w"])
    endt = sb("endt", [128, 1])
    P.barrier(lambda e: e.memset(endt[:, :], 0.0))
    P.op("sync", lambda e: e.engine_nop() if hasattr(e, "engine_nop") else None, r=sorted(allw), w=["__end"])
    P.finalize()
    P.emit(nc, es)
    es.close()
    return nc


def tile_w(w):
    K, N = w.shape
    assert N % 128 == 0
    return np.ascontiguousarray(w.reshape(K // 128, 128, N // 128, 128).transpose(2, 1, 0, 3))


def prep_inputs(inp, b, j):
    x = inp["x"][b]
    w_in = inp["w_in"][0]
    m = {}
    xT_ = np.ascontiguousarray(x.T)
    m["xT"] = np.ascontiguousarray(xT_.reshape(16, 128, T // 128, 128).transpose(2, 1, 0, 3))
    m["xres"] = np.ascontiguousarray(xT_[512 * j:512 * j + 512].reshape(4, 128, T))
    m["cT"] = np.ascontiguousarray(inp["c"][b].reshape(16, 128).T)
    m["pos"] = np.ascontiguousarray(inp["positions"][b].reshape(1, T).astype(np.int32))
    w_ada = inp["w_ada"][0]
    cols = np.concatenate([np.arange(0, 4096), 4096 + 512 * j + np.arange(512)])
    m["wada"] = tile_w(w_ada[:, cols])
    m["bada"] = np.ascontiguousarray(inp["b_ada"][0][cols].reshape(36, 128).T)
    m["nw"] = np.ascontiguousarray(inp["norm_w"][0].reshape(16, 128).T)
    o_aq, o_ak, o_av, o_ag, o_dn, o_dz, o_db, o_da, o_ma, o_mb = np.cumsum([0, 2048, 256, 256, 2048, 8192, 4096, 32, 32, 2048])
    r = np.arange
    dnq = o_dn + 512 * j + r(512)
    dnk = o_dn + 2048 + 512 * j + r(512)
    dnv = o_dn + 4096 + 1024 * j + r(1024)
    dz = o_dz + 1024 * j + r(1024)
    aq = o_aq + 512 * j + r(512)
    kk = np.concatenate([o_ak + 64 * j + r(64), o_ak + 64 * j + r(64)])
    vv = np.concatenate([o_av + 64 * j + r(64), o_av + 64 * j + r(64)])
    ag = o_ag + 512 * j + r(512)
    ma = o_ma + 512 * j + r(512)
    mb = o_mb + 512 * j + r(512)
    bg = np.concatenate([o_db + 8 * j + r(8), o_da + 8 * j + r(8)])
    cols = np.concatenate([dnq, dnk, dnv, dz, aq, kk, vv, ag, ma, mb])
    wbg = np.zeros((2048, 128), np.float32)
    wbg[:, :16] = w_in[:, bg]
    m["wa"] = tile_w(np.concatenate([w_in[:, cols], wbg], axis=1))
    cwc = inp["conv_w"][0][:, np.concatenate([dnq, dnk, dnv]) - o_dn]
    m["cw"] = np.ascontiguousarray(cwc.reshape(4, 16, 128).transpose(2, 1, 0))
    cs = slice(512 * j, 512 * j + 512)
    m["wos"] = tile_w(inp["w_o_swa"][0][:, cs])
    wod_ = inp["w_o_dn"][0][:, cs]
    m["wod"] = np.ascontiguousarray(wod_.reshape(32, 128, 4, 128).transpose(2, 1, 0, 3))
    m["wout"] = tile_w(inp["w_out"][0][:, cs])
    cst = np.zeros((128, 1536), np.float32)
    cst[:, 0:128] = np.eye(128)
    for q in range(2):
        cst[64 * q:64 * q + 64, 128 + 64 * q:128 + 64 * q + 64] = 1.0 / 64
    for q in range(2):
        for i in range(8):
            cst[64 * q + i + 8, 256 + 64 * q + i] = -1.0
            cst[64 * q + i, 256 + 64 * q + i + 8] = 1.0
    ii = np.arange(64)
    cst[0:64, 384:448] = (ii[:, None] <= ii[None, :])
    cst[63, 448:576] = 1.0
    cst[0:64, 640:704] = (ii[:, None] < ii[None, :])
    kq = np.arange(128)
    cst[:, 704:832] = (kq[None, :] < kq[:, None])
    cst[:, 832:960] = (kq[None, :] >= kq[:, None])
    invf = 500000.0 ** (-np.arange(8, dtype=np.float32) * (2.0 / 16))
    for q in range(2):
        cst[64 * q:64 * q + 8, 960] = invf
        cst[64 * q + 8:64 * q + 16, 960] = invf
    cst[:, 1024:1088] = 1.0
    cst[:, 1152 + 64:1280] = 1.0
    cst[0:64, 961] = 1.0
    cst[64:128, 962] = 1.0
    for k in range(64):
        cst[k, 1280 + k] = 1.0
        cst[k, 1408 + 64 + k] = 1.0
    m["cst"] = cst
    sm = np.zeros((128, 32), np.float32)
    sm[:, 0] = np.tile(inp["q_norm_w"][0], 2)
    sm[:, 1] = np.tile(inp["k_norm_w"][0], 2)
    sk = inp["sinks"][0][8 * j:8 * j + 8]
    for t in range(4):
        sm[0:64, 2 + t] = sk[2 * t]
        sm[64:128, 2 + t] = sk[2 * t + 1]
    sm[:, 6] = inp["dn_norm_w"][0]
    sm[:, 8:16] = inp["a_log"][0][8 * j:8 * j + 8][None, :]
    sm[:, 16:24] = inp["dt_bias"][0][8 * j:8 * j + 8][None, :]
    m["sm"] = sm
    return m


_NC_CACHE = {}


def kernel(**inp):
    inp = {k: np.asarray(v) for k, v in inp.items()}
    if "nc" not in _NC_CACHE:
        _NC_CACHE["nc"] = build()
    nc = _NC_CACHE["nc"]
    in_maps = [prep_inputs(inp, c // 4, c % 4) for c in range(8)]
    res = run_bass_kernel_spmd(nc, in_maps, core_ids=list(range(8)))
    out = np.zeros((2, T, D), np.float32)
    for c in range(8):
        b, j = c // 4, c % 4
        out[b][:, 512 * j:512 * j + 512] = np.asarray(res.results[c]["outT"]).T
    return out
```
